# Optimizing a Trainium2 kernel written in Bass

```python
import math
import jax
import jax.numpy as jnp
from jax import lax
import numpy as np

D_MODEL = 1024
BATCH = 16
SEQ = 4096
DEPTH = 4

F32 = jnp.float32
N_MIXERS = 4
N_META = 16
EXPAND = 2
D_INNER = EXPAND * D_MODEL
RMS_EPS = 1e-6
LN_EPS = 1e-5
ROPE_THETA = 10000.0
Q_BLOCK = 128

HY_SHORT = 3
HY_EMB = 33
HY_BANDS = (HY_EMB - 1) // 2
HY_HIDDEN = 64
HY_SHORT_DECAY_PCT = 0.3
HY_LONG_DECAY_PCT = 1.5
HY_DECAY_TARGET = 1e-2
HY_FILTER_EPS = 1e-6

DA_HEAD_DIM = 64
DA_V_DIM = 2 * DA_HEAD_DIM
DA_HEADS = D_INNER // DA_V_DIM
DA_QK = DA_HEADS * 2 * DA_HEAD_DIM
DA_NORM_EPS = 1e-5

GDN_HEAD_DIM = 128
GDN_V_HEADS = D_INNER // GDN_HEAD_DIM
GDN_K_HEADS = GDN_V_HEADS // 2
GDN_QK = GDN_K_HEADS * GDN_HEAD_DIM
GDN_SHORT = 3
GDN_CHUNK = 64

CF_WIDTH = 31

N_A = (DEPTH + 3) // N_MIXERS
N_B = (DEPTH + 2) // N_MIXERS
N_C = (DEPTH + 1) // N_MIXERS
N_D = DEPTH // N_MIXERS

kernel_name = "hybrid_bidir_interleaved_encoder"


def rmsnorm(x, g, eps=RMS_EPS):
    xf = x.astype(F32)
    y = xf * lax.rsqrt(jnp.mean(xf * xf, axis=-1, keepdims=True) + eps)
    return (y * g.astype(F32)).astype(x.dtype)


def layernorm(x, g, b, eps=LN_EPS):
    xf = x.astype(F32)
    mu = jnp.mean(xf, axis=-1, keepdims=True)
    var = jnp.mean(jnp.square(xf - mu), axis=-1, keepdims=True)
    y = (xf - mu) * lax.rsqrt(var + eps)
    return (y * g.astype(F32) + b.astype(F32)).astype(x.dtype)


def l2norm(x, eps=1e-6):
    xf = x.astype(F32)
    return xf * lax.rsqrt(jnp.sum(xf * xf, axis=-1, keepdims=True) + eps)


def dwconv_centred(x, w, b=None):
    width, ch = w.shape
    y = lax.conv_general_dilated(
        x, w[:, None, :].astype(x.dtype), window_strides=(1,),
        padding=[(width // 2, width // 2)],
        dimension_numbers=("NWC", "WIO", "NWC"), feature_group_count=ch)
    if b is not None:
        y = y + b.astype(x.dtype)
    return y


def rope_tables(T, dim):
    inv_freq = ROPE_THETA ** (-jnp.arange(0, dim, 2, dtype=F32) / dim)
    ang = jnp.arange(T, dtype=F32)[:, None] * inv_freq[None, :]
    return jnp.cos(ang), jnp.sin(ang)


def apply_rope(x, cos, sin):
    half = x.shape[-1] // 2
    xf = x.astype(F32)
    x1, x2 = xf[..., :half], xf[..., half:]
    c = cos[None, :, None, None, :]
    s = sin[None, :, None, None, :]
    return jnp.concatenate([x1 * c - x2 * s, x2 * c + x1 * s], axis=-1).astype(x.dtype)


def hyena_filters(T, w1, b1, w2, b2, w3, b3, w4, freq):
    t = jnp.linspace(0.0, 1.0, T, dtype=F32)[:, None]
    w = 2.0 * math.pi * jnp.arange(T, dtype=F32)[:, None] / T
    bands = jnp.linspace(1e-4, HY_BANDS - 1, HY_BANDS, dtype=F32)[None, :]
    z = jnp.concatenate([t, jnp.cos(bands * w), -jnp.sin(bands * w)], axis=-1)
    fr = freq.astype(F32)
    hdn = jnp.sin(fr * (z @ w1.astype(F32) + b1.astype(F32)))
    hdn = jnp.sin(fr * (hdn @ w2.astype(F32) + b2.astype(F32)))
    hdn = jnp.sin(fr * (hdn @ w3.astype(F32) + b3.astype(F32)))
    filt = (hdn @ w4.astype(F32)).reshape(T, 2, D_INNER)
    max_decay = math.log(HY_DECAY_TARGET) / HY_SHORT_DECAY_PCT
    min_decay = math.log(HY_DECAY_TARGET) / HY_LONG_DECAY_PCT
    deltas = jnp.abs(jnp.linspace(min_decay, max_decay, D_INNER, dtype=F32))
    filt = filt * jnp.exp(-t * deltas[None, :])[:, None, :]
    filt = filt / (jnp.sum(jnp.abs(filt), axis=(0, 1), keepdims=True) + HY_FILTER_EPS)
    return filt[:, 0], filt[:, 1]


def bidir_long_conv(v, h_fwd, h_bwd, skip):
    T = v.shape[1]
    n = 2 * T
    taps = jnp.concatenate(
        [h_fwd, jnp.zeros((1, h_fwd.shape[1]), F32), h_bwd[:0:-1]], axis=0)
    vf = v.astype(F32)
    y = jnp.fft.irfft(jnp.fft.rfft(vf, n=n, axis=1) * jnp.fft.rfft(taps, n=n, axis=0)[None],
                      n=n, axis=1)[:, :T]
    return (y + vf * skip.astype(F32)).astype(v.dtype)


def hyena_mixer(h, w_in, b_in, conv_w, conv_b, f_w1, f_b1, f_w2, f_b2, f_w3, f_b3, f_w4,
                f_freq, skip, w_out):
    T = h.shape[1]
    u = h @ w_in + b_in
    streams = dwconv_centred(u[..., :3 * D_INNER], conv_w, conv_b)
    x0, x1, v = jnp.split(streams, 3, axis=-1)
    z = u[..., 3 * D_INNER:]
    h_fwd, h_bwd = hyena_filters(T, f_w1, f_b1, f_w2, f_b2, f_w3, f_b3, f_w4, f_freq)
    v = bidir_long_conv(v * x1, h_fwd, h_bwd, skip)
    y = v * x0
    return (y * jax.nn.silu(z)) @ w_out


def diff_attention_mixer(h, w_in, lam_vecs, subln, w_out, cos, sin, layer_idx):
    B, T, _ = h.shape
    u = h @ w_in
    q = u[..., :DA_QK].reshape(B, T, DA_HEADS, 2, DA_HEAD_DIM)
    k = u[..., DA_QK:2 * DA_QK].reshape(B, T, DA_HEADS, 2, DA_HEAD_DIM)
    v = u[..., 2 * DA_QK:2 * DA_QK + D_INNER].reshape(B, T, DA_HEADS, DA_V_DIM)
    z = u[..., 2 * DA_QK + D_INNER:]
    q = apply_rope(q, cos, sin)
    k = apply_rope(k, cos, sin)
    lam_init = 0.8 - 0.6 * math.exp(-0.3 * layer_idx)
    lv = lam_vecs.astype(F32)
    lam = jnp.exp(jnp.sum(lv[0] * lv[1])) - jnp.exp(jnp.sum(lv[2] * lv[3])) + lam_init
    scale = DA_HEAD_DIM ** -0.5
    n_blk = -(-T // Q_BLOCK)
    q = jnp.pad(q, ((0, 0), (0, n_blk * Q_BLOCK - T), (0, 0), (0, 0), (0, 0)))
    q_blocks = q.reshape(B, n_blk, Q_BLOCK, DA_HEADS, 2, DA_HEAD_DIM).swapaxes(0, 1)

    def attend(qb):
        s = jnp.einsum("bqhmd,bkhmd->bhmqk", qb, k).astype(F32) * scale
        p = jax.nn.softmax(s, axis=-1)
        p = p[:, :, 0] - lam * p[:, :, 1]
        return jnp.einsum("bhqk,bkhd->bqhd", p.astype(v.dtype), v)

    o = lax.map(attend, q_blocks)
    o = o.swapaxes(0, 1).reshape(B, n_blk * Q_BLOCK, DA_HEADS, DA_V_DIM)[:, :T]
    o = rmsnorm(o, subln, DA_NORM_EPS) * (1.0 - lam_init)
    return (o.reshape(B, T, D_INNER) * jax.nn.silu(z)) @ w_out


def chunk_gated_delta(q, k, v, g, beta):
    B, Tp, H, dk = q.shape
    dv = v.shape[-1]
    C = GDN_CHUNK
    n = Tp // C

    def to_chunks(a):
        return a.reshape(B, n, C, H, -1).transpose(0, 3, 1, 2, 4)

    q, k, v = to_chunks(q), to_chunks(k), to_chunks(v)
    beta = to_chunks(beta[..., None])
    g = jnp.cumsum(to_chunks(g[..., None])[..., 0], axis=-1)
    idx = jnp.arange(C)
    strict = idx[:, None] > idx[None, :]
    incl = idx[:, None] >= idx[None, :]
    gdiff = g[..., :, None] - g[..., None, :]
    k_beta = k * beta
    a_mat = jnp.einsum("bhncd,bhnsd->bhncs", k_beta, k) * jnp.exp(jnp.where(strict, gdiff, -jnp.inf))
    rhs = jnp.concatenate([v * beta, k_beta * jnp.exp(g)[..., None]], axis=-1)
    sol = lax.linalg.triangular_solve(a_mat + jnp.eye(C, dtype=F32), rhs,
                                      left_side=True, lower=True)
    u_vals, w_vals = sol[..., :dv], sol[..., dv:]
    attn_intra = jnp.einsum("bhncd,bhnsd->bhncs", q, k) * jnp.exp(jnp.where(incl, gdiff, -jnp.inf))
    q_dec = q * jnp.exp(g)[..., None]
    g_last = g[..., -1]
    k_end = k * jnp.exp(g_last[..., None] - g)[..., None]

    def step(S, xs):
        qd, ke, uc, wc, ac, gl = xs
        v_new = uc - jnp.einsum("bhcd,bhde->bhce", wc, S)
        o = jnp.einsum("bhcd,bhde->bhce", qd, S) + jnp.einsum("bhcs,bhse->bhce", ac, v_new)
        S = S * jnp.exp(gl)[..., None, None] + jnp.einsum("bhcd,bhce->bhde", ke, v_new)
        return S, o

    xs = tuple(jnp.moveaxis(a, 2, 0) for a in (q_dec, k_end, u_vals, w_vals, attn_intra, g_last))
    S0 = jnp.zeros((B, H, dk, dv), F32)
    _, o = lax.scan(step, S0, xs)
    return o.transpose(1, 0, 3, 2, 4).reshape(B, Tp, H, dv)


def gdn_mixer(h, w_in, conv_w, a_log, dt_bias, o_norm, w_out):
    B, T, _ = h.shape
    u = h @ w_in
    n_qkv = 2 * GDN_QK + D_INNER
    qkv = jax.nn.silu(dwconv_centred(u[..., :n_qkv], conv_w))
    z = u[..., n_qkv:n_qkv + D_INNER].reshape(B, T, GDN_V_HEADS, GDN_HEAD_DIM)
    ab = u[..., n_qkv + D_INNER:].astype(F32).reshape(B, T, 2, 2, GDN_V_HEADS)
    q = l2norm(qkv[..., :GDN_QK].reshape(B, T, GDN_K_HEADS, GDN_HEAD_DIM)) * GDN_HEAD_DIM ** -0.5
    k = l2norm(qkv[..., GDN_QK:2 * GDN_QK].reshape(B, T, GDN_K_HEADS, GDN_HEAD_DIM))
    v = qkv[..., 2 * GDN_QK:].astype(F32).reshape(B, T, GDN_V_HEADS, GDN_HEAD_DIM)
    rep = GDN_V_HEADS // GDN_K_HEADS
    q = jnp.repeat(q, rep, axis=2)
    k = jnp.repeat(k, rep, axis=2)
    g = -jnp.exp(a_log.astype(F32)) * jax.nn.softplus(ab[:, :, 0] + dt_bias.astype(F32))
    beta = jax.nn.sigmoid(ab[:, :, 1])
    pad_front = (-N_META) % GDN_CHUNK
    pad_end = (-(T - N_META)) % GDN_CHUNK

    def padt(a):
        return jnp.pad(a, [(0, 0), (pad_front, pad_end)] + [(0, 0)] * (a.ndim - 2))

    q, k, v, g, beta = padt(q), padt(k), padt(v), padt(g), padt(beta)

    def rev(a):
        return jnp.flip(a, axis=1)

    o_fwd = chunk_gated_delta(q, k, v, g[:, :, 0], beta[:, :, 0])
    o_bwd = rev(chunk_gated_delta(rev(q), rev(k), rev(v), rev(g[:, :, 1]), rev(beta[:, :, 1])))
    o = (o_fwd + o_bwd)[:, pad_front:pad_front + T].astype(h.dtype)
    o = rmsnorm(o, o_norm) * jax.nn.silu(z)
    return o.reshape(B, T, D_INNER) @ w_out


def conformer_conv_mixer(h, w_in, b_in, dw_w, dw_b, ln_g, ln_b, w_out, b_out):
    u = h @ w_in + b_in
    a, a_gate, z = jnp.split(u, 3, axis=-1)
    y = a * jax.nn.sigmoid(a_gate)
    y = dwconv_centred(y, dw_w, dw_b)
    y = jax.nn.silu(layernorm(y, ln_g, ln_b))
    return (y * jax.nn.silu(z)) @ w_out + b_out


def setup_inputs(seed: int = 0) -> dict:
    key = jax.random.key(seed)
    ks = jax.random.split(key, 48)
    counter = [0]

    def nk():
        counter[0] += 1
        return ks[counter[0] - 1]

    def nrm(shape, scale):
        return jax.random.normal(nk(), shape, F32) * scale

    def gain(shape):
        return 1.0 + nrm(shape, 0.02)

    E = D_INNER
    dt = jnp.exp(jax.random.uniform(nk(), (N_C, 2, GDN_V_HEADS), F32, math.log(1e-3), math.log(1e-1)))
    a_log = jnp.log(jax.random.uniform(nk(), (N_C, 2, GDN_V_HEADS), F32, 1.0, 16.0))
    return {
        "x": nrm((BATCH, SEQ, D_MODEL), 1.0),
        "meta": nrm((N_META, D_MODEL), 1.0),
        "norm_pre": gain((DEPTH, D_MODEL)),
        "norm_post": gain((DEPTH, D_MODEL)),
        "hy_w_in": nrm((N_A, D_MODEL, 4 * E), D_MODEL ** -0.5),
        "hy_b_in": nrm((N_A, 4 * E), 0.02),
        "hy_conv_w": nrm((N_A, HY_SHORT, 3 * E), HY_SHORT ** -0.5),
        "hy_conv_b": nrm((N_A, 3 * E), 0.02),
        "hy_f_w1": nrm((N_A, HY_EMB, HY_HIDDEN), HY_EMB ** -0.5),
        "hy_f_b1": nrm((N_A, HY_HIDDEN), 0.02),
        "hy_f_w2": nrm((N_A, HY_HIDDEN, HY_HIDDEN), HY_HIDDEN ** -0.5),
        "hy_f_b2": nrm((N_A, HY_HIDDEN), 0.02),
        "hy_f_w3": nrm((N_A, HY_HIDDEN, HY_HIDDEN), HY_HIDDEN ** -0.5),
        "hy_f_b3": nrm((N_A, HY_HIDDEN), 0.02),
        "hy_f_w4": nrm((N_A, HY_HIDDEN, 2 * E), HY_HIDDEN ** -0.5),
        "hy_f_freq": gain((N_A, HY_HIDDEN)),
        "hy_skip": nrm((N_A, E), 1.0),
        "hy_w_out": nrm((N_A, E, D_MODEL), E ** -0.5),
        "da_w_in": nrm((N_B, D_MODEL, 2 * DA_QK + 2 * E), D_MODEL ** -0.5),
        "da_lambda": nrm((N_B, 4, DA_HEAD_DIM), 0.1),
        "da_subln": gain((N_B, DA_V_DIM)),
        "da_w_out": nrm((N_B, E, D_MODEL), E ** -0.5),
        "gdn_w_in": nrm((N_C, D_MODEL, 2 * GDN_QK + 2 * E + 4 * GDN_V_HEADS), D_MODEL ** -0.5),
        "gdn_conv_w": nrm((N_C, GDN_SHORT, 2 * GDN_QK + E), GDN_SHORT ** -0.5),
        "gdn_a_log": a_log,
        "gdn_dt_bias": dt + jnp.log(-jnp.expm1(-dt)),
        "gdn_o_norm": gain((N_C, GDN_HEAD_DIM)),
        "gdn_w_out": nrm((N_C, E, D_MODEL), E ** -0.5),
        "cf_w_in": nrm((N_D, D_MODEL, 3 * E), D_MODEL ** -0.5),
        "cf_b_in": nrm((N_D, 3 * E), 0.02),
        "cf_dw_w": nrm((N_D, CF_WIDTH, E), CF_WIDTH ** -0.5),
        "cf_dw_b": nrm((N_D, E), 0.02),
        "cf_ln_g": gain((N_D, E)),
        "cf_ln_b": nrm((N_D, E), 0.02),
        "cf_w_out": nrm((N_D, E, D_MODEL), E ** -0.5),
        "cf_b_out": nrm((N_D, D_MODEL), 0.02),
    }


def reference(x, meta, norm_pre, norm_post,
              hy_w_in, hy_b_in, hy_conv_w, hy_conv_b, hy_f_w1, hy_f_b1, hy_f_w2, hy_f_b2,
              hy_f_w3, hy_f_b3, hy_f_w4, hy_f_freq, hy_skip, hy_w_out,
              da_w_in, da_lambda, da_subln, da_w_out,
              gdn_w_in, gdn_conv_w, gdn_a_log, gdn_dt_bias, gdn_o_norm, gdn_w_out,
              cf_w_in, cf_b_in, cf_dw_w, cf_dw_b, cf_ln_g, cf_ln_b, cf_w_out, cf_b_out):
    B = x.shape[0]
    h = jnp.concatenate(
        [jnp.broadcast_to(meta[None].astype(x.dtype), (B, N_META, D_MODEL)), x], axis=1)
    T = h.shape[1]
    cos, sin = rope_tables(T, DA_HEAD_DIM)
    for i in range(DEPTH):
        m, j = i % N_MIXERS, i // N_MIXERS
        y = rmsnorm(h, norm_pre[i])
        if m == 0:
            y = hyena_mixer(y, hy_w_in[j], hy_b_in[j], hy_conv_w[j], hy_conv_b[j],
                            hy_f_w1[j], hy_f_b1[j], hy_f_w2[j], hy_f_b2[j], hy_f_w3[j], hy_f_b3[j],
                            hy_f_w4[j], hy_f_freq[j], hy_skip[j], hy_w_out[j])
        elif m == 1:
            y = diff_attention_mixer(y, da_w_in[j], da_lambda[j], da_subln[j], da_w_out[j],
                                     cos, sin, i)
        elif m == 2:
            y = gdn_mixer(y, gdn_w_in[j], gdn_conv_w[j], gdn_a_log[j], gdn_dt_bias[j],
                          gdn_o_norm[j], gdn_w_out[j])
        else:
            y = conformer_conv_mixer(y, cf_w_in[j], cf_b_in[j], cf_dw_w[j], cf_dw_b[j],
                                     cf_ln_g[j], cf_ln_b[j], cf_w_out[j], cf_b_out[j])
        h = h + rmsnorm(y, norm_post[i])
    return h[:, N_META:]
```

```python
import math
from contextlib import ExitStack
import numpy as np
import concourse.bass as bass
import concourse.mybir as mybir
from concourse.bass_utils import run_bass_kernel_spmd

F32 = mybir.dt.float32
BF16 = mybir.dt.bfloat16
AF = mybir.ActivationFunctionType
ALU = mybir.AluOpType
AX = mybir.AxisListType

D = 1024
E = 2048
NMETA = 16
KC = D // 128


class _Eng:
    def __init__(self, name, eng, sem):
        self.name = name
        self.eng = eng
        self.sem = sem
        self.count = 0
        self.waited = {}


class Sched:
    def __init__(self, nc, stack, n_dma_sems=24):
        self.nc = nc
        self.E = {}
        for name, eng in (("pe", nc.tensor), ("act", nc.scalar), ("dve", nc.vector),
                          ("pool", nc.gpsimd), ("sp", nc.sync)):
            sem = stack.enter_context(nc.semaphore("s_" + name))
            self.E[name] = _Eng(name, eng, sem)
        self.dma_sems = [stack.enter_context(nc.semaphore("s_dma%d" % i)) for i in range(n_dma_sems)]
        self.dma_issued = [0] * n_dma_sems
        self.dma_rr = 0
        self.dma_rr2 = 0
        self.last_write = {}
        self.readers = {}
        self.ninstr = 0
        self.strict = True

    def _wait(self, e, ev):
        if ev is None:
            return
        if ev[0] == "e":
            if ev[1] == e.name and not (self.strict and e.name in ("act", "dve", "pool")):
                return
            src = self.E[ev[1]]
            key = ("e", ev[1])
            val = ev[2]
            sem = src.sem
        else:
            idx = ev[1]
            key = ("d", idx)
            val = 16 * self.dma_issued[idx]
            sem = self.dma_sems[idx]
        if e.waited.get(key, 0) >= val:
            return
        e.waited[key] = val
        e.eng.wait_ge(sem, val)
        self.ninstr += 1

    def _deps(self, e, reads, writes):
        for k in reads:
            self._wait(e, self.last_write.get(k))
            if isinstance(k, str) and k[0] == "P":
                for ev in self.readers.get(k, {}).values():
                    if ev[0] == "e" and ev[1] != e.name:
                        self._wait(e, ev)
        for k in writes:
            self._wait(e, self.last_write.get(k))
            for ev in self.readers.get(k, {}).values():
                self._wait(e, ev)

    def _record(self, ev, reads, writes):
        for k in reads:
            d = self.readers.setdefault(k, {})
            d[ev[:2]] = ev
        for k in writes:
            self.last_write[k] = ev
            self.readers[k] = {}

    def op(self, engname, emit, reads=(), writes=()):
        e = self.E[engname]
        self._deps(e, reads, writes)
        ins = emit(e.eng)
        e.count += 1
        ins.then_inc(e.sem, 1)
        self.ninstr += 1
        self._record(("e", engname, e.count), reads, writes)
        return ins

    def dma(self, qname, out, in_, reads=(), writes=(), semgroup=None, **kw):
        e = self.E[qname]
        self._deps(e, reads, writes)
        if qname == "sp":
            idx = self.dma_rr % 16
            self.dma_rr += 1
        else:
            idx = 16 + self.dma_rr2 % 8
            self.dma_rr2 += 1
        ins = e.eng.dma_start(out=out, in_=in_, **kw)
        ins.then_inc(self.dma_sems[idx], 16)
        self.dma_issued[idx] += 1
        self.ninstr += 1
        self._record(("d", idx), reads, writes)
        return ins

    def barrier(self):
        for name in self.E:
            self.finish(name)

    def finish(self, engname="sp"):
        e = self.E[engname]
        for idx in range(len(self.dma_sems)):
            if self.dma_issued[idx]:
                self._wait(e, ("d", idx))
        for name, src in self.E.items():
            if name != engname and src.count:
                self._wait(e, ("e", name, src.count))


class Ring:
    def __init__(self, name, bufs, keys=None):
        self.name = name
        self.bufs = bufs
        self.keys = keys if keys is not None else ["%s#%d" % (name, j) for j in range(len(bufs))]
        self.i = 0

    def next(self):
        j = self.i % len(self.bufs)
        self.i += 1
        return self.bufs[j], self.keys[j]


def tok_blocks(T, bs):
    out = []
    t = 0
    while t < T:
        n = min(bs, T - t)
        out.append((t, n))
        t += n
    return out


class Builder:
    def __init__(self, T, nseq, layers, params_shapes):
        self.T = T
        self.nseq = nseq
        self.layers = layers
        self.nc = bass.Bass("TRN2", target_bir_lowering=False)
        self.stack = ExitStack()
        nc = self.nc
        self.S = Sched(nc, self.stack)
        self.dram = {}
        for name, shp in params_shapes.items():
            dt_ = BF16 if name.startswith("c_dft") else F32
            self.dram[name] = nc.dram_tensor(name, list(shp), dt_, kind="ExternalInput").ap()
        self.x = nc.dram_tensor("x", [nseq, T - NMETA, D], F32, kind="ExternalInput").ap()
        self.out = nc.dram_tensor("out", [nseq, T - NMETA, D], F32, kind="ExternalOutput").ap()
        self.h = nc.dram_tensor("h_scr", [nseq, T, D], F32).ap()
        self.G = nc.dram_tensor("g_scr", [nseq, E, T], BF16).ap()
        self.uid = 0
        self.cur = self.stack

    def sb(self, name, shape, dtype=F32):
        self.uid += 1
        return self.cur.enter_context(self.nc.sbuf_tensor("%s_%d" % (name, self.uid), list(shape), dtype))

    def open_scope(self):
        self.cur = ExitStack()

    def close_scope(self):
        self.S.barrier()
        self.cur.close()
        self.cur = self.stack

    def ring(self, name, n, shape, dtype=F32):
        return Ring(name, [self.sb("%s_%d" % (name, i), shape, dtype) for i in range(n)])

    def dt(self, name, shape, dtype=F32):
        return self.nc.dram_tensor(name, list(shape), dtype).ap()

    def setup_common(self):
        nc, S = self.nc, self.S
        pb = [self.stack.enter_context(nc.psum_tensor("ps%d" % i, [128, 512], F32)) for i in range(6)]
        self.PB = pb
        self.PBK = ["P%d" % i for i in range(6)]
        self.PW = self.stack.enter_context(nc.psum_tensor("pw", [128, 1024], F32))
        self.ps = Ring("ps", pb, self.PBK)
        self.ident = self.sb("ident", [128, 128], F32)
        self.identb = self.sb("identb", [128, 128], BF16)
        self.ones_b = self.sb("ones_b", [128, 128], BF16)
        self.ones_f = self.sb("ones_f", [128, 128], F32)
        S.dma("sp", self.ident[:], self.dram["c_ident"], writes=["ident"])
        S.op("dve", lambda e: e.tensor_copy(out=self.identb[:], in_=self.ident[:]), reads=["ident"], writes=["identb"])
        S.op("dve", lambda e: e.memset(self.ones_b[:], 1.0), writes=["ones_b"])
        S.op("dve", lambda e: e.memset(self.ones_f[:], 1.0), writes=["ones_f"])
        self.ht = self.ring("ht", 1, [128, D], F32)
        self.junk = self.ring("junk", 1, [128, D], F32)
        self.col = self.ring("col", 4, [128, 1], F32)

    def setup_am(self):
        self.yT = self.sb("yT", [128, KC, self.T], BF16)
        self.wring = self.ring("wf", 2, [128, KC, 128], F32)
        self.wbring = self.ring("wb", 2, [128, KC, 128], BF16)
        self.gpre = self.sb("gpre", [128, KC], F32)

    def setup_z(self):
        self.gpost = self.sb("gpost", [128, D], F32)
        self.bout = self.sb("bout", [128, D], F32)
        self.wout = self.sb("wout", [128, E // 128, D], BF16)
        self.woutf = self.ring("woutf", 2, [128, D], F32)
        self.gt = self.ring("gt", 2, [128, E // 128, 128], BF16)
        self.ot = self.ring("ot", 2, [128, D], F32)

    def init_h(self):
        S = self.S
        for s in range(self.nseq):
            S.dma("sp", self.h[s, 0:NMETA, :], self.dram["meta"], writes=[("h", s)])
            S.dma("sp", self.h[s, NMETA:, :], self.x[s], writes=[("h", s)])

    def phase_a(self, li, s):
        nc, S, T = self.nc, self.S, self.T
        S.dma("sp", self.gpre[:], self.dram["norm_pre"][li].rearrange("(kc p) -> p kc", p=128),
              reads=[], writes=["gpre"], allow_slow_non_contiguous=True)
        pw = self.PW
        pwk = ["PWa", "PWb"]
        for (t0, nt) in tok_blocks(T, 128):
            ht, hk = self.ht.next()
            S.dma("sp", ht[:nt, :], self.h[s, t0:t0 + nt, :], reads=[("h", s)], writes=[hk])
            jk, jkk = self.junk.next()
            cs, ck = self.col.next()
            S.op("dve", lambda e: e.memset(cs[:nt, :], 0.0), writes=[ck])
            S.op("act", lambda e: e.activation(out=jk[:nt, :], in_=ht[:nt, :], func=AF.Square, accum_out=cs[:nt, :]),
                 reads=[hk, ck], writes=[jkk, ck])
            S.op("act", lambda e: e.activation(out=cs[:nt, :], in_=cs[:nt, :], func=AF.Ln, scale=1.0 / D, bias=1e-6),
                 reads=[ck], writes=[ck])
            S.op("act", lambda e: e.activation(out=cs[:nt, :], in_=cs[:nt, :], func=AF.Exp, scale=-0.5),
                 reads=[ck], writes=[ck])
            S.op("act", lambda e: e.activation(out=jk[:nt, :], in_=ht[:nt, :], func=AF.Identity, scale=cs[:nt, :]),
                 reads=[hk, ck], writes=[jkk])
            for kc in range(KC):
                S.op("pe", lambda e: e.transpose(out=pw[:, kc * 128:kc * 128 + nt], in_=jk[:nt, kc * 128:(kc + 1) * 128],
                                                 identity=self.ident[:nt, :nt]),
                     reads=[jkk, "ident"], writes=pwk)
            pv = pw[:].rearrange("p (kc t) -> p kc t", kc=KC)[:, :, :nt]
            S.op("dve", lambda e: e.tensor_tensor(out=self.yT[:, :, t0:t0 + nt], in0=pv,
                                                  in1=self.gpre[:].unsqueeze(2).to_broadcast([128, KC, nt]), op=ALU.mult),
                 reads=pwk + ["gpre"], writes=["yT"])

    def load_wcols(self, wname, li_idx, col0, width):
        S = self.S
        wf, wfk = self.wring.next()
        wb, wbk = self.wbring.next()
        src = self.dram[wname][li_idx][:, col0:col0 + width].rearrange("(kc p) w -> p kc w", p=128)
        S.dma("pool", wf[:, :, :width], src, writes=[wfk])
        S.op("pool", lambda e: e.tensor_copy(out=wb[:, :, :width], in_=wf[:, :, :width]), reads=[wfk], writes=[wbk])
        return wb, wbk

    def proj_fm(self, wb, wbk, width, t0, nt, ring=None):
        S = self.S
        ps, pk = (ring or self.ps).next()
        for kc in range(KC):
            S.op("pe", lambda e: e.matmul(ps[:width, :nt], lhsT=wb[:, kc, :width], rhs=self.yT[:, kc, t0:t0 + nt],
                                          start=(kc == 0), stop=(kc == KC - 1)),
                 reads=[wbk, "yT"], writes=[pk])
        return ps, pk

    def phase_z(self, li, s, wout_name, j, bias_name=None, final=False):
        nc, S, T = self.nc, self.S, self.T
        pw = self.PW
        pwk = ["PWa", "PWb"]
        if s == 0:
            S.dma("sp", self.gpost[:], self.dram["norm_post"][li].partition_broadcast(128), writes=["gpost"])
            if bias_name is not None:
                S.dma("sp", self.bout[:], self.dram[bias_name][j].partition_broadcast(128), writes=["bout"])
            for ec in range(E // 128):
                wf, wfk = self.woutf.next()
                S.dma("pool", wf[:], self.dram[wout_name][j][ec * 128:(ec + 1) * 128, :], writes=[wfk])
                S.op("pool", lambda e: e.tensor_copy(out=self.wout[:, ec, :], in_=wf[:]), reads=[wfk], writes=["wout"])
        for (t0, nt) in tok_blocks(T, 128):
            gt, gk = self.gt.next()
            S.dma("sp", gt[:, :, :nt], self.G[s, :, t0:t0 + nt].rearrange("(ec p) t -> p ec t", p=128),
                  reads=[("G", s)], writes=[gk])
            ht, hk = self.ht.next()
            S.dma("sp", ht[:nt, :], self.h[s, t0:t0 + nt, :], reads=[("h", s)], writes=[hk])
            for half in range(2):
                for ec in range(E // 128):
                    S.op("pe", lambda e: e.matmul(pw[:nt, half * 512:(half + 1) * 512], lhsT=gt[:, ec, :nt],
                                                  rhs=self.wout[:, ec, half * 512:(half + 1) * 512],
                                                  start=(ec == 0), stop=(ec == E // 128 - 1)),
                         reads=[gk, "wout"], writes=pwk)
            ot, ok = self.ot.next()
            if bias_name is not None:
                S.op("dve", lambda e: e.tensor_tensor(out=ot[:nt, :], in0=pw[:nt, :], in1=self.bout[:nt, :], op=ALU.add),
                     reads=pwk + ["bout"], writes=[ok])
            else:
                S.op("dve", lambda e: e.tensor_copy(out=ot[:nt, :], in_=pw[:nt, :]), reads=pwk, writes=[ok])
            jk, jkk = self.junk.next()
            cs, ck = self.col.next()
            S.op("dve", lambda e: e.memset(cs[:nt, :], 0.0), writes=[ck])
            S.op("act", lambda e: e.activation(out=jk[:nt, :], in_=ot[:nt, :], func=AF.Square, accum_out=cs[:nt, :]),
                 reads=[ok, ck], writes=[jkk, ck])
            S.op("act", lambda e: e.activation(out=cs[:nt, :], in_=cs[:nt, :], func=AF.Ln, scale=1.0 / D, bias=1e-6),
                 reads=[ck], writes=[ck])
            S.op("act", lambda e: e.activation(out=cs[:nt, :], in_=cs[:nt, :], func=AF.Exp, scale=-0.5),
                 reads=[ck], writes=[ck])
            S.op("dve", lambda e: e.scalar_tensor_tensor(out=ot[:nt, :], in0=ot[:nt, :], scalar=cs[:nt, :],
                                                         in1=self.gpost[:nt, :], op0=ALU.mult, op1=ALU.mult),
                 reads=[ok, ck, "gpost"], writes=[ok])
            S.op("dve", lambda e: e.tensor_tensor(out=ot[:nt, :], in0=ot[:nt, :], in1=ht[:nt, :], op=ALU.add),
                 reads=[ok, hk], writes=[ok])
            if not final:
                S.dma("sp", self.h[s, t0:t0 + nt, :], ot[:nt, :], reads=[ok], writes=[("h", s)])
            else:
                if t0 == 0:
                    S.dma("sp", self.out[s, 0:nt - NMETA, :], ot[NMETA:nt, :], reads=[ok], writes=[("out", s)])
                else:
                    S.dma("sp", self.out[s, t0 - NMETA:t0 - NMETA + nt, :], ot[:nt, :], reads=[ok], writes=[("out", s)])

    def head_norm_gate(self, o, okey, nt, gain_col, gate, gatekey, s, row0, t0, eps, extra_scale):
        S = self.S
        sq, sqk = self.hn_sq.next()
        S.op("act", lambda e: e.activation(out=sq[:, :nt], in_=o[:, :nt], func=AF.Square), reads=[okey], writes=[sqk])
        ps, pk = self.hn_ps.next()
        S.op("pe", lambda e: e.matmul(ps[:, :nt], lhsT=self.ones_b[:], rhs=sq[:, :nt], start=True, stop=True),
             reads=["ones_b", sqk], writes=[pk])
        r, rk = self.hn_r.next()
        S.op("act", lambda e: e.activation(out=r[:, :nt], in_=ps[:, :nt], func=AF.Ln, scale=1.0 / 128, bias=eps), reads=[pk], writes=[rk])
        S.op("act", lambda e: e.activation(out=r[:, :nt], in_=r[:, :nt], func=AF.Exp, scale=-0.5), reads=[rk], writes=[rk])
        S.op("dve", lambda e: e.scalar_tensor_tensor(out=r[:, :nt], in0=o[:, :nt], scalar=gain_col, in1=r[:, :nt],
                                                     op0=ALU.mult, op1=ALU.mult), reads=[okey, rk, "hn_gain"], writes=[rk])
        g, gk = self.hn_g.next()
        S.op("dve", lambda e: e.scalar_tensor_tensor(out=g[:, :nt], in0=r[:, :nt], scalar=float(extra_scale), in1=gate,
                                                     op0=ALU.mult, op1=ALU.mult), reads=[rk, gatekey], writes=[gk])
        S.dma("sp", self.G[s, row0:row0 + 128, t0:t0 + nt], g[:, :nt], reads=[gk], writes=[("G", s)])

    def setup_head_norm(self):
        self.hn_sq = self.ring("hn_sq", 2, [128, 512], BF16)
        self.hn_r = self.ring("hn_r", 2, [128, 512], F32)
        self.hn_g = self.ring("hn_g", 2, [128, 512], BF16)

    def setup_attention(self):
        T = self.T
        self.setup_head_norm()
        self.da_cos = self.sb("da_cos", [128, T], F32)
        self.da_sin = self.sb("da_sin", [128, T], F32)
        self.da_perm = self.sb("da_perm", [128, 128], BF16)
        self.da_permf = self.sb("da_permf", [128, 128], F32)
        self.da_q = self.sb("da_q", [128, T], BF16)
        self.da_k = self.sb("da_k", [128, T], BF16)
        self.da_v = self.sb("da_v", [128, (T + 127) // 128, 128], BF16)
        self.da_sz = self.sb("da_sz", [128, T], BF16)
        self.da_xb = self.ring("da_xb", 2, [128, 512], BF16)
        self.da_t = self.ring("da_t", 6, [128, 512], F32)
        self.da_p = self.ring("da_p", 4, [128, 512], BF16)
        self.da_lv = self.sb("da_lv", [128, 4, 64], F32)
        self.da_lam = self.sb("da_lam", [128, 4], F32)
        self.da_gain = self.sb("da_gain", [128, 1], F32)
        self.da_o = self.ring("da_o", 2, [128, 512], F32)

    def mixer_attention(self, li, j, s, layer_idx):
        nc, S, T = self.nc, self.S, self.T
        lam_init = 0.8 - 0.6 * math.exp(-0.3 * layer_idx)
        if s == 0:
            S.dma("sp", self.da_cos[:], self.dram["c_rope_cos"], writes=["da_cos"])
            S.dma("sp", self.da_sin[:], self.dram["c_rope_sin"], writes=["da_sin"])
            S.dma("sp", self.da_permf[:], self.dram["c_rope_perm"], writes=["da_permf"])
            S.op("dve", lambda e: e.tensor_copy(out=self.da_perm[:], in_=self.da_permf[:]), reads=["da_permf"], writes=["da_perm"])
            S.dma("sp", self.da_gain[:], self.dram["da_subln"][j].rearrange("(p o) -> p o", o=1), writes=["hn_gain"])
            S.dma("sp", self.da_lv[:].rearrange("p a b -> p (a b)"),
                  self.dram["da_lambda"][j].rearrange("a b -> (a b)").partition_broadcast(128), writes=["da_lv"])
            lam = self.da_lam
            S.op("dve", lambda e: e.tensor_tensor(out=self.da_lv[:, 0, :], in0=self.da_lv[:, 0, :], in1=self.da_lv[:, 1, :], op=ALU.mult),
                 reads=["da_lv"], writes=["da_lv"])
            S.op("dve", lambda e: e.tensor_tensor(out=self.da_lv[:, 2, :], in0=self.da_lv[:, 2, :], in1=self.da_lv[:, 3, :], op=ALU.mult),
                 reads=["da_lv"], writes=["da_lv"])
            S.op("dve", lambda e: e.tensor_reduce(out=lam[:, 0:1], in_=self.da_lv[:, 0, :], axis=AX.X, op=ALU.add), reads=["da_lv"], writes=["da_lam"])
            S.op("dve", lambda e: e.tensor_reduce(out=lam[:, 1:2], in_=self.da_lv[:, 2, :], axis=AX.X, op=ALU.add), reads=["da_lv"], writes=["da_lam"])
            S.op("act", lambda e: e.activation(out=lam[:, 0:2], in_=lam[:, 0:2], func=AF.Exp), reads=["da_lam"], writes=["da_lam"])
            S.op("dve", lambda e: e.tensor_tensor(out=lam[:, 2:3], in0=lam[:, 1:2], in1=lam[:, 0:1], op=ALU.subtract), reads=["da_lam"], writes=["da_lam"])
            S.op("dve", lambda e: e.tensor_scalar(out=lam[:, 2:3], in0=lam[:, 2:3], scalar1=-lam_init, scalar2=None, op0=ALU.add),
                 reads=["da_lam"], writes=["da_lam"])
        blocks = tok_blocks(T, 512)
        tiles = tok_blocks(T, 128)
        acc = self.PB[0:4]
        acck = self.PBK[0:4]
        sring = Ring("sc", [self.PB[4], self.PB[5], self.PW[:, 0:512], self.PW[:, 512:1024]], ["P4", "P5", "PWa", "PWb"])
        self.hn_ps = Ring("hnps", [self.PB[4], self.PB[5]], ["P4", "P5"])
        import os
        cut = int(os.environ.get("ATTCUT", "9"))
        if os.environ.get("ATTRING"):
            sring = Ring("sc", [self.PB[4], self.PB[5]], ["P4", "P5"])
        for hd in range(16):
            if cut <= 0:
                continue
            for which, dst, dkey, col0 in (("q", self.da_q, "da_q", hd * 128), ("k", self.da_k, "da_k", E + hd * 128)):
                wb, wbk = self.load_wcols("da_w_in", j, col0, 128)
                for (t0, nt) in blocks:
                    ps, pk = self.proj_fm(wb, wbk, 128, t0, nt, ring=sring)
                    xf, xfk = self.da_t.next()
                    S.op("act", lambda e: e.activation(out=xf[:, :nt], in_=ps[:, :nt], func=AF.Identity), reads=[pk], writes=[xfk])
                    xb, xbk = self.da_xb.next()
                    S.op("pool", lambda e: e.tensor_copy(out=xb[:, :nt], in_=xf[:, :nt]), reads=[xfk], writes=[xbk])
                    pr, prk = sring.next()
                    S.op("pe", lambda e: e.matmul(pr[:, :nt], lhsT=self.da_perm[:], rhs=xb[:, :nt], start=True, stop=True),
                         reads=["da_perm", xbk], writes=[prk])
                    t1, t1k = self.da_t.next()
                    t2, t2k = self.da_t.next()
                    S.op("dve", lambda e: e.tensor_tensor(out=t1[:, :nt], in0=xf[:, :nt], in1=self.da_cos[:, t0:t0 + nt], op=ALU.mult),
                         reads=[xfk, "da_cos"], writes=[t1k])
                    S.op("dve", lambda e: e.tensor_tensor(out=t2[:, :nt], in0=pr[:, :nt], in1=self.da_sin[:, t0:t0 + nt], op=ALU.mult),
                         reads=[prk, "da_sin"], writes=[t2k])
                    S.op("dve", lambda e: e.tensor_tensor(out=dst[:, t0:t0 + nt], in0=t1[:, :nt], in1=t2[:, :nt], op=ALU.add),
                         reads=[t1k, t2k], writes=[dkey])
            if cut <= 1:
                continue
            wb, wbk = self.load_wcols("da_w_in", j, 2 * E + hd * 128, 128)
            for ti, (t0, nt) in enumerate(tiles):
                ps, pk = sring.next()
                for kc in range(KC):
                    S.op("pe", lambda e: e.matmul(ps[:nt, :128], lhsT=self.yT[:, kc, t0:t0 + nt], rhs=wb[:, kc, :],
                                                  start=(kc == 0), stop=(kc == KC - 1)), reads=[wbk, "yT"], writes=[pk])
                S.op("act", lambda e: e.activation(out=self.da_v[:nt, ti, :], in_=ps[:nt, :128], func=AF.Identity),
                     reads=[pk], writes=["da_v"])
            wb, wbk = self.load_wcols("da_w_in", j, 3 * E + hd * 128, 128)
            for (t0, nt) in blocks:
                ps, pk = self.proj_fm(wb, wbk, 128, t0, nt, ring=sring)
                S.op("act", lambda e: e.activation(out=self.da_sz[:, t0:t0 + nt], in_=ps[:, :nt], func=AF.Silu),
                     reads=[pk], writes=["da_sz"])
            if cut <= 2:
                continue
            for (q0, nq) in blocks:
                steps = [(ti, k0, nk, m) for ti, (k0, nk) in enumerate(tiles) for m in range(2)]

                def score(st):
                    ti, k0, nk, m = st
                    ps, pk = sring.next()
                    S.op("pe", lambda e: e.matmul(ps[:nk, :nq], lhsT=self.da_k[m * 64:(m + 1) * 64, k0:k0 + nk],
                                                  rhs=self.da_q[m * 64:(m + 1) * 64, q0:q0 + nq], start=True, stop=True),
                         reads=["da_k", "da_q"], writes=[pk])
                    return ps, pk
                pend = score(steps[0])
                for i, st in enumerate(steps):
                    ti, k0, nk, m = st
                    ps, pk = pend
                    if i + 1 < len(steps):
                        pend = score(steps[i + 1])
                    p, pkk = self.da_p.next()
                    S.op("act", lambda e: e.activation(out=p[:nk, :nq], in_=ps[:nk, :nq], func=AF.Exp, scale=0.125),
                         reads=[pk], writes=[pkk])
                    first = (ti == 0)
                    last = (ti == len(tiles) - 1)
                    S.op("pe", lambda e: e.matmul(acc[m][:, :nq], lhsT=self.da_v[:nk, ti, :], rhs=p[:nk, :nq], start=first, stop=last),
                         reads=["da_v", pkk], writes=[acck[m]])
                    S.op("pe", lambda e: e.matmul(acc[2 + m][:, :nq], lhsT=self.ones_b[:nk, :], rhs=p[:nk, :nq], start=first, stop=last),
                         reads=["ones_b", pkk], writes=[acck[2 + m]])
                if cut <= 3:
                    continue
                r0, r0k = self.da_t.next()
                r1, r1k = self.da_t.next()
                S.op("dve", lambda e: e.reciprocal(out=r0[:, :nq], in_=acc[2][:, :nq]), reads=[acck[2]], writes=[r0k])
                S.op("dve", lambda e: e.reciprocal(out=r1[:, :nq], in_=acc[3][:, :nq]), reads=[acck[3]], writes=[r1k])
                S.op("dve", lambda e: e.tensor_tensor(out=r0[:, :nq], in0=acc[0][:, :nq], in1=r0[:, :nq], op=ALU.mult),
                     reads=[acck[0], r0k], writes=[r0k])
                S.op("dve", lambda e: e.tensor_tensor(out=r1[:, :nq], in0=acc[1][:, :nq], in1=r1[:, :nq], op=ALU.mult),
                     reads=[acck[1], r1k], writes=[r1k])
                o, ok = self.da_o.next()
                S.op("dve", lambda e: e.scalar_tensor_tensor(out=o[:, :nq], in0=r1[:, :nq], scalar=self.da_lam[:, 2:3], in1=r0[:, :nq],
                                                             op0=ALU.mult, op1=ALU.add), reads=[r0k, r1k, "da_lam"], writes=[ok])
                self.head_norm_gate(o, ok, nq, self.da_gain[:, 0:1], self.da_sz[:, q0:q0 + nq], "da_sz", s, hd * 128, q0,
                                    1e-5, 1.0 - lam_init)

    def hy_dims(self):
        T = self.T
        self.NT = (T + 127) // 128
        self.NFC = (T + 1 + 127) // 128
        self.NK = 2 * self.NFC

    def sin_reduce(self, x, xk, kf, kfk, ki, kik, rows, n):
        S = self.S
        xx, k_, i_ = x[:rows, :n], kf[:rows, :n], ki[:rows, :n]
        S.op("dve", lambda e: e.tensor_scalar(out=k_, in0=xx, scalar1=1.0 / (2 * math.pi), scalar2=None, op0=ALU.mult), reads=[xk], writes=[kfk])
        S.op("dve", lambda e: e.tensor_copy(out=i_, in_=k_), reads=[kfk], writes=[kik])
        S.op("dve", lambda e: e.tensor_copy(out=k_, in_=i_), reads=[kik], writes=[kfk])
        S.op("dve", lambda e: e.scalar_tensor_tensor(out=xx, in0=k_, scalar=-2 * math.pi, in1=xx, op0=ALU.mult, op1=ALU.add), reads=[kfk, xk], writes=[xk])
        S.op("dve", lambda e: e.tensor_scalar(out=k_, in0=xx, scalar1=math.pi, scalar2=-2 * math.pi, op0=ALU.is_gt, op1=ALU.mult), reads=[xk], writes=[kfk])
        S.op("dve", lambda e: e.tensor_tensor(out=xx, in0=xx, in1=k_, op=ALU.add), reads=[xk, kfk], writes=[xk])
        S.op("dve", lambda e: e.tensor_scalar(out=k_, in0=xx, scalar1=-math.pi, scalar2=2 * math.pi, op0=ALU.is_lt, op1=ALU.mult), reads=[xk], writes=[kfk])
        S.op("dve", lambda e: e.tensor_tensor(out=xx, in0=xx, in1=k_, op=ALU.add), reads=[xk, kfk], writes=[xk])

    def hyena_filters(self, j):
        nc, S, T, NT = self.nc, self.S, self.T, self.NT
        self.HF = self.dt("hy_hf", [2, T, E], BF16)
        self.RN = self.dt("hy_rn", [128, E], F32)
        zT = self.sb("hy_zT", [33, T], F32)
        hA = self.sb("hy_hA", [64, T], F32)
        hB = self.sb("hy_hB", [64, T], F32)
        kf = self.sb("hy_kf", [64, T], F32)
        ki = self.sb("hy_ki", [64, T], mybir.dt.int32)
        w1 = self.sb("hy_w1", [33, 64], F32)
        w2 = self.sb("hy_w2", [64, 64], F32)
        w3 = self.sb("hy_w3", [64, 64], F32)
        w4 = self.sb("hy_w4", [64, 2 * E], F32)
        vec = self.sb("hy_fv", [64, 8], F32)
        delta = self.sb("hy_delta", [128, E], F32)
        tl = self.sb("hy_tl", [128, NT], F32)
        ssum = self.sb("hy_ssum", [128, 2 * E], F32)
        dec = self.ring("hy_dec", 2, [128, 512], F32)
        ft = self.ring("hy_ft", 2, [128, 512], F32)
        fa = self.ring("hy_fa", 2, [128, 512], F32)
        fb = self.ring("hy_fb", 2, [128, 512], BF16)
        S.dma("sp", zT[:], self.dram["c_hy_z"], writes=["hy_zT"])
        S.dma("sp", w1[:], self.dram["hy_f_w1"][j], writes=["hy_w"])
        S.dma("sp", w2[:], self.dram["hy_f_w2"][j], writes=["hy_w"])
        S.dma("sp", w3[:], self.dram["hy_f_w3"][j], writes=["hy_w"])
        S.dma("sp", w4[:], self.dram["hy_f_w4"][j], writes=["hy_w"])
        for i_, nm in enumerate(("hy_f_b1", "hy_f_b2", "hy_f_b3", "hy_f_freq")):
            S.dma("sp", vec[:, i_:i_ + 1], self.dram[nm][j].rearrange("(p o) -> p o", o=1), writes=["hy_fv"])
        S.dma("sp", delta[:], self.dram["c_hy_delta"].partition_broadcast(128), writes=["hy_delta"])
        S.dma("sp", tl[:], self.dram["c_hy_tl"], writes=["hy_tl"])
        for i_ in range(3):
            S.op("dve", lambda e: e.tensor_tensor(out=vec[:, 4 + i_:5 + i_], in0=vec[:, i_:i_ + 1], in1=vec[:, 3:4], op=ALU.mult), reads=["hy_fv"], writes=["hy_fv"])
        src, srck, krows = zT, "hy_zT", 33
        for li_, (w_, dst, dk_) in enumerate(((w1, hA, "hy_hA"), (w2, hB, "hy_hB"), (w3, hA, "hy_hA"))):
            for (t0, nt) in tok_blocks(T, 512):
                ps, pk = self.ps.next()
                S.op("pe", lambda e: e.matmul(ps[:64, :nt], lhsT=w_[:krows, :], rhs=src[:krows, t0:t0 + nt], start=True, stop=True),
                     reads=["hy_w", srck], writes=[pk])
                S.op("act", lambda e: e.activation(out=dst[:, t0:t0 + nt], in_=ps[:64, :nt], func=AF.Identity, scale=vec[:, 3:4],
                                                   bias=vec[:, 4 + li_:5 + li_]), reads=[pk, "hy_fv"], writes=[dk_])
            self.sin_reduce(dst, dk_, kf, "hy_kf", ki, "hy_ki", 64, T)
            S.op("act", lambda e: e.activation(out=dst[:, :], in_=dst[:, :], func=AF.Sin), reads=[dk_], writes=[dk_])
            src, srck, krows = dst, dk_, 64
        h3 = src
        for cb in range(2 * E // 512):
            half = cb // (E // 512)
            c0 = (cb % (E // 512)) * 512
            pacc, pacck = self.ps.next()
            for n in range(NT):
                t0 = n * 128
                nt = min(128, T - t0)
                ps, pk = self.ps.next()
                if ps is pacc:
                    ps, pk = self.ps.next()
                S.op("pe", lambda e: e.matmul(ps[:nt, :], lhsT=h3[:, t0:t0 + nt], rhs=w4[:, cb * 512:(cb + 1) * 512], start=True, stop=True),
                     reads=["hy_hA", "hy_w"], writes=[pk])
                d_, dk2 = dec.next()
                S.op("act", lambda e: e.activation(out=d_[:nt, :], in_=delta[:nt, c0:c0 + 512], func=AF.Exp, scale=tl[:nt, n:n + 1]),
                     reads=["hy_delta", "hy_tl"], writes=[dk2])
                f_, fk = ft.next()
                S.op("dve", lambda e: e.tensor_tensor(out=f_[:nt, :], in0=ps[:nt, :], in1=d_[:nt, :], op=ALU.mult), reads=[pk, dk2], writes=[fk])
                a_, ak = fa.next()
                S.op("act", lambda e: e.activation(out=a_[:nt, :], in_=f_[:nt, :], func=AF.Abs), reads=[fk], writes=[ak])
                S.op("pe", lambda e: e.matmul(pacc[:, :], lhsT=self.ones_f[:nt, :], rhs=a_[:nt, :], start=(n == 0), stop=(n == NT - 1)),
                     reads=["ones_f", ak], writes=[pacck])
                b_, bk = fb.next()
                S.op("pool", lambda e: e.tensor_copy(out=b_[:nt, :], in_=f_[:nt, :]), reads=[fk], writes=[bk])
                if half == 1 and n == 0:
                    S.op("pool", lambda e: e.memset(b_[0:1, :], 0.0), reads=[bk], writes=[bk])
                S.dma("sp", self.HF[half, t0:t0 + nt, c0:c0 + 512], b_[:nt, :], reads=[bk], writes=["HF"])
            S.op("dve", lambda e: e.tensor_copy(out=ssum[:, cb * 512:(cb + 1) * 512], in_=pacc[:, :]), reads=[pacck], writes=["hy_ssum"])
        S.op("dve", lambda e: e.tensor_tensor(out=ssum[:, 0:E], in0=ssum[:, 0:E], in1=ssum[:, E:2 * E], op=ALU.add), reads=["hy_ssum"], writes=["hy_ssum"])
        S.op("dve", lambda e: e.tensor_scalar(out=ssum[:, 0:E], in0=ssum[:, 0:E], scalar1=1e-6, scalar2=None, op0=ALU.add), reads=["hy_ssum"], writes=["hy_ssum"])
        S.op("dve", lambda e: e.reciprocal(out=ssum[:, 0:E], in_=ssum[:, 0:E]), reads=["hy_ssum"], writes=["hy_ssum"])
        S.dma("sp", self.RN, ssum[:, 0:E], reads=["hy_ssum"], writes=["RN"])

    def setup_hyena_proj(self):
        T = self.T
        self.hy_xp = self.sb("hy_xp", [128, T + 2], F32)
        self.hy_st = [self.sb("hy_st%d" % i, [128, T], F32) for i in range(3)]
        self.hy_pv = self.sb("hy_pvec", [128, 8], F32)
        self.hy_bf = self.ring("hy_bf", 2, [128, T], BF16)
        self.hy_tb = self.ring("hy_tb", 2, [128, 128], BF16)
        self.hy_sz = self.ring("hy_sz", 2, [128, 512], F32)
        self.VX = self.dt("hy_vx", [self.nseq, E, T], BF16)
        self.GX = self.dt("hy_gx", [self.nseq, E, T], BF16)
        self.VXT = self.dt("hy_vxt", [self.nseq, T, E], BF16)

    def hyena_proj(self, j, s):
        nc, S, T, NT = self.nc, self.S, self.T, self.NT
        xp, pv = self.hy_xp, self.hy_pv
        if s == 0:
            S.op("dve", lambda e: e.memset(xp[:], 0.0), writes=["hy_xp"])
        col1 = lambda ap: ap.rearrange("(p o) -> p o", o=1)
        for c in range(E // 128):
            for si in range(3):
                col0 = si * E + c * 128
                wb, wbk = self.load_wcols("hy_w_in", j, col0, 128)
                S.dma("sp", pv[:, 0:1], col1(self.dram["hy_b_in"][j][col0:col0 + 128]), writes=["hy_pv"])
                S.dma("sp", pv[:, 1:4], self.dram["hy_conv_w"][j][:, col0:col0 + 128].rearrange("k p -> p k"), writes=["hy_pv"],
                      allow_slow_non_contiguous=True)
                S.dma("sp", pv[:, 4:5], col1(self.dram["hy_conv_b"][j][col0:col0 + 128]), writes=["hy_pv"])
                for (t0, nt) in tok_blocks(T, 512):
                    ps, pk = self.proj_fm(wb, wbk, 128, t0, nt)
                    S.op("act", lambda e: e.activation(out=xp[:, 1 + t0:1 + t0 + nt], in_=ps[:, :nt], func=AF.Identity, bias=pv[:, 0:1]),
                         reads=[pk, "hy_pv"], writes=["hy_xp"])
                st, stk = self.hy_st[si], "hy_st%d" % si
                S.op("dve", lambda e: e.tensor_scalar(out=st[:], in0=xp[:, 0:T], scalar1=pv[:, 1:2], scalar2=pv[:, 4:5], op0=ALU.mult, op1=ALU.add),
                     reads=["hy_xp", "hy_pv"], writes=[stk])
                S.op("dve", lambda e: e.scalar_tensor_tensor(out=st[:], in0=xp[:, 1:1 + T], scalar=pv[:, 2:3], in1=st[:], op0=ALU.mult, op1=ALU.add),
                     reads=["hy_xp", "hy_pv", stk], writes=[stk])
                S.op("dve", lambda e: e.scalar_tensor_tensor(out=st[:], in0=xp[:, 2:2 + T], scalar=pv[:, 3:4], in1=st[:], op0=ALU.mult, op1=ALU.add),
                     reads=["hy_xp", "hy_pv", stk], writes=[stk])
            x0, x1, v = self.hy_st
            S.op("dve", lambda e: e.tensor_tensor(out=v[:], in0=v[:], in1=x1[:], op=ALU.mult), reads=["hy_st2", "hy_st1"], writes=["hy_st2"])
            vb, vbk = self.hy_bf.next()
            S.op("pool", lambda e: e.tensor_copy(out=vb[:], in_=v[:]), reads=["hy_st2"], writes=[vbk])
            S.dma("sp", self.VX[s, c * 128:(c + 1) * 128, :], vb[:], reads=[vbk], writes=[("VX", s)])
            for n in range(NT):
                t0 = n * 128
                nt = min(128, T - t0)
                ps, pk = self.ps.next()
                S.op("pe", lambda e: e.transpose(out=ps[:nt, :128], in_=v[:, t0:t0 + nt], identity=self.ident[:]), reads=["hy_st2", "ident"], writes=[pk])
                tb, tbk = self.hy_tb.next()
                S.op("act", lambda e: e.activation(out=tb[:nt, :], in_=ps[:nt, :128], func=AF.Identity), reads=[pk], writes=[tbk])
                S.dma("sp", self.VXT[s, t0:t0 + nt, c * 128:(c + 1) * 128], tb[:nt, :], reads=[tbk], writes=[("VXT", s)])
            col0 = 3 * E + c * 128
            wb, wbk = self.load_wcols("hy_w_in", j, col0, 128)
            S.dma("sp", pv[:, 5:6], col1(self.dram["hy_b_in"][j][col0:col0 + 128]), writes=["hy_pv"])
            for (t0, nt) in tok_blocks(T, 512):
                ps, pk = self.proj_fm(wb, wbk, 128, t0, nt)
                sz, szk = self.hy_sz.next()
                S.op("act", lambda e: e.activation(out=sz[:, :nt], in_=ps[:, :nt], func=AF.Silu, bias=pv[:, 5:6]), reads=[pk, "hy_pv"], writes=[szk])
                S.op("dve", lambda e: e.tensor_tensor(out=x0[:, t0:t0 + nt], in0=x0[:, t0:t0 + nt], in1=sz[:, :nt], op=ALU.mult),
                     reads=["hy_st0", szk], writes=["hy_st0"])
            gb, gbk = self.hy_bf.next()
            S.op("pool", lambda e: e.tensor_copy(out=gb[:], in_=x0[:]), reads=["hy_st0"], writes=[gbk])
            S.dma("sp", self.GX[s, c * 128:(c + 1) * 128, :], gb[:], reads=[gbk], writes=[("GX", s)])

    def load_tokmajor(self, dst, dkey, src2d, c0, ncols, rkeys):
        S, T, NT = self.S, self.T, self.NT
        nfull = T // 128
        if nfull:
            S.dma("sp", dst[:, 0:nfull, :ncols], src2d[0:nfull * 128, c0:c0 + ncols].rearrange("(n p) c -> p n c", p=128),
                  reads=rkeys, writes=[dkey])
        rem = T - nfull * 128
        if rem:
            S.dma("sp", dst[:rem, nfull, :ncols], src2d[nfull * 128:T, c0:c0 + ncols], reads=rkeys, writes=[dkey])

    def hyena_spectral(self, j):
        nc, S, T, NT, NFC, NK = self.nc, self.S, self.T, self.NT, self.NFC, self.NK
        Fm = self.dram["c_dft_f"]
        Gm = self.dram["c_dft_g"]
        self.HS = self.dt("hy_hs", [NK * 128, E], F32)
        fch = self.ring("hy_fch", 2, [128, NT, 128], BF16)
        rn = self.sb("hy_rnb", [128, E], F32)
        S.dma("sp", rn[:], self.RN, reads=["RN"], writes=["hy_rnb"])
        dat = [self.sb("hy_dat%d" % i, [128, NT, 512], BF16) for i in range(3)]
        hs_t = self.ring("hy_hst", 2, [128, 512], F32)
        for i_ in range(3):
            S.op("pool", lambda e: e.memset(dat[i_][:], 0.0), writes=["hy_dat%d" % i_])

        def load_f(kc):
            f_, fk = fch.next()
            self.load_tokmajor(f_, fk, Fm, kc * 128, 128, [])
            return f_, fk

        def fwd(kc, srcs, ps, pk):
            f_, fk = load_f(kc)
            tot = len(srcs) * NT
            i_ = 0
            for (d_, dk_) in srcs:
                for n in range(NT):
                    nt = min(128, T - n * 128)
                    S.op("pe", lambda e: e.matmul(ps[:, :], lhsT=f_[:nt, n, :], rhs=d_[:nt, n, :], start=(i_ == 0), stop=(i_ == tot - 1)),
                         reads=[fk, dk_], writes=[pk])
                    i_ += 1
        for cb in range(E // 512):
            c0 = cb * 512
            self.load_tokmajor(dat[0], "hy_dat0", self.HF[0], c0, 512, ["HF"])
            self.load_tokmajor(dat[1], "hy_dat1", self.HF[1], c0, 512, ["HF"])
            S.op("pool", lambda e: e.tensor_scalar(out=dat[2][:].rearrange("p n c -> p (n c)"), in0=dat[1][:].rearrange("p n c -> p (n c)"),
                                                   scalar1=-1.0, scalar2=None, op0=ALU.mult), reads=["hy_dat1"], writes=["hy_dat2"])
            for kc in range(NK):
                ps, pk = self.ps.next()
                second = (dat[1], "hy_dat1") if kc < NFC else (dat[2], "hy_dat2")
                fwd(kc, [(dat[0], "hy_dat0"), second], ps, pk)
                h_, hk = hs_t.next()
                S.op("dve", lambda e: e.tensor_tensor(out=h_[:], in0=ps[:, :], in1=rn[:, c0:c0 + 512], op=ALU.mult), reads=[pk, "hy_rnb"], writes=[hk])
                S.dma("sp", self.HS[kc * 128:(kc + 1) * 128, c0:c0 + 512], h_[:], reads=[hk], writes=["HS"])
        self.close_scope()
        self.open_scope()
        fch = self.ring("hy_fch2", 2, [128, NT, 128], BF16)
        skip = self.sb("hy_skip", [128, E // 128], F32)
        S.dma("sp", skip[:], self.dram["hy_skip"][j].rearrange("(c p) -> p c", p=128), writes=["hy_skip"], allow_slow_non_contiguous=True)
        dat = [self.sb("hy_dat0b", [128, NT, 512], BF16)]
        S.op("pool", lambda e: e.memset(dat[0][:], 0.0), writes=["hy_dat0"])
        zt = self.sb("hy_zt", [128, NK, 512], BF16)
        hr_t = self.ring("hy_hr", 2, [128, 512], F32)
        hi_t = self.ring("hy_hi", 2, [128, 512], F32)
        tmp = self.ring("hy_tmp", 4, [128, 512], F32)
        gti = self.ring("hy_gt", 3, [128, 512], BF16)
        vxb = self.ring("hy_vxb", 2, [128, 512], BF16)
        gxb = self.ring("hy_gxb", 2, [128, 512], BF16)
        yo = self.ring("hy_yo", 2, [128, 512], F32)
        go = self.ring("hy_go", 2, [128, 512], BF16)
        acc = Ring("hacc", self.PB[0:4], self.PBK[0:4])
        for s in range(self.nseq):
            for cb in range(E // 512):
                c0 = cb * 512
                self.load_tokmajor(dat[0], "hy_dat0", self.VXT[s], c0, 512, [("VXT", s)])
                for i_ in range(NFC):
                    pr, prk = self.PB[4], self.PBK[4]
                    pi, pik = self.PB[5], self.PBK[5]
                    fwd(i_, [(dat[0], "hy_dat0")], pr, prk)
                    fwd(NFC + i_, [(dat[0], "hy_dat0")], pi, pik)
                    hr, hrk = hr_t.next()
                    hi, hik = hi_t.next()
                    S.dma("sp", hr[:], self.HS[i_ * 128:(i_ + 1) * 128, c0:c0 + 512], reads=["HS"], writes=[hrk])
                    S.dma("sp", hi[:], self.HS[(NFC + i_) * 128:(NFC + i_ + 1) * 128, c0:c0 + 512], reads=["HS"], writes=[hik])
                    t1, t1k = tmp.next()
                    t2, t2k = tmp.next()
                    S.op("dve", lambda e: e.tensor_tensor(out=t1[:], in0=pr[:, :], in1=hr[:], op=ALU.mult), reads=[prk, hrk], writes=[t1k])
                    S.op("dve", lambda e: e.tensor_tensor(out=t2[:], in0=pi[:, :], in1=hi[:], op=ALU.mult), reads=[pik, hik], writes=[t2k])
                    S.op("dve", lambda e: e.tensor_tensor(out=zt[:, i_, :], in0=t1[:], in1=t2[:], op=ALU.subtract), reads=[t1k, t2k], writes=["hy_zt"])
                    t3, t3k = tmp.next()
                    t4, t4k = tmp.next()
                    S.op("dve", lambda e: e.tensor_tensor(out=t3[:], in0=pr[:, :], in1=hi[:], op=ALU.mult), reads=[prk, hik], writes=[t3k])
                    S.op("dve", lambda e: e.tensor_tensor(out=t4[:], in0=pi[:, :], in1=hr[:], op=ALU.mult), reads=[pik, hrk], writes=[t4k])
                    S.op("dve", lambda e: e.tensor_tensor(out=zt[:, NFC + i_, :], in0=t3[:], in1=t4[:], op=ALU.add), reads=[t3k, t4k], writes=["hy_zt"])
                for (t0, nt) in tok_blocks(T, 512):
                    for kc in range(NK):
                        g_, gk = gti.next()
                        S.dma("sp", g_[:, :nt], Gm[kc * 128:(kc + 1) * 128, t0:t0 + nt], writes=[gk])
                        for cs in range(4):
                            S.op("pe", lambda e: e.matmul(acc.bufs[cs][:, :nt], lhsT=zt[:, kc, cs * 128:(cs + 1) * 128], rhs=g_[:, :nt],
                                                          start=(kc == 0), stop=(kc == NK - 1)), reads=["hy_zt", gk], writes=[acc.keys[cs]])
                    for cs in range(4):
                        r0 = c0 + cs * 128
                        vx_, vxk = vxb.next()
                        gx_, gxk = gxb.next()
                        S.dma("sp", vx_[:, :nt], self.VX[s, r0:r0 + 128, t0:t0 + nt], reads=[("VX", s)], writes=[vxk])
                        S.dma("sp", gx_[:, :nt], self.GX[s, r0:r0 + 128, t0:t0 + nt], reads=[("GX", s)], writes=[gxk])
                        y_, yk = yo.next()
                        S.op("dve", lambda e: e.scalar_tensor_tensor(out=y_[:, :nt], in0=vx_[:, :nt], scalar=skip[:, r0 // 128:r0 // 128 + 1],
                                                                     in1=acc.bufs[cs][:, :nt], op0=ALU.mult, op1=ALU.add),
                             reads=[vxk, "hy_skip", acc.keys[cs]], writes=[yk])
                        g2, g2k = go.next()
                        S.op("dve", lambda e: e.tensor_tensor(out=g2[:, :nt], in0=y_[:, :nt], in1=gx_[:, :nt], op=ALU.mult), reads=[yk, gxk], writes=[g2k])
                        S.dma("sp", self.G[s, r0:r0 + 128, t0:t0 + nt], g2[:, :nt], reads=[g2k], writes=[("G", s)])

    def setup_gdn(self):
        T = self.T
        self.NT = (T + 127) // 128
        Tp = self.NT * 128
        self.Tp = Tp
        NT = self.NT
        self.setup_head_norm()
        self.gd_c = self.sb("gd_c", [128, 10, 128], F32)
        self.gd_xp = self.sb("gd_xp", [128, Tp + 2], BF16)
        self.gd_xs = self.sb("gd_xs", [128, Tp + 2], F32)
        self.gd_q = self.sb("gd_q", [128, Tp], BF16)
        self.gd_k = self.sb("gd_k", [128, Tp], BF16)
        self.gd_ktok = self.sb("gd_ktok", [128, NT, 128], BF16)
        self.gd_vtok = self.sb("gd_vtok", [128, NT, 128], BF16)
        self.gd_o = self.sb("gd_o", [128, Tp], BF16)
        self.gd_tab = {n: self.sb("gd_" + n, [128, NT, 32], F32) for n in ("gam", "beta", "nb", "ksc", "egl")}
        self.gd_cw = self.sb("gd_cw", [128, 3], F32)
        self.gd_gc = self.sb("gd_gc", [128, 2], F32)
        self.gd_gain = self.sb("gd_gain", [128, 1], F32)
        self.gd_wgb = self.sb("gd_wgb", [128, KC, 128], BF16)
        self.gd_sq = self.hn_sq
        self.gd_r = self.hn_r
        self.gd_gtn = self.ring("gd_gtn", 2, [128, 128], F32)
        self.gd_m = self.ring("gd_m", 12, [128, 128], F32)
        self.gd_e = self.ring("gd_e", 4, [128, 128], F32)
        self.gd_mb = self.ring("gd_mb", 8, [128, 128], BF16)
        self.gd_kk = self.ring("gd_kk", 2, [128, 3, 128], F32)
        self.gd_S = [self.sb("gd_S%d" % i, [128, 128], F32) for i in range(4)]
        self.gd_Sb = [self.sb("gd_Sb%d" % i, [128, 128], BF16) for i in range(4)]
        self.gd_sz = self.ring("gd_sz", 2, [128, 512], BF16)

    def gdn_qkv_fm(self, j, col0, cw_row0, dst_fp32):
        S, T, Tp = self.S, self.T, self.Tp
        wb, wbk = self.load_wcols("gdn_w_in", j, col0, 128)
        S.dma("sp", self.gd_cw[:], self.dram["gdn_conv_w"][j][:, cw_row0:cw_row0 + 128].rearrange("k p -> p k"),
              writes=["gd_cw"], allow_slow_non_contiguous=True)
        xp = self.gd_xp
        for (t0, nt) in tok_blocks(T, 512):
            ps, pk = self.proj_fm(wb, wbk, 128, t0, nt)
            S.op("act", lambda e: e.activation(out=xp[:, 1 + t0:1 + t0 + nt], in_=ps[:, :nt], func=AF.Identity), reads=[pk], writes=["gd_xp"])
        xs = dst_fp32
        S.op("dve", lambda e: e.tensor_scalar(out=xs[:, 1:1 + T], in0=xp[:, 0:T], scalar1=self.gd_cw[:, 0:1], scalar2=None, op0=ALU.mult),
             reads=["gd_xp", "gd_cw"], writes=["gd_xs"])
        S.op("dve", lambda e: e.scalar_tensor_tensor(out=xs[:, 1:1 + T], in0=xp[:, 1:1 + T], scalar=self.gd_cw[:, 1:2], in1=xs[:, 1:1 + T],
                                                     op0=ALU.mult, op1=ALU.add), reads=["gd_xp", "gd_cw", "gd_xs"], writes=["gd_xs"])
        S.op("dve", lambda e: e.scalar_tensor_tensor(out=xs[:, 1:1 + T], in0=xp[:, 2:2 + T], scalar=self.gd_cw[:, 2:3], in1=xs[:, 1:1 + T],
                                                     op0=ALU.mult, op1=ALU.add), reads=["gd_xp", "gd_cw", "gd_xs"], writes=["gd_xs"])
        S.op("act", lambda e: e.activation(out=xs[:, 1:1 + T], in_=xs[:, 1:1 + T], func=AF.Silu), reads=["gd_xs"], writes=["gd_xs"])

    def gdn_l2norm_to(self, dst_bf, dkey, scale):
        S, T = self.S, self.T
        xs = self.gd_xs
        for (t0, nt) in tok_blocks(T, 512):
            sq, sqk = self.gd_sq.next()
            S.op("act", lambda e: e.activation(out=sq[:, :nt], in_=xs[:, 1 + t0:1 + t0 + nt], func=AF.Square), reads=["gd_xs"], writes=[sqk])
            ps, pk = self.ps.next()
            S.op("pe", lambda e: e.matmul(ps[:, :nt], lhsT=self.ones_b[:], rhs=sq[:, :nt], start=True, stop=True), reads=["ones_b", sqk], writes=[pk])
            r, rk = self.gd_r.next()
            S.op("act", lambda e: e.activation(out=r[:, :nt], in_=ps[:, :nt], func=AF.Ln, bias=1e-6), reads=[pk], writes=[rk])
            S.op("act", lambda e: e.activation(out=r[:, :nt], in_=r[:, :nt], func=AF.Exp, scale=-0.5), reads=[rk], writes=[rk])
            S.op("dve", lambda e: e.scalar_tensor_tensor(out=xs[:, 1 + t0:1 + t0 + nt], in0=xs[:, 1 + t0:1 + t0 + nt], scalar=float(scale),
                                                         in1=r[:, :nt], op0=ALU.mult, op1=ALU.mult), reads=["gd_xs", rk], writes=["gd_xs"])
        S.op("pool", lambda e: e.tensor_copy(out=dst_bf[:, 0:T], in_=xs[:, 1:1 + T]), reads=["gd_xs"], writes=[dkey])

    def gdn_to_tok(self, dst, dkey):
        S, NT = self.S, self.NT
        xs = self.gd_xs
        for n in range(NT):
            ps, pk = self.ps.next()
            S.op("pe", lambda e: e.transpose(out=ps[:, :128], in_=xs[:, 1 + n * 128:1 + (n + 1) * 128], identity=self.ident[:]),
                 reads=["gd_xs", "ident"], writes=[pk])
            S.op("act", lambda e: e.activation(out=dst[:, n, :], in_=ps[:, :128], func=AF.Identity), reads=[pk], writes=[dkey])

    def mixer_gdn(self, li, j, s):
        nc, S, T, Tp, NT = self.nc, self.S, self.T, self.Tp, self.NT
        C = self.gd_c
        tab = self.gd_tab
        Uf, Ub, SELL, SELF = C[:, 0, :], C[:, 1, :], C[:, 2, :], C[:, 3, :]
        maskA = [C[:, 4, :], C[:, 5, :]]
        maskT = [C[:, 6, :], C[:, 7, :]]
        nstrict = [C[:, 8, :], C[:, 9, :]]
        if s == 0:
            S.dma("sp", C[:], self.dram["c_gdn"], writes=["gd_c"])
            S.dma("sp", self.gd_gain[:], self.dram["gdn_o_norm"][j].rearrange("(p o) -> p o", o=1), writes=["hn_gain"])
            wg_, wgk_ = self.wring.next()
            S.op("dve", lambda e: e.memset(wg_[:], 0.0), writes=[wgk_])
            S.op("dve", lambda e: e.memset(self.gd_gc[:], 0.0), writes=["gd_gc"])
            for q4 in range(4):
                src = self.dram["gdn_w_in"][j][:, 3 * E + q4 * 16:3 * E + (q4 + 1) * 16].rearrange("(kc p) w -> p kc w", p=128)
                S.dma("sp", wg_[:, :, q4 * 32:q4 * 32 + 16], src, writes=[wgk_])
            for d in range(2):
                S.dma("sp", self.gd_gc[d * 32:d * 32 + 16, 0:1], self.dram["gdn_dt_bias"][j][d].rearrange("(p o) -> p o", o=1), writes=["gd_gc"])
                S.dma("sp", self.gd_gc[d * 32:d * 32 + 16, 1:2], self.dram["gdn_a_log"][j][d].rearrange("(p o) -> p o", o=1), writes=["gd_gc"])
            S.op("pool", lambda e: e.tensor_copy(out=self.gd_wgb[:], in_=wg_[:]), reads=[wgk_], writes=["gd_wgb"])
            S.op("act", lambda e: e.activation(out=self.gd_gc[0:64, 1:2], in_=self.gd_gc[0:64, 1:2], func=AF.Exp), reads=["gd_gc"], writes=["gd_gc"])
            S.op("dve", lambda e: e.tensor_scalar(out=self.gd_gc[0:64, 1:2], in0=self.gd_gc[0:64, 1:2], scalar1=-1.0, scalar2=None, op0=ALU.mult),
                 reads=["gd_gc"], writes=["gd_gc"])
        gf = self.gd_xs
        S.op("dve", lambda e: e.memset(gf[:], 0.0), writes=["gd_xs"])
        for (t0, nt) in tok_blocks(T, 512):
            ps, pk = self.proj_fm(self.gd_wgb, "gd_wgb", 128, t0, nt)
            S.op("act", lambda e: e.activation(out=gf[0:64, t0:t0 + nt], in_=ps[0:64, :nt], func=AF.Exp, bias=self.gd_gc[0:64, 0:1]),
                 reads=[pk, "gd_gc"], writes=["gd_xs"])
            S.op("act", lambda e: e.activation(out=gf[64:128, t0:t0 + nt], in_=ps[64:128, :nt], func=AF.Sigmoid), reads=[pk], writes=["gd_xs"])
        S.op("act", lambda e: e.activation(out=gf[0:64, 0:T], in_=gf[0:64, 0:T], func=AF.Ln, bias=1.0), reads=["gd_xs"], writes=["gd_xs"])
        S.op("dve", lambda e: e.tensor_scalar(out=gf[0:64, 0:T], in0=gf[0:64, 0:T], scalar1=self.gd_gc[0:64, 1:2], scalar2=None, op0=ALU.mult),
             reads=["gd_xs", "gd_gc"], writes=["gd_xs"])
        for n in range(NT):
            ps, pk = self.ps.next()
            S.op("pe", lambda e: e.transpose(out=ps[:, :128], in_=gf[:, n * 128:(n + 1) * 128], identity=self.ident[:]),
                 reads=["gd_xs", "ident"], writes=[pk])
            gtn, gtk = self.gd_gtn.next()
            S.op("act", lambda e: e.activation(out=gtn[:], in_=ps[:, :128], func=AF.Identity), reads=[pk], writes=[gtk])
            ps, pk = self.ps.next()
            S.op("pe", lambda e: e.matmul(ps[:, 0:16], lhsT=Uf, rhs=gtn[:, 0:16], start=True, stop=True), reads=["gd_c", gtk], writes=[pk])
            S.op("pe", lambda e: e.matmul(ps[:, 16:32], lhsT=Ub, rhs=gtn[:, 32:48], start=True, stop=True), reads=["gd_c", gtk], writes=[pk])
            S.op("dve", lambda e: e.tensor_copy(out=tab["gam"][:, n, :], in_=ps[:, 0:32]), reads=[pk], writes=["gd_gam"])
            S.op("pool", lambda e: e.tensor_copy(out=tab["beta"][:, n, 0:16], in_=gtn[:, 64:80]), reads=[gtk], writes=["gd_beta"])
            S.op("pool", lambda e: e.tensor_copy(out=tab["beta"][:, n, 16:32], in_=gtn[:, 96:112]), reads=[gtk], writes=["gd_beta"])
        for n in range(NT):
            ps, pk = self.ps.next()
            S.op("pe", lambda e: e.matmul(ps[:, 0:16], lhsT=SELL, rhs=tab["gam"][:, n, 0:16], start=True, stop=True), reads=["gd_c", "gd_gam"], writes=[pk])
            S.op("pe", lambda e: e.matmul(ps[:, 16:32], lhsT=SELF, rhs=tab["gam"][:, n, 16:32], start=True, stop=True), reads=["gd_c", "gd_gam"], writes=[pk])
            S.op("dve", lambda e: e.tensor_copy(out=tab["egl"][:, n, :], in_=ps[:, 0:32]), reads=[pk], writes=["gd_egl"])
        fl = lambda t: t[:].rearrange("p n c -> p (n c)")
        S.op("dve", lambda e: e.tensor_tensor(out=fl(tab["ksc"]), in0=fl(tab["egl"]), in1=fl(tab["gam"]), op=ALU.subtract), reads=["gd_egl", "gd_gam"], writes=["gd_ksc"])
        S.op("act", lambda e: e.activation(out=fl(tab["ksc"]), in_=fl(tab["ksc"]), func=AF.Exp), reads=["gd_ksc"], writes=["gd_ksc"])
        S.op("act", lambda e: e.activation(out=fl(tab["egl"]), in_=fl(tab["egl"]), func=AF.Exp), reads=["gd_egl", "gd_ksc"], writes=["gd_egl"])
        S.op("act", lambda e: e.activation(out=fl(tab["nb"]), in_=fl(tab["gam"]), func=AF.Exp), reads=["gd_gam"], writes=["gd_nb"])
        S.op("dve", lambda e: e.scalar_tensor_tensor(out=fl(tab["nb"]), in0=fl(tab["nb"]), scalar=-1.0, in1=fl(tab["beta"]), op0=ALU.mult, op1=ALU.mult),
             reads=["gd_nb", "gd_beta"], writes=["gd_nb"])
        import os
        dbg = bool(os.environ.get("GDNDBG"))
        if dbg:
            for nm in ("gam", "beta", "nb", "ksc", "egl"):
                d_ = nc.dram_tensor("dbg_" + nm, [128, NT, 32], F32, kind="ExternalOutput").ap()
                S.dma("sp", d_, tab[nm][:], reads=["gd_" + nm], writes=["dbg_" + nm])
        S.op("dve", lambda e: e.memset(self.gd_xs[:], 0.0), writes=["gd_xs"])
        S.op("dve", lambda e: e.memset(self.gd_xp[:], 0.0), writes=["gd_xp"])
        S.op("dve", lambda e: e.memset(self.gd_q[:], 0.0), writes=["gd_q"])
        S.op("dve", lambda e: e.memset(self.gd_k[:], 0.0), writes=["gd_k"])
        tabkeys = ["gd_gam", "gd_beta", "gd_nb", "gd_ksc", "gd_egl"]
        for kh in range(8):
            self.gdn_qkv_fm(j, kh * 128, kh * 128, self.gd_xs)
            self.gdn_l2norm_to(self.gd_q, "gd_q", 128 ** -0.5)
            self.gdn_qkv_fm(j, D + kh * 128, D + kh * 128, self.gd_xs)
            self.gdn_l2norm_to(self.gd_k, "gd_k", 1.0)
            self.gdn_to_tok(self.gd_ktok, "gd_ktok")
            for hv in range(2):
                h = kh * 2 + hv
                self.gdn_qkv_fm(j, 2 * D + h * 128, 2 * D + h * 128, self.gd_xs)
                self.gdn_to_tok(self.gd_vtok, "gd_vtok")
                for d in range(2):
                    col = d * 16 + h
                    Sf, Sb = self.gd_S[d], self.gd_Sb[d]
                    sk, sbk = "gd_S%d" % d, "gd_Sb%d" % d
                    S.op("dve", lambda e: e.memset(Sf[:], 0.0), writes=[sk])
                    S.op("dve", lambda e: e.memset(Sb[:], 0.0), writes=[sbk])
                    order = range(NT) if d == 0 else range(NT - 1, -1, -1)
                    for n in order:
                        c0 = n * 128
                        kc_ = self.gd_k[:, c0:c0 + 128]
                        qc_ = self.gd_q[:, c0:c0 + 128]
                        gcol = tab["gam"][:, n, col:col + 1]
                        kk, kkk = self.gd_kk.next()
                        ps, pk = self.ps.next()
                        S.op("pe", lambda e: e.matmul(ps[:, 0:128], lhsT=kc_, rhs=kc_, start=True, stop=True), reads=["gd_k"], writes=[pk])
                        S.op("pe", lambda e: e.matmul(ps[:, 128:256], lhsT=kc_, rhs=qc_, start=True, stop=True), reads=["gd_k", "gd_q"], writes=[pk])
                        S.op("dve", lambda e: e.tensor_tensor(out=kk[:, 0, :], in0=ps[:, 0:128], in1=nstrict[d], op=ALU.mult), reads=[pk, "gd_c"], writes=[kkk])
                        S.op("dve", lambda e: e.tensor_copy(out=kk[:, 1, :], in_=ps[:, 128:256]), reads=[pk], writes=[kkk])
                        dg, dgk = self.gd_m.next()
                        S.op("dve", lambda e: e.tensor_scalar(out=dg[:], in0=self.ident[:], scalar1=gcol, scalar2=None, op0=ALU.mult),
                             reads=["ident", "gd_gam"], writes=[dgk])
                        pg, pgk = self.ps.next()
                        S.op("pe", lambda e: e.matmul(pg[:, 0:128], lhsT=self.ones_f[:], rhs=dg[:], start=True, stop=True), reads=["ones_f", dgk], writes=[pgk])
                        e1, e1k = self.gd_m.next()
                        S.op("dve", lambda e: e.scalar_tensor_tensor(out=e1[:], in0=pg[:, 0:128], scalar=gcol, in1=maskA[d], op0=ALU.subtract, op1=ALU.add),
                             reads=[pgk, "gd_gam", "gd_c"], writes=[e1k])
                        e2, e2k = self.gd_e.next()
                        S.op("dve", lambda e: e.scalar_tensor_tensor(out=e2[:], in0=pg[:, 0:128], scalar=gcol, in1=maskT[d], op0=ALU.subtract, op1=ALU.add),
                             reads=[pgk, "gd_gam", "gd_c"], writes=[e2k])
                        eg, egk = self.gd_e.next()
                        S.op("dve", lambda e: e.tensor_copy(out=eg[:], in_=pg[:, 0:128]), reads=[pgk], writes=[egk])
                        S.op("act", lambda e: e.activation(out=e1[:], in_=e1[:], func=AF.Exp, scale=-1.0), reads=[e1k], writes=[e1k])
                        S.op("act", lambda e: e.activation(out=e2[:], in_=e2[:], func=AF.Exp), reads=[e2k], writes=[e2k])
                        S.op("act", lambda e: e.activation(out=eg[:], in_=eg[:], func=AF.Exp), reads=[egk], writes=[egk])
                        L, Lk = self.gd_m.next()
                        S.op("dve", lambda e: e.scalar_tensor_tensor(out=L[:], in0=e1[:], scalar=tab["beta"][:, n, col:col + 1], in1=kk[:, 0, :],
                                                                     op0=ALU.mult, op1=ALU.mult), reads=[e1k, "gd_beta", kkk], writes=[Lk])
                        pt, ptk = self.ps.next()
                        S.op("pe", lambda e: e.transpose(out=pt[:, 0:128], in_=L[:], identity=self.ident[:]), reads=[Lk, "ident"], writes=[ptk])
                        M, Mk = self.gd_m.next()
                        S.op("act", lambda e: e.activation(out=M[:], in_=pt[:, 0:128], func=AF.Identity), reads=[ptk], writes=[Mk])
                        P, Pk = self.gd_m.next()
                        S.op("dve", lambda e: e.tensor_tensor(out=P[:], in0=M[:], in1=self.ident[:], op=ALU.add), reads=[Mk, "ident"], writes=[Pk])
                        for lvl in range(1, 7):
                            p2, p2k = self.ps.next()
                            S.op("pe", lambda e: e.matmul(p2[:, 0:128], lhsT=M[:], rhs=L[:], start=True, stop=True), reads=[Mk, Lk], writes=[p2k])
                            if lvl < 6:
                                S.op("pe", lambda e: e.matmul(p2[:, 128:256], lhsT=L[:], rhs=M[:], start=True, stop=True), reads=[Mk, Lk], writes=[p2k])
                            L2, L2k = self.gd_m.next()
                            S.op("act", lambda e: e.activation(out=L2[:], in_=p2[:, 0:128], func=AF.Identity), reads=[p2k], writes=[L2k])
                            if lvl < 6:
                                M2, M2k = self.gd_m.next()
                                S.op("act", lambda e: e.activation(out=M2[:], in_=p2[:, 128:256], func=AF.Identity), reads=[p2k], writes=[M2k])
                            p3, p3k = self.ps.next()
                            S.op("pe", lambda e: e.matmul(p3[:, 0:128], lhsT=L2[:], rhs=P[:], start=True, stop=True), reads=[L2k, Pk], writes=[p3k])
                            P2, P2k = self.gd_m.next()
                            S.op("dve", lambda e: e.tensor_tensor(out=P2[:], in0=p3[:, 0:128], in1=P[:], op=ALU.add), reads=[p3k, Pk], writes=[P2k])
                            P, Pk = P2, P2k
                            L, Lk = L2, L2k
                            if lvl < 6:
                                M, Mk = M2, M2k
                        TT, TTk = self.gd_mb.next()
                        S.op("pool", lambda e: e.tensor_copy(out=TT[:], in_=P[:]), reads=[Pk], writes=[TTk])
                        first_dbg = dbg and h == 0 and d == 0 and n == 0
                        if first_dbg:
                            for nm, t_, k_ in (("P", P, Pk), ("e1", e1, e1k), ("e2", e2, e2k), ("eg", eg, egk), ("kk0", kk[:, 0, :], kkk), ("kk1", kk[:, 1, :], kkk)):
                                d_ = nc.dram_tensor("dbg_" + nm, [128, 128], F32, kind="ExternalOutput").ap()
                                S.dma("sp", d_, t_[:] if nm[0] != "k" else t_, reads=[k_], writes=["dbg_" + nm])
                        pks, pksk = self.ps.next()
                        S.op("pe", lambda e: e.matmul(pks[:, 0:128], lhsT=kc_, rhs=Sb[:], start=True, stop=True), reads=["gd_k", sbk], writes=[pksk])
                        vb, vbk = self.gd_m.next()
                        S.op("pool", lambda e: e.tensor_scalar(out=vb[:], in0=self.gd_vtok[:, n, :], scalar1=tab["beta"][:, n, col:col + 1], scalar2=None, op0=ALU.mult),
                             reads=["gd_vtok", "gd_beta"], writes=[vbk])
                        R, Rk = self.gd_mb.next()
                        S.op("dve", lambda e: e.scalar_tensor_tensor(out=R[:], in0=pks[:, 0:128], scalar=tab["nb"][:, n, col:col + 1], in1=vb[:],
                                                                     op0=ALU.mult, op1=ALU.add), reads=[pksk, "gd_nb", vbk], writes=[Rk])
                        pv, pvk = self.ps.next()
                        S.op("pe", lambda e: e.matmul(pv[:, 0:128], lhsT=TT[:], rhs=R[:], start=True, stop=True), reads=[TTk, Rk], writes=[pvk])
                        VN, VNk = self.gd_mb.next()
                        S.op("act", lambda e: e.activation(out=VN[:], in_=pv[:, 0:128], func=AF.Identity), reads=[pvk], writes=[VNk])
                        if first_dbg:
                            for nm, t_, k_ in (("VN", VN, VNk), ("R", R, Rk)):
                                d_ = nc.dram_tensor("dbg_" + nm, [128, 128], BF16, kind="ExternalOutput").ap()
                                S.dma("sp", d_, t_[:], reads=[k_], writes=["dbg_" + nm])
                        qg, qgk = self.gd_mb.next()
                        S.op("dve", lambda e: e.tensor_tensor(out=qg[:], in0=qc_, in1=eg[:], op=ALU.mult), reads=["gd_q", egk], writes=[qgk])
                        at, atk = self.gd_mb.next()
                        S.op("dve", lambda e: e.tensor_tensor(out=at[:], in0=kk[:, 1, :], in1=e2[:], op=ALU.mult), reads=[kkk, e2k], writes=[atk])
                        po, pok = self.ps.next()
                        S.op("pe", lambda e: e.matmul(po[:, 0:128], lhsT=Sb[:], rhs=qg[:], start=True, stop=False), reads=[sbk, qgk], writes=[pok])
                        S.op("pe", lambda e: e.matmul(po[:, 0:128], lhsT=VN[:], rhs=at[:], start=False, stop=True), reads=[VNk, atk], writes=[pok])
                        if d == 0:
                            S.op("act", lambda e: e.activation(out=self.gd_o[:, c0:c0 + 128], in_=po[:, 0:128], func=AF.Identity), reads=[pok], writes=["gd_o"])
                        else:
                            S.op("dve", lambda e: e.tensor_tensor(out=self.gd_o[:, c0:c0 + 128], in0=po[:, 0:128], in1=self.gd_o[:, c0:c0 + 128], op=ALU.add),
                                 reads=[pok, "gd_o"], writes=["gd_o"])
                        ke, kek = self.gd_mb.next()
                        S.op("pool", lambda e: e.tensor_scalar(out=ke[:], in0=self.gd_ktok[:, n, :], scalar1=tab["ksc"][:, n, col:col + 1], scalar2=None, op0=ALU.mult),
                             reads=["gd_ktok", "gd_ksc"], writes=[kek])
                        pS, pSk = self.ps.next()
                        S.op("pe", lambda e: e.matmul(pS[:, 0:128], lhsT=ke[:], rhs=VN[:], start=True, stop=True), reads=[kek, VNk], writes=[pSk])
                        S.op("dve", lambda e: e.scalar_tensor_tensor(out=Sf[:], in0=Sf[:], scalar=tab["egl"][:, n, col:col + 1], in1=pS[:, 0:128],
                                                                     op0=ALU.mult, op1=ALU.add), reads=[sk, "gd_egl", pSk], writes=[sk])
                        S.op("act", lambda e: e.activation(out=Sb[:], in_=Sf[:], func=AF.Identity), reads=[sk], writes=[sbk])
                if dbg and h in (0, 15):
                    d_ = nc.dram_tensor("dbg_o%d" % h, [128, Tp], F32, kind="ExternalOutput").ap()
                    S.dma("sp", d_, self.gd_o[:], reads=["gd_o"], writes=["dbg_o%d" % h])
                    if h == 0:
                        for nm, t_ in (("q", self.gd_q), ("k", self.gd_k)):
                            d_ = nc.dram_tensor("dbg_" + nm, [128, Tp], BF16, kind="ExternalOutput").ap()
                            S.dma("sp", d_, t_[:], reads=["gd_" + nm], writes=["dbg_" + nm])
                        d_ = nc.dram_tensor("dbg_v", [128, NT, 128], BF16, kind="ExternalOutput").ap()
                        S.dma("sp", d_, self.gd_vtok[:], reads=["gd_vtok"], writes=["dbg_v"])
                wz, wzk = self.load_wcols("gdn_w_in", j, 2 * E + h * 128, 128)
                self.hn_ps = self.ps
                for (t0, nt) in tok_blocks(T, 512):
                    ps, pk = self.proj_fm(wz, wzk, 128, t0, nt)
                    sz, szk = self.gd_sz.next()
                    S.op("act", lambda e: e.activation(out=sz[:, :nt], in_=ps[:, :nt], func=AF.Silu), reads=[pk], writes=[szk])
                    self.head_norm_gate(self.gd_o[:, t0:t0 + nt], "gd_o", nt, self.gd_gain[:, 0:1], sz[:, :nt], szk, s, h * 128, t0, 1e-6, 1.0)

    def setup_conformer(self):
        NE = E // 128
        self.cf_ypad = self.sb("cf_ypad", [128, self.T + 30], BF16)
        self.cf_a = self.ring("cf_a", 2, [128, 512], F32)
        self.cf_sg = self.ring("cf_sg", 2, [128, 512], F32)
        self.cf_sz = self.ring("cf_sz", 2, [128, 512], BF16)
        self.cf_cz = self.ring("cf_cz", 2, [128, 512], F32)
        self.cf_diag = self.sb("cf_diag", [128, 31, 128], BF16)
        self.cf_dw = self.sb("cf_dw", [128, 31], F32)
        self.cf_vec = self.sb("cf_vec", [128, 8, NE], F32)
        self.CZ = self.dt("cf_cz_scr", [self.nseq, E, self.T], F32)
        self.SZ = self.dt("cf_sz_scr", [self.nseq, E, self.T], BF16)
        self.cf_czall = self.ring("cf_czall", 1, [128, NE, 512], F32)
        self.cf_szall = self.ring("cf_szall", 1, [128, NE, 512], BF16)
        self.cf_tmpb = self.ring("cf_tmpb", 2, [128, 512], BF16)
        self.cf_stat = self.ring("cf_stat", 2, [128, 3, 512], F32)
        self.cf_n = self.ring("cf_n", 2, [128, 512], F32)
        self.cf_g = self.ring("cf_g", 2, [128, 512], BF16)

    def mixer_conformer(self, li, j, s):
        nc, S, T = self.nc, self.S, self.T
        NE = E // 128
        vec = self.cf_vec
        if s == 0:
            def ldv(slot, ap):
                S.dma("sp", vec[:, slot, :], ap.rearrange("(c p) -> p c", p=128), writes=["cf_vec"],
                      allow_slow_non_contiguous=True)
            ldv(0, self.dram["cf_b_in"][j][0:E])
            ldv(1, self.dram["cf_b_in"][j][E:2 * E])
            ldv(2, self.dram["cf_b_in"][j][2 * E:3 * E])
            ldv(3, self.dram["cf_dw_b"][j])
            ldv(4, self.dram["cf_ln_g"][j])
            ldv(5, self.dram["cf_ln_b"][j])
            S.op("dve", lambda e: e.memset(self.cf_ypad[:], 0.0), writes=["cf_ypad"])
        blocks = tok_blocks(T, 512)
        for c in range(NE):
            S.dma("sp", self.cf_dw[:], self.dram["cf_dw_w"][j][:, c * 128:(c + 1) * 128].rearrange("k p -> p k"),
                  writes=["cf_dw"], allow_slow_non_contiguous=True)
            for k in range(31):
                S.op("pool", lambda e: e.tensor_scalar(out=self.cf_diag[:, k, :], in0=self.ident[:], scalar1=self.cf_dw[:, k:k + 1],
                                                       scalar2=None, op0=ALU.mult),
                     reads=["ident", "cf_dw"], writes=["cf_diag"])
            wa, wak = self.load_wcols("cf_w_in", j, c * 128, 128)
            wg, wgk = self.load_wcols("cf_w_in", j, E + c * 128, 128)
            for (t0, nt) in blocks:
                ps, pk = self.proj_fm(wa, wak, 128, t0, nt)
                a, ak = self.cf_a.next()
                S.op("act", lambda e: e.activation(out=a[:, :nt], in_=ps[:, :nt], func=AF.Identity,
                                                   bias=vec[:, 0, c:c + 1]), reads=[pk, "cf_vec"], writes=[ak])
                ps, pk = self.proj_fm(wg, wgk, 128, t0, nt)
                sg, sgk = self.cf_sg.next()
                S.op("act", lambda e: e.activation(out=sg[:, :nt], in_=ps[:, :nt], func=AF.Sigmoid,
                                                   bias=vec[:, 1, c:c + 1]), reads=[pk, "cf_vec"], writes=[sgk])
                S.op("dve", lambda e: e.tensor_tensor(out=self.cf_ypad[:, 15 + t0:15 + t0 + nt], in0=a[:, :nt], in1=sg[:, :nt], op=ALU.mult),
                     reads=[ak, sgk], writes=["cf_ypad"])
            wz, wzk = self.load_wcols("cf_w_in", j, 2 * E + c * 128, 128)
            for (t0, nt) in blocks:
                ps, pk = self.proj_fm(wz, wzk, 128, t0, nt)
                sz, szk = self.cf_sz.next()
                S.op("act", lambda e: e.activation(out=sz[:, :nt], in_=ps[:, :nt], func=AF.Silu,
                                                   bias=vec[:, 2, c:c + 1]), reads=[pk, "cf_vec"], writes=[szk])
                S.dma("sp", self.SZ[s, c * 128:(c + 1) * 128, t0:t0 + nt], sz[:, :nt], reads=[szk], writes=[("SZ", s)])
            for (t0, nt) in blocks:
                ps, pk = self.ps.next()
                for k in range(31):
                    S.op("pe", lambda e: e.matmul(ps[:, :nt], lhsT=self.cf_diag[:, k, :], rhs=self.cf_ypad[:, t0 + k:t0 + k + nt],
                                                  start=(k == 0), stop=(k == 30)),
                         reads=["cf_diag", "cf_ypad"], writes=[pk])
                cz, czk = self.cf_cz.next()
                S.op("act", lambda e: e.activation(out=cz[:, :nt], in_=ps[:, :nt], func=AF.Identity,
                                                   bias=vec[:, 3, c:c + 1]), reads=[pk, "cf_vec"], writes=[czk])
                S.dma("sp", self.CZ[s, c * 128:(c + 1) * 128, t0:t0 + nt], cz[:, :nt], reads=[czk], writes=[("CZ", s)])
        for (t0, nt) in blocks:
            ca, cak = self.cf_czall.next()
            sa, sak = self.cf_szall.next()
            S.dma("sp", ca[:, :, :nt], self.CZ[s, :, t0:t0 + nt].rearrange("(c p) t -> p c t", p=128),
                  reads=[("CZ", s)], writes=[cak])
            S.dma("sp", sa[:, :, :nt], self.SZ[s, :, t0:t0 + nt].rearrange("(c p) t -> p c t", p=128),
                  reads=[("SZ", s)], writes=[sak])
            p1, p1k = self.ps.next()
            p2, p2k = self.ps.next()
            for c in range(NE):
                tb, tbk = self.cf_tmpb.next()
                S.op("dve", lambda e: e.tensor_copy(out=tb[:, :nt], in_=ca[:, c, :nt]), reads=[cak], writes=[tbk])
                S.op("pe", lambda e: e.matmul(p1[:, :nt], lhsT=self.ones_b[:], rhs=tb[:, :nt], start=(c == 0), stop=(c == NE - 1)),
                     reads=["ones_b", tbk], writes=[p1k])
                tb2, tb2k = self.cf_tmpb.next()
                S.op("act", lambda e: e.activation(out=tb2[:, :nt], in_=ca[:, c, :nt], func=AF.Square), reads=[cak], writes=[tb2k])
                S.op("pe", lambda e: e.matmul(p2[:, :nt], lhsT=self.ones_b[:], rhs=tb2[:, :nt], start=(c == 0), stop=(c == NE - 1)),
                     reads=["ones_b", tb2k], writes=[p2k])
            st, stk = self.cf_stat.next()
            S.op("dve", lambda e: e.tensor_scalar(out=st[:, 0, :nt], in0=p1[:, :nt], scalar1=1.0 / E, scalar2=None, op0=ALU.mult),
                 reads=[p1k], writes=[stk])
            S.op("dve", lambda e: e.tensor_tensor(out=st[:, 1, :nt], in0=st[:, 0, :nt], in1=st[:, 0, :nt], op=ALU.mult),
                 reads=[stk], writes=[stk])
            S.op("dve", lambda e: e.scalar_tensor_tensor(out=st[:, 1, :nt], in0=p2[:, :nt], scalar=1.0 / E, in1=st[:, 1, :nt],
                                                         op0=ALU.mult, op1=ALU.subtract), reads=[p2k, stk], writes=[stk])
            S.op("act", lambda e: e.activation(out=st[:, 2, :nt], in_=st[:, 1, :nt], func=AF.Ln, bias=1e-5), reads=[stk], writes=[stk])
            S.op("act", lambda e: e.activation(out=st[:, 2, :nt], in_=st[:, 2, :nt], func=AF.Exp, scale=-0.5), reads=[stk], writes=[stk])
            for c in range(NE):
                n, nk = self.cf_n.next()
                S.op("dve", lambda e: e.tensor_tensor(out=n[:, :nt], in0=ca[:, c, :nt], in1=st[:, 0, :nt], op=ALU.subtract),
                     reads=[cak, stk], writes=[nk])
                S.op("dve", lambda e: e.tensor_tensor(out=n[:, :nt], in0=n[:, :nt], in1=st[:, 2, :nt], op=ALU.mult),
                     reads=[nk, stk], writes=[nk])
                S.op("act", lambda e: e.activation(out=n[:, :nt], in_=n[:, :nt], func=AF.Silu, scale=vec[:, 4, c:c + 1],
                                                   bias=vec[:, 5, c:c + 1]), reads=[nk, "cf_vec"], writes=[nk])
                g, gk = self.cf_g.next()
                S.op("dve", lambda e: e.tensor_tensor(out=g[:, :nt], in0=n[:, :nt], in1=sa[:, c, :nt], op=ALU.mult),
                     reads=[nk, sak], writes=[gk])
                S.dma("sp", self.G[s, c * 128:(c + 1) * 128, t0:t0 + nt], g[:, :nt], reads=[gk], writes=[("G", s)])

    def build(self):
        self.setup_common()
        self.init_h()
        nl = len(self.layers)
        WOUT = {0: ("hy_w_out", None), 1: ("da_w_out", None), 2: ("gdn_w_out", None), 3: ("cf_w_out", "cf_b_out")}
        for li, m in enumerate(self.layers):
            j = 0
            if m == 0:
                self.hy_dims()
                self.open_scope()
                self.hyena_filters(j)
                self.close_scope()
            self.open_scope()
            self.setup_am()
            if m == 0:
                self.setup_hyena_proj()
            if m == 1:
                self.setup_attention()
            elif m == 2:
                self.setup_gdn()
            elif m == 3:
                self.setup_conformer()
            for s in range(self.nseq):
                self.phase_a(li, s)
                if m == 0:
                    self.hyena_proj(j, s)
                elif m == 1:
                    self.mixer_attention(li, j, s, self.layer_index(li))
                elif m == 2:
                    self.mixer_gdn(li, j, s)
                elif m == 3:
                    self.mixer_conformer(li, j, s)
            self.close_scope()
            if m == 0:
                self.open_scope()
                self.hyena_spectral(j)
                self.close_scope()
            self.open_scope()
            self.setup_z()
            for s in range(self.nseq):
                self.phase_z(li, s, WOUT[m][0], j, bias_name=WOUT[m][1], final=(li == nl - 1))
            self.close_scope()
        self.S.finish("sp")
        self.stack.close()
        return self.nc

    def layer_index(self, li):
        return 1 if self.layers != [0, 1, 2, 3] else li


def rope_consts(T):
    inv = (10000.0 ** (-np.arange(0, 64, 2, dtype=np.float32) / np.float32(64))).astype(np.float32)
    ang = (np.arange(T, dtype=np.float32)[:, None] * inv[None, :]).astype(np.float32)
    cos = np.cos(ang).astype(np.float32)
    sin = np.sin(ang).astype(np.float32)
    idx = np.arange(128) % 32
    C = np.ascontiguousarray(cos[:, idx].T)
    Sg = np.ascontiguousarray(sin[:, idx].T)
    P = np.zeros((128, 128), np.float32)
    for p in range(128):
        r = p % 64
        if r < 32:
            P[p, p + 32] = -1.0
        else:
            P[p, p - 32] = 1.0
    return C, Sg, np.ascontiguousarray(P.T)


def gdn_consts():
    i = np.arange(128)[:, None]
    j = np.arange(128)[None, :]
    c = np.zeros((10, 128, 128), np.float32)
    c[0] = (i <= j)
    c[1] = (i >= j)
    c[2][127, :] = 1.0
    c[3][0, :] = 1.0
    big = 1.0e4
    c[4] = np.where(i > j, 0.0, big)
    c[5] = np.where(i < j, 0.0, big)
    c[6] = np.where(j >= i, 0.0, -big)
    c[7] = np.where(j <= i, 0.0, -big)
    c[8] = np.where(i > j, -1.0, 0.0)
    c[9] = np.where(i < j, -1.0, 0.0)
    return np.ascontiguousarray(c.transpose(1, 0, 2))


def hyena_consts(T):
    import ml_dtypes
    N = 2 * T
    NF = T + 1
    NFC = (NF + 127) // 128
    t = np.arange(T, dtype=np.int64)
    k = np.arange(NFC * 128, dtype=np.int64)
    valid = (k < NF)
    ang = 2.0 * np.pi * ((t[:, None] * k[None, :]) % N).astype(np.float64) / N
    Fc = np.cos(ang) * valid[None, :]
    Fs = -np.sin(ang) * valid[None, :]
    F = np.concatenate([Fc, Fs], axis=1)
    ck = np.where((k == 0) | (k == N // 2), 1.0, 2.0) * valid / N
    Gc = (np.cos(ang) * ck[None, :]).T
    Gs = (-np.sin(ang) * ck[None, :]).T
    G = np.concatenate([Gc, Gs], axis=0)
    tl = np.linspace(0.0, 1.0, T, dtype=np.float32)[:, None]
    w = (2.0 * np.float32(math.pi) * np.arange(T, dtype=np.float32)[:, None] / np.float32(T)).astype(np.float32)
    bands = np.linspace(1e-4, 15, 16, dtype=np.float32)[None, :]
    z = np.concatenate([tl, np.cos(bands * w), -np.sin(bands * w)], axis=-1).astype(np.float32)
    max_decay = math.log(1e-2) / 0.3
    min_decay = math.log(1e-2) / 1.5
    deltas = np.abs(np.linspace(min_decay, max_decay, E, dtype=np.float32)).astype(np.float32)
    NT = (T + 127) // 128
    tlp = np.zeros((NT * 128,), np.float32)
    tlp[:T] = -tl[:, 0]
    return {"c_dft_f": np.ascontiguousarray(F).astype(ml_dtypes.bfloat16),
            "c_dft_g": np.ascontiguousarray(G).astype(ml_dtypes.bfloat16),
            "c_hy_z": np.ascontiguousarray(z.T), "c_hy_delta": deltas,
            "c_hy_tl": np.ascontiguousarray(tlp.reshape(NT, 128).T)}


def const_inputs(T):
    C, Sg, PT = rope_consts(T)
    return {"c_ident": np.eye(128, dtype=np.float32), "c_rope_cos": C, "c_rope_sin": Sg, "c_rope_perm": PT,
            "c_gdn": gdn_consts(), **hyena_consts(T)}


def build_program(T, nseq, layers, inputs):
    shapes = {k: v.shape for k, v in inputs.items() if k != "x"}
    b = Builder(T, nseq, layers, shapes)
    nc = b.build()
    return nc, b


def kernel(**inputs):
    ncores = 8
    x = np.ascontiguousarray(inputs["x"], dtype=np.float32)
    B, L, _ = x.shape
    nseq = B // ncores
    T = L + NMETA
    consts = const_inputs(T)
    params = {k: np.ascontiguousarray(v, dtype=np.float32) for k, v in inputs.items() if k != "x"}
    params.update(consts)
    nc, b = build_program(T, nseq, [0, 1, 2, 3], dict(params, x=x))
    in_maps = []
    for c in range(ncores):
        m = dict(params)
        m["x"] = x[c * nseq:(c + 1) * nseq]
        in_maps.append(m)
    res = run_bass_kernel_spmd(nc, in_maps, core_ids=list(range(ncores)))
    return np.concatenate([r["out"] for r in res.results], axis=0)
```

```python
import math
from contextlib import ExitStack
import numpy as np
import concourse.bass as bass
import concourse.mybir as mybir
from concourse.bass_utils import run_bass_kernel_spmd

F32 = mybir.dt.float32
BF16 = mybir.dt.bfloat16
AF = mybir.ActivationFunctionType
ALU = mybir.AluOpType
AX = mybir.AxisListType

D = 1024
E = 2048
NMETA = 16
KC = D // 128


class _Eng:
    def __init__(self, name, eng, sem):
        self.name = name
        self.eng = eng
        self.sem = sem
        self.count = 0
        self.waited = {}


class Sched:
    def __init__(self, nc, stack, n_dma_sems=24):
        self.nc = nc
        self.E = {}
        for name, eng in (("pe", nc.tensor), ("act", nc.scalar), ("dve", nc.vector),
                          ("pool", nc.gpsimd), ("sp", nc.sync)):
            sem = stack.enter_context(nc.semaphore("s_" + name))
            self.E[name] = _Eng(name, eng, sem)
        self.dma_sems = [stack.enter_context(nc.semaphore("s_dma%d" % i)) for i in range(n_dma_sems)]
        self.dma_issued = [0] * n_dma_sems
        self.dma_rr = 0
        self.dma_rr2 = 0
        self.last_write = {}
        self.readers = {}
        self.ninstr = 0
        self.strict = True

    def _wait(self, e, ev):
        if ev is None:
            return
        if ev[0] == "e":
            if ev[1] == e.name and not (self.strict and e.name in ("act", "dve", "pool")):
                return
            src = self.E[ev[1]]
            key = ("e", ev[1])
            val = ev[2]
            sem = src.sem
        else:
            idx = ev[1]
            key = ("d", idx)
            val = 16 * self.dma_issued[idx]
            sem = self.dma_sems[idx]
        if e.waited.get(key, 0) >= val:
            return
        e.waited[key] = val
        e.eng.wait_ge(sem, val)
        self.ninstr += 1

    def _deps(self, e, reads, writes):
        for k in reads:
            self._wait(e, self.last_write.get(k))
            if isinstance(k, str) and k[0] == "P":
                for ev in self.readers.get(k, {}).values():
                    if ev[0] == "e" and ev[1] != e.name:
                        self._wait(e, ev)
        for k in writes:
            self._wait(e, self.last_write.get(k))
            for ev in self.readers.get(k, {}).values():
                self._wait(e, ev)

    def _record(self, ev, reads, writes):
        for k in reads:
            d = self.readers.setdefault(k, {})
            d[ev[:2]] = ev
        for k in writes:
            self.last_write[k] = ev
            self.readers[k] = {}

    def op(self, engname, emit, reads=(), writes=()):
        e = self.E[engname]
        self._deps(e, reads, writes)
        ins = emit(e.eng)
        e.count += 1
        ins.then_inc(e.sem, 1)
        self.ninstr += 1
        self._record(("e", engname, e.count), reads, writes)
        return ins

    def dma(self, qname, out, in_, reads=(), writes=(), semgroup=None, **kw):
        e = self.E[qname]
        self._deps(e, reads, writes)
        if qname == "sp":
            idx = self.dma_rr % 16
            self.dma_rr += 1
        else:
            idx = 16 + self.dma_rr2 % 8
            self.dma_rr2 += 1
        ins = e.eng.dma_start(out=out, in_=in_, **kw)
        ins.then_inc(self.dma_sems[idx], 16)
        self.dma_issued[idx] += 1
        self.ninstr += 1
        self._record(("d", idx), reads, writes)
        return ins

    def barrier(self):
        for name in self.E:
            self.finish(name)

    def finish(self, engname="sp"):
        e = self.E[engname]
        for idx in range(len(self.dma_sems)):
            if self.dma_issued[idx]:
                self._wait(e, ("d", idx))
        for name, src in self.E.items():
            if name != engname and src.count:
                self._wait(e, ("e", name, src.count))


class Ring:
    def __init__(self, name, bufs, keys=None):
        self.name = name
        self.bufs = bufs
        self.keys = keys if keys is not None else ["%s#%d" % (name, j) for j in range(len(bufs))]
        self.i = 0

    def next(self):
        j = self.i % len(self.bufs)
        self.i += 1
        return self.bufs[j], self.keys[j]


def tok_blocks(T, bs):
    out = []
    t = 0
    while t < T:
        n = min(bs, T - t)
        out.append((t, n))
        t += n
    return out


class Builder:
    def __init__(self, T, nseq, layers, params_shapes):
        self.T = T
        self.nseq = nseq
        self.layers = layers
        self.nc = bass.Bass("TRN2", target_bir_lowering=False)
        self.stack = ExitStack()
        nc = self.nc
        self.S = Sched(nc, self.stack)
        self.dram = {}
        for name, shp in params_shapes.items():
            dt_ = BF16 if name.startswith("c_dft") else F32
            self.dram[name] = nc.dram_tensor(name, list(shp), dt_, kind="ExternalInput").ap()
        self.x = nc.dram_tensor("x", [nseq, T - NMETA, D], F32, kind="ExternalInput").ap()
        self.out = nc.dram_tensor("out", [nseq, T - NMETA, D], F32, kind="ExternalOutput").ap()
        self.h = nc.dram_tensor("h_scr", [nseq, T, D], F32).ap()
        self.G = nc.dram_tensor("g_scr", [nseq, E, T], BF16).ap()
        self.uid = 0
        self.cur = self.stack

    def sb(self, name, shape, dtype=F32):
        self.uid += 1
        return self.cur.enter_context(self.nc.sbuf_tensor("%s_%d" % (name, self.uid), list(shape), dtype))

    def open_scope(self):
        self.cur = ExitStack()

    def close_scope(self):
        self.S.barrier()
        self.cur.close()
        self.cur = self.stack

    def ring(self, name, n, shape, dtype=F32):
        return Ring(name, [self.sb("%s_%d" % (name, i), shape, dtype) for i in range(n)])

    def dt(self, name, shape, dtype=F32):
        return self.nc.dram_tensor(name, list(shape), dtype).ap()

    def setup_common(self):
        nc, S = self.nc, self.S
        pb = [self.stack.enter_context(nc.psum_tensor("ps%d" % i, [128, 512], F32)) for i in range(6)]
        self.PB = pb
        self.PBK = ["P%d" % i for i in range(6)]
        self.PW = self.stack.enter_context(nc.psum_tensor("pw", [128, 1024], F32))
        self.ps = Ring("ps", pb, self.PBK)
        self.ident = self.sb("ident", [128, 128], F32)
        self.identb = self.sb("identb", [128, 128], BF16)
        self.ones_b = self.sb("ones_b", [128, 128], BF16)
        self.ones_f = self.sb("ones_f", [128, 128], F32)
        S.dma("sp", self.ident[:], self.dram["c_ident"], writes=["ident"])
        S.op("dve", lambda e: e.tensor_copy(out=self.identb[:], in_=self.ident[:]), reads=["ident"], writes=["identb"])
        S.op("dve", lambda e: e.memset(self.ones_b[:], 1.0), writes=["ones_b"])
        S.op("dve", lambda e: e.memset(self.ones_f[:], 1.0), writes=["ones_f"])
        self.ht = self.ring("ht", 1, [128, D], F32)
        self.junk = self.ring("junk", 1, [128, D], F32)
        self.col = self.ring("col", 4, [128, 1], F32)

    def setup_am(self):
        self.yT = self.sb("yT", [128, KC, self.T], BF16)
        self.wring = self.ring("wf", 2, [128, KC, 128], F32)
        self.wbring = self.ring("wb", 2, [128, KC, 128], BF16)
        self.gpre = self.sb("gpre", [128, KC], F32)

    def setup_z(self):
        self.gpost = self.sb("gpost", [128, D], F32)
        self.bout = self.sb("bout", [128, D], F32)
        self.wout = self.sb("wout", [128, E // 128, D], BF16)
        self.woutf = self.ring("woutf", 2, [128, D], F32)
        self.gt = self.ring("gt", 2, [128, E // 128, 128], BF16)
        self.ot = self.ring("ot", 2, [128, D], F32)

    def init_h(self):
        S = self.S
        for s in range(self.nseq):
            S.dma("sp", self.h[s, 0:NMETA, :], self.dram["meta"], writes=[("h", s)])
            S.dma("sp", self.h[s, NMETA:, :], self.x[s], writes=[("h", s)])

    def phase_a(self, li, s):
        nc, S, T = self.nc, self.S, self.T
        S.dma("sp", self.gpre[:], self.dram["norm_pre"][li].rearrange("(kc p) -> p kc", p=128),
              reads=[], writes=["gpre"], allow_slow_non_contiguous=True)
        pw = self.PW
        pwk = ["PWa", "PWb"]
        for (t0, nt) in tok_blocks(T, 128):
            ht, hk = self.ht.next()
            S.dma("sp", ht[:nt, :], self.h[s, t0:t0 + nt, :], reads=[("h", s)], writes=[hk])
            jk, jkk = self.junk.next()
            cs, ck = self.col.next()
            S.op("dve", lambda e: e.memset(cs[:nt, :], 0.0), writes=[ck])
            S.op("act", lambda e: e.activation(out=jk[:nt, :], in_=ht[:nt, :], func=AF.Square, accum_out=cs[:nt, :]),
                 reads=[hk, ck], writes=[jkk, ck])
            S.op("act", lambda e: e.activation(out=cs[:nt, :], in_=cs[:nt, :], func=AF.Ln, scale=1.0 / D, bias=1e-6),
                 reads=[ck], writes=[ck])
            S.op("act", lambda e: e.activation(out=cs[:nt, :], in_=cs[:nt, :], func=AF.Exp, scale=-0.5),
                 reads=[ck], writes=[ck])
            S.op("act", lambda e: e.activation(out=jk[:nt, :], in_=ht[:nt, :], func=AF.Identity, scale=cs[:nt, :]),
                 reads=[hk, ck], writes=[jkk])
            for kc in range(KC):
                S.op("pe", lambda e: e.transpose(out=pw[:, kc * 128:kc * 128 + nt], in_=jk[:nt, kc * 128:(kc + 1) * 128],
                                                 identity=self.ident[:nt, :nt]),
                     reads=[jkk, "ident"], writes=pwk)
            pv = pw[:].rearrange("p (kc t) -> p kc t", kc=KC)[:, :, :nt]
            S.op("dve", lambda e: e.tensor_tensor(out=self.yT[:, :, t0:t0 + nt], in0=pv,
                                                  in1=self.gpre[:].unsqueeze(2).to_broadcast([128, KC, nt]), op=ALU.mult),
                 reads=pwk + ["gpre"], writes=["yT"])

    def load_wcols(self, wname, li_idx, col0, width):
        S = self.S
        wf, wfk = self.wring.next()
        wb, wbk = self.wbring.next()
        src = self.dram[wname][li_idx][:, col0:col0 + width].rearrange("(kc p) w -> p kc w", p=128)
        S.dma("pool", wf[:, :, :width], src, writes=[wfk])
        S.op("pool", lambda e: e.tensor_copy(out=wb[:, :, :width], in_=wf[:, :, :width]), reads=[wfk], writes=[wbk])
        return wb, wbk

    def proj_fm(self, wb, wbk, width, t0, nt, ring=None):
        S = self.S
        ps, pk = (ring or self.ps).next()
        for kc in range(KC):
            S.op("pe", lambda e: e.matmul(ps[:width, :nt], lhsT=wb[:, kc, :width], rhs=self.yT[:, kc, t0:t0 + nt],
                                          start=(kc == 0), stop=(kc == KC - 1)),
                 reads=[wbk, "yT"], writes=[pk])
        return ps, pk

    def phase_z(self, li, s, wout_name, j, bias_name=None, final=False):
        nc, S, T = self.nc, self.S, self.T
        pw = self.PW
        pwk = ["PWa", "PWb"]
        if s == 0:
            S.dma("sp", self.gpost[:], self.dram["norm_post"][li].partition_broadcast(128), writes=["gpost"])
            if bias_name is not None:
                S.dma("sp", self.bout[:], self.dram[bias_name][j].partition_broadcast(128), writes=["bout"])
            for ec in range(E // 128):
                wf, wfk = self.woutf.next()
                S.dma("pool", wf[:], self.dram[wout_name][j][ec * 128:(ec + 1) * 128, :], writes=[wfk])
                S.op("pool", lambda e: e.tensor_copy(out=self.wout[:, ec, :], in_=wf[:]), reads=[wfk], writes=["wout"])
        for (t0, nt) in tok_blocks(T, 128):
            gt, gk = self.gt.next()
            S.dma("sp", gt[:, :, :nt], self.G[s, :, t0:t0 + nt].rearrange("(ec p) t -> p ec t", p=128),
                  reads=[("G", s)], writes=[gk])
            ht, hk = self.ht.next()
            S.dma("sp", ht[:nt, :], self.h[s, t0:t0 + nt, :], reads=[("h", s)], writes=[hk])
            for half in range(2):
                for ec in range(E // 128):
                    S.op("pe", lambda e: e.matmul(pw[:nt, half * 512:(half + 1) * 512], lhsT=gt[:, ec, :nt],
                                                  rhs=self.wout[:, ec, half * 512:(half + 1) * 512],
                                                  start=(ec == 0), stop=(ec == E // 128 - 1)),
                         reads=[gk, "wout"], writes=pwk)
            ot, ok = self.ot.next()
            if bias_name is not None:
                S.op("dve", lambda e: e.tensor_tensor(out=ot[:nt, :], in0=pw[:nt, :], in1=self.bout[:nt, :], op=ALU.add),
                     reads=pwk + ["bout"], writes=[ok])
            else:
                S.op("dve", lambda e: e.tensor_copy(out=ot[:nt, :], in_=pw[:nt, :]), reads=pwk, writes=[ok])
            jk, jkk = self.junk.next()
            cs, ck = self.col.next()
            S.op("dve", lambda e: e.memset(cs[:nt, :], 0.0), writes=[ck])
            S.op("act", lambda e: e.activation(out=jk[:nt, :], in_=ot[:nt, :], func=AF.Square, accum_out=cs[:nt, :]),
                 reads=[ok, ck], writes=[jkk, ck])
            S.op("act", lambda e: e.activation(out=cs[:nt, :], in_=cs[:nt, :], func=AF.Ln, scale=1.0 / D, bias=1e-6),
                 reads=[ck], writes=[ck])
            S.op("act", lambda e: e.activation(out=cs[:nt, :], in_=cs[:nt, :], func=AF.Exp, scale=-0.5),
                 reads=[ck], writes=[ck])
            S.op("dve", lambda e: e.scalar_tensor_tensor(out=ot[:nt, :], in0=ot[:nt, :], scalar=cs[:nt, :],
                                                         in1=self.gpost[:nt, :], op0=ALU.mult, op1=ALU.mult),
                 reads=[ok, ck, "gpost"], writes=[ok])
            S.op("dve", lambda e: e.tensor_tensor(out=ot[:nt, :], in0=ot[:nt, :], in1=ht[:nt, :], op=ALU.add),
                 reads=[ok, hk], writes=[ok])
            if not final:
                S.dma("sp", self.h[s, t0:t0 + nt, :], ot[:nt, :], reads=[ok], writes=[("h", s)])
            else:
                if t0 == 0:
                    S.dma("sp", self.out[s, 0:nt - NMETA, :], ot[NMETA:nt, :], reads=[ok], writes=[("out", s)])
                else:
                    S.dma("sp", self.out[s, t0 - NMETA:t0 - NMETA + nt, :], ot[:nt, :], reads=[ok], writes=[("out", s)])

    def head_norm_gate(self, o, okey, nt, gain_col, gate, gatekey, s, row0, t0, eps, extra_scale):
        S = self.S
        sq, sqk = self.hn_sq.next()
        S.op("act", lambda e: e.activation(out=sq[:, :nt], in_=o[:, :nt], func=AF.Square), reads=[okey], writes=[sqk])
        ps, pk = self.hn_ps.next()
        S.op("pe", lambda e: e.matmul(ps[:, :nt], lhsT=self.ones_b[:], rhs=sq[:, :nt], start=True, stop=True),
             reads=["ones_b", sqk], writes=[pk])
        r, rk = self.hn_r.next()
        S.op("act", lambda e: e.activation(out=r[:, :nt], in_=ps[:, :nt], func=AF.Ln, scale=1.0 / 128, bias=eps), reads=[pk], writes=[rk])
        S.op("act", lambda e: e.activation(out=r[:, :nt], in_=r[:, :nt], func=AF.Exp, scale=-0.5), reads=[rk], writes=[rk])
        S.op("dve", lambda e: e.scalar_tensor_tensor(out=r[:, :nt], in0=o[:, :nt], scalar=gain_col, in1=r[:, :nt],
                                                     op0=ALU.mult, op1=ALU.mult), reads=[okey, rk, "hn_gain"], writes=[rk])
        g, gk = self.hn_g.next()
        S.op("dve", lambda e: e.scalar_tensor_tensor(out=g[:, :nt], in0=r[:, :nt], scalar=float(extra_scale), in1=gate,
                                                     op0=ALU.mult, op1=ALU.mult), reads=[rk, gatekey], writes=[gk])
        S.dma("sp", self.G[s, row0:row0 + 128, t0:t0 + nt], g[:, :nt], reads=[gk], writes=[("G", s)])

    def setup_head_norm(self):
        self.hn_sq = self.ring("hn_sq", 2, [128, 512], BF16)
        self.hn_r = self.ring("hn_r", 2, [128, 512], F32)
        self.hn_g = self.ring("hn_g", 2, [128, 512], BF16)

    def setup_attention(self):
        T = self.T
        self.setup_head_norm()
        self.da_cos = self.sb("da_cos", [128, T], F32)
        self.da_sin = self.sb("da_sin", [128, T], F32)
        self.da_perm = self.sb("da_perm", [128, 128], BF16)
        self.da_permf = self.sb("da_permf", [128, 128], F32)
        self.da_q = self.sb("da_q", [128, T], BF16)
        self.da_k = self.sb("da_k", [128, T], BF16)
        self.da_v = self.sb("da_v", [128, (T + 127) // 128, 128], BF16)
        self.da_sz = self.sb("da_sz", [128, T], BF16)
        self.da_xb = self.ring("da_xb", 2, [128, 512], BF16)
        self.da_t = self.ring("da_t", 6, [128, 512], F32)
        self.da_p = self.ring("da_p", 4, [128, 512], BF16)
        self.da_lv = self.sb("da_lv", [128, 4, 64], F32)
        self.da_lam = self.sb("da_lam", [128, 4], F32)
        self.da_gain = self.sb("da_gain", [128, 1], F32)
        self.da_o = self.ring("da_o", 2, [128, 512], F32)

    def mixer_attention(self, li, j, s, layer_idx):
        nc, S, T = self.nc, self.S, self.T
        lam_init = 0.8 - 0.6 * math.exp(-0.3 * layer_idx)
        if s == 0:
            S.dma("sp", self.da_cos[:], self.dram["c_rope_cos"], writes=["da_cos"])
            S.dma("sp", self.da_sin[:], self.dram["c_rope_sin"], writes=["da_sin"])
            S.dma("sp", self.da_permf[:], self.dram["c_rope_perm"], writes=["da_permf"])
            S.op("dve", lambda e: e.tensor_copy(out=self.da_perm[:], in_=self.da_permf[:]), reads=["da_permf"], writes=["da_perm"])
            S.dma("sp", self.da_gain[:], self.dram["da_subln"][j].rearrange("(p o) -> p o", o=1), writes=["hn_gain"])
            S.dma("sp", self.da_lv[:].rearrange("p a b -> p (a b)"),
                  self.dram["da_lambda"][j].rearrange("a b -> (a b)").partition_broadcast(128), writes=["da_lv"])
            lam = self.da_lam
            S.op("dve", lambda e: e.tensor_tensor(out=self.da_lv[:, 0, :], in0=self.da_lv[:, 0, :], in1=self.da_lv[:, 1, :], op=ALU.mult),
                 reads=["da_lv"], writes=["da_lv"])
            S.op("dve", lambda e: e.tensor_tensor(out=self.da_lv[:, 2, :], in0=self.da_lv[:, 2, :], in1=self.da_lv[:, 3, :], op=ALU.mult),
                 reads=["da_lv"], writes=["da_lv"])
            S.op("dve", lambda e: e.tensor_reduce(out=lam[:, 0:1], in_=self.da_lv[:, 0, :], axis=AX.X, op=ALU.add), reads=["da_lv"], writes=["da_lam"])
            S.op("dve", lambda e: e.tensor_reduce(out=lam[:, 1:2], in_=self.da_lv[:, 2, :], axis=AX.X, op=ALU.add), reads=["da_lv"], writes=["da_lam"])
            S.op("act", lambda e: e.activation(out=lam[:, 0:2], in_=lam[:, 0:2], func=AF.Exp), reads=["da_lam"], writes=["da_lam"])
            S.op("dve", lambda e: e.tensor_tensor(out=lam[:, 2:3], in0=lam[:, 1:2], in1=lam[:, 0:1], op=ALU.subtract), reads=["da_lam"], writes=["da_lam"])
            S.op("dve", lambda e: e.tensor_scalar(out=lam[:, 2:3], in0=lam[:, 2:3], scalar1=-lam_init, scalar2=None, op0=ALU.add),
                 reads=["da_lam"], writes=["da_lam"])
        blocks = tok_blocks(T, 512)
        tiles = tok_blocks(T, 128)
        acc = self.PB[0:4]
        acck = self.PBK[0:4]
        sring = Ring("sc", [self.PB[4], self.PB[5], self.PW[:, 0:512], self.PW[:, 512:1024]], ["P4", "P5", "PWa", "PWb"])
        self.hn_ps = Ring("hnps", [self.PB[4], self.PB[5]], ["P4", "P5"])
        import os
        cut = int(os.environ.get("ATTCUT", "9"))
        if os.environ.get("ATTRING"):
            sring = Ring("sc", [self.PB[4], self.PB[5]], ["P4", "P5"])
        for hd in range(16):
            if cut <= 0:
                continue
            for which, dst, dkey, col0 in (("q", self.da_q, "da_q", hd * 128), ("k", self.da_k, "da_k", E + hd * 128)):
                wb, wbk = self.load_wcols("da_w_in", j, col0, 128)
                for (t0, nt) in blocks:
                    ps, pk = self.proj_fm(wb, wbk, 128, t0, nt, ring=sring)
                    xf, xfk = self.da_t.next()
                    S.op("act", lambda e: e.activation(out=xf[:, :nt], in_=ps[:, :nt], func=AF.Identity), reads=[pk], writes=[xfk])
                    xb, xbk = self.da_xb.next()
                    S.op("pool", lambda e: e.tensor_copy(out=xb[:, :nt], in_=xf[:, :nt]), reads=[xfk], writes=[xbk])
                    pr, prk = sring.next()
                    S.op("pe", lambda e: e.matmul(pr[:, :nt], lhsT=self.da_perm[:], rhs=xb[:, :nt], start=True, stop=True),
                         reads=["da_perm", xbk], writes=[prk])
                    t1, t1k = self.da_t.next()
                    t2, t2k = self.da_t.next()
                    S.op("dve", lambda e: e.tensor_tensor(out=t1[:, :nt], in0=xf[:, :nt], in1=self.da_cos[:, t0:t0 + nt], op=ALU.mult),
                         reads=[xfk, "da_cos"], writes=[t1k])
                    S.op("dve", lambda e: e.tensor_tensor(out=t2[:, :nt], in0=pr[:, :nt], in1=self.da_sin[:, t0:t0 + nt], op=ALU.mult),
                         reads=[prk, "da_sin"], writes=[t2k])
                    S.op("dve", lambda e: e.tensor_tensor(out=dst[:, t0:t0 + nt], in0=t1[:, :nt], in1=t2[:, :nt], op=ALU.add),
                         reads=[t1k, t2k], writes=[dkey])
            if cut <= 1:
                continue
            wb, wbk = self.load_wcols("da_w_in", j, 2 * E + hd * 128, 128)
            for ti, (t0, nt) in enumerate(tiles):
                ps, pk = sring.next()
                for kc in range(KC):
                    S.op("pe", lambda e: e.matmul(ps[:nt, :128], lhsT=self.yT[:, kc, t0:t0 + nt], rhs=wb[:, kc, :],
                                                  start=(kc == 0), stop=(kc == KC - 1)), reads=[wbk, "yT"], writes=[pk])
                S.op("act", lambda e: e.activation(out=self.da_v[:nt, ti, :], in_=ps[:nt, :128], func=AF.Identity),
                     reads=[pk], writes=["da_v"])
            wb, wbk = self.load_wcols("da_w_in", j, 3 * E + hd * 128, 128)
            for (t0, nt) in blocks:
                ps, pk = self.proj_fm(wb, wbk, 128, t0, nt, ring=sring)
                S.op("act", lambda e: e.activation(out=self.da_sz[:, t0:t0 + nt], in_=ps[:, :nt], func=AF.Silu),
                     reads=[pk], writes=["da_sz"])
            if cut <= 2:
                continue
            for (q0, nq) in blocks:
                steps = [(ti, k0, nk, m) for ti, (k0, nk) in enumerate(tiles) for m in range(2)]

                def score(st):
                    ti, k0, nk, m = st
                    ps, pk = sring.next()
                    S.op("pe", lambda e: e.matmul(ps[:nk, :nq], lhsT=self.da_k[m * 64:(m + 1) * 64, k0:k0 + nk],
                                                  rhs=self.da_q[m * 64:(m + 1) * 64, q0:q0 + nq], start=True, stop=True),
                         reads=["da_k", "da_q"], writes=[pk])
                    return ps, pk
                pend = score(steps[0])
                for i, st in enumerate(steps):
                    ti, k0, nk, m = st
                    ps, pk = pend
                    if i + 1 < len(steps):
                        pend = score(steps[i + 1])
                    p, pkk = self.da_p.next()
                    S.op("act", lambda e: e.activation(out=p[:nk, :nq], in_=ps[:nk, :nq], func=AF.Exp, scale=0.125),
                         reads=[pk], writes=[pkk])
                    first = (ti == 0)
                    last = (ti == len(tiles) - 1)
                    S.op("pe", lambda e: e.matmul(acc[m][:, :nq], lhsT=self.da_v[:nk, ti, :], rhs=p[:nk, :nq], start=first, stop=last),
                         reads=["da_v", pkk], writes=[acck[m]])
                    S.op("pe", lambda e: e.matmul(acc[2 + m][:, :nq], lhsT=self.ones_b[:nk, :], rhs=p[:nk, :nq], start=first, stop=last),
                         reads=["ones_b", pkk], writes=[acck[2 + m]])
                if cut <= 3:
                    continue
                r0, r0k = self.da_t.next()
                r1, r1k = self.da_t.next()
                S.op("dve", lambda e: e.reciprocal(out=r0[:, :nq], in_=acc[2][:, :nq]), reads=[acck[2]], writes=[r0k])
                S.op("dve", lambda e: e.reciprocal(out=r1[:, :nq], in_=acc[3][:, :nq]), reads=[acck[3]], writes=[r1k])
                S.op("dve", lambda e: e.tensor_tensor(out=r0[:, :nq], in0=acc[0][:, :nq], in1=r0[:, :nq], op=ALU.mult),
                     reads=[acck[0], r0k], writes=[r0k])
                S.op("dve", lambda e: e.tensor_tensor(out=r1[:, :nq], in0=acc[1][:, :nq], in1=r1[:, :nq], op=ALU.mult),
                     reads=[acck[1], r1k], writes=[r1k])
                o, ok = self.da_o.next()
                S.op("dve", lambda e: e.scalar_tensor_tensor(out=o[:, :nq], in0=r1[:, :nq], scalar=self.da_lam[:, 2:3], in1=r0[:, :nq],
                                                             op0=ALU.mult, op1=ALU.add), reads=[r0k, r1k, "da_lam"], writes=[ok])
                self.head_norm_gate(o, ok, nq, self.da_gain[:, 0:1], self.da_sz[:, q0:q0 + nq], "da_sz", s, hd * 128, q0,
                                    1e-5, 1.0 - lam_init)

    def hy_dims(self):
        T = self.T
        self.NT = (T + 127) // 128
        self.NFC = (T + 1 + 127) // 128
        self.NK = 2 * self.NFC

    def sin_reduce(self, x, xk, kf, kfk, ki, kik, rows, n):
        S = self.S
        xx, k_, i_ = x[:rows, :n], kf[:rows, :n], ki[:rows, :n]
        S.op("dve", lambda e: e.tensor_scalar(out=k_, in0=xx, scalar1=1.0 / (2 * math.pi), scalar2=None, op0=ALU.mult), reads=[xk], writes=[kfk])
        S.op("dve", lambda e: e.tensor_copy(out=i_, in_=k_), reads=[kfk], writes=[kik])
        S.op("dve", lambda e: e.tensor_copy(out=k_, in_=i_), reads=[kik], writes=[kfk])
        S.op("dve", lambda e: e.scalar_tensor_tensor(out=xx, in0=k_, scalar=-2 * math.pi, in1=xx, op0=ALU.mult, op1=ALU.add), reads=[kfk, xk], writes=[xk])
        S.op("dve", lambda e: e.tensor_scalar(out=k_, in0=xx, scalar1=math.pi, scalar2=-2 * math.pi, op0=ALU.is_gt, op1=ALU.mult), reads=[xk], writes=[kfk])
        S.op("dve", lambda e: e.tensor_tensor(out=xx, in0=xx, in1=k_, op=ALU.add), reads=[xk, kfk], writes=[xk])
        S.op("dve", lambda e: e.tensor_scalar(out=k_, in0=xx, scalar1=-math.pi, scalar2=2 * math.pi, op0=ALU.is_lt, op1=ALU.mult), reads=[xk], writes=[kfk])
        S.op("dve", lambda e: e.tensor_tensor(out=xx, in0=xx, in1=k_, op=ALU.add), reads=[xk, kfk], writes=[xk])

    def hyena_filters(self, j):
        nc, S, T, NT = self.nc, self.S, self.T, self.NT
        self.HF = self.dt("hy_hf", [2, T, E], BF16)
        self.RN = self.dt("hy_rn", [128, E], F32)
        zT = self.sb("hy_zT", [33, T], F32)
        hA = self.sb("hy_hA", [64, T], F32)
        hB = self.sb("hy_hB", [64, T], F32)
        kf = self.sb("hy_kf", [64, T], F32)
        ki = self.sb("hy_ki", [64, T], mybir.dt.int32)
        w1 = self.sb("hy_w1", [33, 64], F32)
        w2 = self.sb("hy_w2", [64, 64], F32)
        w3 = self.sb("hy_w3", [64, 64], F32)
        w4 = self.sb("hy_w4", [64, 2 * E], F32)
        vec = self.sb("hy_fv", [64, 8], F32)
        delta = self.sb("hy_delta", [128, E], F32)
        tl = self.sb("hy_tl", [128, NT], F32)
        ssum = self.sb("hy_ssum", [128, 2 * E], F32)
        dec = self.ring("hy_dec", 2, [128, 512], F32)
        ft = self.ring("hy_ft", 2, [128, 512], F32)
        fa = self.ring("hy_fa", 2, [128, 512], F32)
        fb = self.ring("hy_fb", 2, [128, 512], BF16)
        S.dma("sp", zT[:], self.dram["c_hy_z"], writes=["hy_zT"])
        S.dma("sp", w1[:], self.dram["hy_f_w1"][j], writes=["hy_w"])
        S.dma("sp", w2[:], self.dram["hy_f_w2"][j], writes=["hy_w"])
        S.dma("sp", w3[:], self.dram["hy_f_w3"][j], writes=["hy_w"])
        S.dma("sp", w4[:], self.dram["hy_f_w4"][j], writes=["hy_w"])
        for i_, nm in enumerate(("hy_f_b1", "hy_f_b2", "hy_f_b3", "hy_f_freq")):
            S.dma("sp", vec[:, i_:i_ + 1], self.dram[nm][j].rearrange("(p o) -> p o", o=1), writes=["hy_fv"])
        S.dma("sp", delta[:], self.dram["c_hy_delta"].partition_broadcast(128), writes=["hy_delta"])
        S.dma("sp", tl[:], self.dram["c_hy_tl"], writes=["hy_tl"])
        for i_ in range(3):
            S.op("dve", lambda e: e.tensor_tensor(out=vec[:, 4 + i_:5 + i_], in0=vec[:, i_:i_ + 1], in1=vec[:, 3:4], op=ALU.mult), reads=["hy_fv"], writes=["hy_fv"])
        src, srck, krows = zT, "hy_zT", 33
        for li_, (w_, dst, dk_) in enumerate(((w1, hA, "hy_hA"), (w2, hB, "hy_hB"), (w3, hA, "hy_hA"))):
            for (t0, nt) in tok_blocks(T, 512):
                ps, pk = self.ps.next()
                S.op("pe", lambda e: e.matmul(ps[:64, :nt], lhsT=w_[:krows, :], rhs=src[:krows, t0:t0 + nt], start=True, stop=True),
                     reads=["hy_w", srck], writes=[pk])
                S.op("act", lambda e: e.activation(out=dst[:, t0:t0 + nt], in_=ps[:64, :nt], func=AF.Identity, scale=vec[:, 3:4],
                                                   bias=vec[:, 4 + li_:5 + li_]), reads=[pk, "hy_fv"], writes=[dk_])
            self.sin_reduce(dst, dk_, kf, "hy_kf", ki, "hy_ki", 64, T)
            S.op("act", lambda e: e.activation(out=dst[:, :], in_=dst[:, :], func=AF.Sin), reads=[dk_], writes=[dk_])
            src, srck, krows = dst, dk_, 64
        h3 = src
        for cb in range(2 * E // 512):
            half = cb // (E // 512)
            c0 = (cb % (E // 512)) * 512
            pacc, pacck = self.ps.next()
            for n in range(NT):
                t0 = n * 128
                nt = min(128, T - t0)
                ps, pk = self.ps.next()
                if ps is pacc:
                    ps, pk = self.ps.next()
                S.op("pe", lambda e: e.matmul(ps[:nt, :], lhsT=h3[:, t0:t0 + nt], rhs=w4[:, cb * 512:(cb + 1) * 512], start=True, stop=True),
                     reads=["hy_hA", "hy_w"], writes=[pk])
                d_, dk2 = dec.next()
                S.op("act", lambda e: e.activation(out=d_[:nt, :], in_=delta[:nt, c0:c0 + 512], func=AF.Exp, scale=tl[:nt, n:n + 1]),
                     reads=["hy_delta", "hy_tl"], writes=[dk2])
                f_, fk = ft.next()
                S.op("dve", lambda e: e.tensor_tensor(out=f_[:nt, :], in0=ps[:nt, :], in1=d_[:nt, :], op=ALU.mult), reads=[pk, dk2], writes=[fk])
                a_, ak = fa.next()
                S.op("act", lambda e: e.activation(out=a_[:nt, :], in_=f_[:nt, :], func=AF.Abs), reads=[fk], writes=[ak])
                S.op("pe", lambda e: e.matmul(pacc[:, :], lhsT=self.ones_f[:nt, :], rhs=a_[:nt, :], start=(n == 0), stop=(n == NT - 1)),
                     reads=["ones_f", ak], writes=[pacck])
                b_, bk = fb.next()
                S.op("pool", lambda e: e.tensor_copy(out=b_[:nt, :], in_=f_[:nt, :]), reads=[fk], writes=[bk])
                if half == 1 and n == 0:
                    S.op("pool", lambda e: e.memset(b_[0:1, :], 0.0), reads=[bk], writes=[bk])
                S.dma("sp", self.HF[half, t0:t0 + nt, c0:c0 + 512], b_[:nt, :], reads=[bk], writes=["HF"])
            S.op("dve", lambda e: e.tensor_copy(out=ssum[:, cb * 512:(cb + 1) * 512], in_=pacc[:, :]), reads=[pacck], writes=["hy_ssum"])
        S.op("dve", lambda e: e.tensor_tensor(out=ssum[:, 0:E], in0=ssum[:, 0:E], in1=ssum[:, E:2 * E], op=ALU.add), reads=["hy_ssum"], writes=["hy_ssum"])
        S.op("dve", lambda e: e.tensor_scalar(out=ssum[:, 0:E], in0=ssum[:, 0:E], scalar1=1e-6, scalar2=None, op0=ALU.add), reads=["hy_ssum"], writes=["hy_ssum"])
        S.op("dve", lambda e: e.reciprocal(out=ssum[:, 0:E], in_=ssum[:, 0:E]), reads=["hy_ssum"], writes=["hy_ssum"])
        S.dma("sp", self.RN, ssum[:, 0:E], reads=["hy_ssum"], writes=["RN"])

    def setup_hyena_proj(self):
        T = self.T
        self.hy_xp = self.sb("hy_xp", [128, T + 2], F32)
        self.hy_st = [self.sb("hy_st%d" % i, [128, T], F32) for i in range(3)]
        self.hy_pv = self.sb("hy_pvec", [128, 8], F32)
        self.hy_bf = self.ring("hy_bf", 2, [128, T], BF16)
        self.hy_tb = self.ring("hy_tb", 2, [128, 128], BF16)
        self.hy_sz = self.ring("hy_sz", 2, [128, 512], F32)
        self.VX = self.dt("hy_vx", [self.nseq, E, T], BF16)
        self.GX = self.dt("hy_gx", [self.nseq, E, T], BF16)
        self.VXT = self.dt("hy_vxt", [self.nseq, T, E], BF16)

    def hyena_proj(self, j, s):
        nc, S, T, NT = self.nc, self.S, self.T, self.NT
        xp, pv = self.hy_xp, self.hy_pv
        if s == 0:
            S.op("dve", lambda e: e.memset(xp[:], 0.0), writes=["hy_xp"])
        col1 = lambda ap: ap.rearrange("(p o) -> p o", o=1)
        for c in range(E // 128):
            for si in range(3):
                col0 = si * E + c * 128
                wb, wbk = self.load_wcols("hy_w_in", j, col0, 128)
                S.dma("sp", pv[:, 0:1], col1(self.dram["hy_b_in"][j][col0:col0 + 128]), writes=["hy_pv"])
                S.dma("sp", pv[:, 1:4], self.dram["hy_conv_w"][j][:, col0:col0 + 128].rearrange("k p -> p k"), writes=["hy_pv"],
                      allow_slow_non_contiguous=True)
                S.dma("sp", pv[:, 4:5], col1(self.dram["hy_conv_b"][j][col0:col0 + 128]), writes=["hy_pv"])
                for (t0, nt) in tok_blocks(T, 512):
                    ps, pk = self.proj_fm(wb, wbk, 128, t0, nt)
                    S.op("act", lambda e: e.activation(out=xp[:, 1 + t0:1 + t0 + nt], in_=ps[:, :nt], func=AF.Identity, bias=pv[:, 0:1]),
                         reads=[pk, "hy_pv"], writes=["hy_xp"])
                st, stk = self.hy_st[si], "hy_st%d" % si
                S.op("dve", lambda e: e.tensor_scalar(out=st[:], in0=xp[:, 0:T], scalar1=pv[:, 1:2], scalar2=pv[:, 4:5], op0=ALU.mult, op1=ALU.add),
                     reads=["hy_xp", "hy_pv"], writes=[stk])
                S.op("dve", lambda e: e.scalar_tensor_tensor(out=st[:], in0=xp[:, 1:1 + T], scalar=pv[:, 2:3], in1=st[:], op0=ALU.mult, op1=ALU.add),
                     reads=["hy_xp", "hy_pv", stk], writes=[stk])
                S.op("dve", lambda e: e.scalar_tensor_tensor(out=st[:], in0=xp[:, 2:2 + T], scalar=pv[:, 3:4], in1=st[:], op0=ALU.mult, op1=ALU.add),
                     reads=["hy_xp", "hy_pv", stk], writes=[stk])
            x0, x1, v = self.hy_st
            S.op("dve", lambda e: e.tensor_tensor(out=v[:], in0=v[:], in1=x1[:], op=ALU.mult), reads=["hy_st2", "hy_st1"], writes=["hy_st2"])
            vb, vbk = self.hy_bf.next()
            S.op("pool", lambda e: e.tensor_copy(out=vb[:], in_=v[:]), reads=["hy_st2"], writes=[vbk])
            S.dma("sp", self.VX[s, c * 128:(c + 1) * 128, :], vb[:], reads=[vbk], writes=[("VX", s)])
            for n in range(NT):
                t0 = n * 128
                nt = min(128, T - t0)
                ps, pk = self.ps.next()
                S.op("pe", lambda e: e.transpose(out=ps[:nt, :128], in_=v[:, t0:t0 + nt], identity=self.ident[:]), reads=["hy_st2", "ident"], writes=[pk])
                tb, tbk = self.hy_tb.next()
                S.op("act", lambda e: e.activation(out=tb[:nt, :], in_=ps[:nt, :128], func=AF.Identity), reads=[pk], writes=[tbk])
                S.dma("sp", self.VXT[s, t0:t0 + nt, c * 128:(c + 1) * 128], tb[:nt, :], reads=[tbk], writes=[("VXT", s)])
            col0 = 3 * E + c * 128
            wb, wbk = self.load_wcols("hy_w_in", j, col0, 128)
            S.dma("sp", pv[:, 5:6], col1(self.dram["hy_b_in"][j][col0:col0 + 128]), writes=["hy_pv"])
            for (t0, nt) in tok_blocks(T, 512):
                ps, pk = self.proj_fm(wb, wbk, 128, t0, nt)
                sz, szk = self.hy_sz.next()
                S.op("act", lambda e: e.activation(out=sz[:, :nt], in_=ps[:, :nt], func=AF.Silu, bias=pv[:, 5:6]), reads=[pk, "hy_pv"], writes=[szk])
                S.op("dve", lambda e: e.tensor_tensor(out=x0[:, t0:t0 + nt], in0=x0[:, t0:t0 + nt], in1=sz[:, :nt], op=ALU.mult),
                     reads=["hy_st0", szk], writes=["hy_st0"])
            gb, gbk = self.hy_bf.next()
            S.op("pool", lambda e: e.tensor_copy(out=gb[:], in_=x0[:]), reads=["hy_st0"], writes=[gbk])
            S.dma("sp", self.GX[s, c * 128:(c + 1) * 128, :], gb[:], reads=[gbk], writes=[("GX", s)])

    def load_tokmajor(self, dst, dkey, src2d, c0, ncols, rkeys):
        S, T, NT = self.S, self.T, self.NT
        nfull = T // 128
        if nfull:
            S.dma("sp", dst[:, 0:nfull, :ncols], src2d[0:nfull * 128, c0:c0 + ncols].rearrange("(n p) c -> p n c", p=128),
                  reads=rkeys, writes=[dkey])
        rem = T - nfull * 128
        if rem:
            S.dma("sp", dst[:rem, nfull, :ncols], src2d[nfull * 128:T, c0:c0 + ncols], reads=rkeys, writes=[dkey])

    def hyena_spectral(self, j):
        nc, S, T, NT, NFC, NK = self.nc, self.S, self.T, self.NT, self.NFC, self.NK
        Fm = self.dram["c_dft_f"]
        Gm = self.dram["c_dft_g"]
        self.HS = self.dt("hy_hs", [NK * 128, E], F32)
        fch = self.ring("hy_fch", 2, [128, NT, 128], BF16)
        rn = self.sb("hy_rnb", [128, E], F32)
        S.dma("sp", rn[:], self.RN, reads=["RN"], writes=["hy_rnb"])
        dat = [self.sb("hy_dat%d" % i, [128, NT, 512], BF16) for i in range(3)]
        hs_t = self.ring("hy_hst", 2, [128, 512], F32)
        for i_ in range(3):
            S.op("pool", lambda e: e.memset(dat[i_][:], 0.0), writes=["hy_dat%d" % i_])

        def load_f(kc):
            f_, fk = fch.next()
            self.load_tokmajor(f_, fk, Fm, kc * 128, 128, [])
            return f_, fk

        def fwd(kc, srcs, ps, pk):
            f_, fk = load_f(kc)
            tot = len(srcs) * NT
            i_ = 0
            for (d_, dk_) in srcs:
                for n in range(NT):
                    nt = min(128, T - n * 128)
                    S.op("pe", lambda e: e.matmul(ps[:, :], lhsT=f_[:nt, n, :], rhs=d_[:nt, n, :], start=(i_ == 0), stop=(i_ == tot - 1)),
                         reads=[fk, dk_], writes=[pk])
                    i_ += 1
        for cb in range(E // 512):
            c0 = cb * 512
            self.load_tokmajor(dat[0], "hy_dat0", self.HF[0], c0, 512, ["HF"])
            self.load_tokmajor(dat[1], "hy_dat1", self.HF[1], c0, 512, ["HF"])
            S.op("pool", lambda e: e.tensor_scalar(out=dat[2][:].rearrange("p n c -> p (n c)"), in0=dat[1][:].rearrange("p n c -> p (n c)"),
                                                   scalar1=-1.0, scalar2=None, op0=ALU.mult), reads=["hy_dat1"], writes=["hy_dat2"])
            for kc in range(NK):
                ps, pk = self.ps.next()
                second = (dat[1], "hy_dat1") if kc < NFC else (dat[2], "hy_dat2")
                fwd(kc, [(dat[0], "hy_dat0"), second], ps, pk)
                h_, hk = hs_t.next()
                S.op("dve", lambda e: e.tensor_tensor(out=h_[:], in0=ps[:, :], in1=rn[:, c0:c0 + 512], op=ALU.mult), reads=[pk, "hy_rnb"], writes=[hk])
                S.dma("sp", self.HS[kc * 128:(kc + 1) * 128, c0:c0 + 512], h_[:], reads=[hk], writes=["HS"])
        self.close_scope()
        self.open_scope()
        fch = self.ring("hy_fch2", 2, [128, NT, 128], BF16)
        skip = self.sb("hy_skip", [128, E // 128], F32)
        S.dma("sp", skip[:], self.dram["hy_skip"][j].rearrange("(c p) -> p c", p=128), writes=["hy_skip"], allow_slow_non_contiguous=True)
        dat = [self.sb("hy_dat0b", [128, NT, 512], BF16)]
        S.op("pool", lambda e: e.memset(dat[0][:], 0.0), writes=["hy_dat0"])
        zt = self.sb("hy_zt", [128, NK, 512], BF16)
        hr_t = self.ring("hy_hr", 2, [128, 512], F32)
        hi_t = self.ring("hy_hi", 2, [128, 512], F32)
        tmp = self.ring("hy_tmp", 4, [128, 512], F32)
        gti = self.ring("hy_gt", 6, [128, 512], BF16)
        vxb = self.ring("hy_vxb", 2, [128, 512], BF16)
        gxb = self.ring("hy_gxb", 2, [128, 512], BF16)
        yo = self.ring("hy_yo", 2, [128, 512], F32)
        go = self.ring("hy_go", 2, [128, 512], BF16)
        acc = Ring("hacc", self.PB[0:4], self.PBK[0:4])
        for s in range(self.nseq):
            for cb in range(E // 512):
                c0 = cb * 512
                self.load_tokmajor(dat[0], "hy_dat0", self.VXT[s], c0, 512, [("VXT", s)])
                for i_ in range(NFC):
                    pr, prk = self.PB[4], self.PBK[4]
                    pi, pik = self.PB[5], self.PBK[5]
                    fwd(i_, [(dat[0], "hy_dat0")], pr, prk)
                    fwd(NFC + i_, [(dat[0], "hy_dat0")], pi, pik)
                    hr, hrk = hr_t.next()
                    hi, hik = hi_t.next()
                    S.dma("sp", hr[:], self.HS[i_ * 128:(i_ + 1) * 128, c0:c0 + 512], reads=["HS"], writes=[hrk])
                    S.dma("sp", hi[:], self.HS[(NFC + i_) * 128:(NFC + i_ + 1) * 128, c0:c0 + 512], reads=["HS"], writes=[hik])
                    t1, t1k = tmp.next()
                    t2, t2k = tmp.next()
                    S.op("dve", lambda e: e.tensor_tensor(out=t1[:], in0=pr[:, :], in1=hr[:], op=ALU.mult), reads=[prk, hrk], writes=[t1k])
                    S.op("dve", lambda e: e.tensor_tensor(out=t2[:], in0=pi[:, :], in1=hi[:], op=ALU.mult), reads=[pik, hik], writes=[t2k])
                    S.op("dve", lambda e: e.tensor_tensor(out=zt[:, i_, :], in0=t1[:], in1=t2[:], op=ALU.subtract), reads=[t1k, t2k], writes=["hy_zt"])
                    t3, t3k = tmp.next()
                    t4, t4k = tmp.next()
                    S.op("dve", lambda e: e.tensor_tensor(out=t3[:], in0=pr[:, :], in1=hi[:], op=ALU.mult), reads=[prk, hik], writes=[t3k])
                    S.op("dve", lambda e: e.tensor_tensor(out=t4[:], in0=pi[:, :], in1=hr[:], op=ALU.mult), reads=[pik, hrk], writes=[t4k])
                    S.op("dve", lambda e: e.tensor_tensor(out=zt[:, NFC + i_, :], in0=t3[:], in1=t4[:], op=ALU.add), reads=[t3k, t4k], writes=["hy_zt"])
                for (t0, nt) in tok_blocks(T, 512):
                    for kc in range(NK):
                        g_, gk = gti.next()
                        S.dma("sp" if kc % 2 == 0 else "pool", g_[:, :nt], Gm[kc * 128:(kc + 1) * 128, t0:t0 + nt], writes=[gk])
                        for cs in range(4):
                            S.op("pe", lambda e: e.matmul(acc.bufs[cs][:, :nt], lhsT=zt[:, kc, cs * 128:(cs + 1) * 128], rhs=g_[:, :nt],
                                                          start=(kc == 0), stop=(kc == NK - 1)), reads=["hy_zt", gk], writes=[acc.keys[cs]])
                    for cs in range(4):
                        r0 = c0 + cs * 128
                        vx_, vxk = vxb.next()
                        gx_, gxk = gxb.next()
                        S.dma("pool", vx_[:, :nt], self.VX[s, r0:r0 + 128, t0:t0 + nt], reads=[("VX", s)], writes=[vxk])
                        S.dma("pool", gx_[:, :nt], self.GX[s, r0:r0 + 128, t0:t0 + nt], reads=[("GX", s)], writes=[gxk])
                        y_, yk = yo.next()
                        S.op("dve", lambda e: e.scalar_tensor_tensor(out=y_[:, :nt], in0=vx_[:, :nt], scalar=skip[:, r0 // 128:r0 // 128 + 1],
                                                                     in1=acc.bufs[cs][:, :nt], op0=ALU.mult, op1=ALU.add),
                             reads=[vxk, "hy_skip", acc.keys[cs]], writes=[yk])
                        g2, g2k = go.next()
                        S.op("dve", lambda e: e.tensor_tensor(out=g2[:, :nt], in0=y_[:, :nt], in1=gx_[:, :nt], op=ALU.mult), reads=[yk, gxk], writes=[g2k])
                        S.dma("sp", self.G[s, r0:r0 + 128, t0:t0 + nt], g2[:, :nt], reads=[g2k], writes=[("G", s)])

    def setup_gdn(self):
        T = self.T
        self.NT = (T + 127) // 128
        Tp = self.NT * 128
        self.Tp = Tp
        NT = self.NT
        self.setup_head_norm()
        self.gd_c = self.sb("gd_c", [128, 10, 128], F32)
        self.gd_xp = self.sb("gd_xp", [128, Tp + 2], BF16)
        self.gd_xs = self.sb("gd_xs", [128, Tp + 2], F32)
        self.gd_q = self.sb("gd_q", [128, Tp], BF16)
        self.gd_k = self.sb("gd_k", [128, Tp], BF16)
        self.gd_ktok = self.sb("gd_ktok", [128, NT, 128], BF16)
        self.gd_vtok = self.sb("gd_vtok", [128, NT, 128], BF16)
        self.gd_o = self.sb("gd_o", [128, Tp], BF16)
        self.gd_tab = {n: self.sb("gd_" + n, [128, NT, 32], F32) for n in ("gam", "beta", "nb", "ksc", "egl")}
        self.gd_cw = self.sb("gd_cw", [128, 3], F32)
        self.gd_gc = self.sb("gd_gc", [128, 2], F32)
        self.gd_gain = self.sb("gd_gain", [128, 1], F32)
        self.gd_wgb = self.sb("gd_wgb", [128, KC, 128], BF16)
        self.gd_sq = self.hn_sq
        self.gd_r = self.hn_r
        self.gd_m = self.ring("gd_m", 16, [128, 128], F32)
        self.gd_e = self.ring("gd_e", 6, [128, 128], F32)
        self.gd_tt = self.ring("gd_tt", 3, [128, 128], BF16)
        self.gd_gtn = self.gd_m
        self.gd_mb = self.ring("gd_mb", 8, [128, 128], BF16)
        self.gd_kk = self.ring("gd_kk", 3, [128, 2, 128], F32)
        self.gd_S = [self.sb("gd_S%d" % i, [128, 128], F32) for i in range(2)]
        self.gd_Sb = [self.sb("gd_Sb%d" % i, [128, 128], BF16) for i in range(2)]
        self.gd_sz = self.ring("gd_sz", 1, [128, 512], BF16)

    def gdn_qkv_fm(self, j, col0, cw_row0, dst_fp32):
        S, T, Tp = self.S, self.T, self.Tp
        wb, wbk = self.load_wcols("gdn_w_in", j, col0, 128)
        S.dma("sp", self.gd_cw[:], self.dram["gdn_conv_w"][j][:, cw_row0:cw_row0 + 128].rearrange("k p -> p k"),
              writes=["gd_cw"], allow_slow_non_contiguous=True)
        xp = self.gd_xp
        for (t0, nt) in tok_blocks(T, 512):
            ps, pk = self.proj_fm(wb, wbk, 128, t0, nt)
            S.op("act", lambda e: e.activation(out=xp[:, 1 + t0:1 + t0 + nt], in_=ps[:, :nt], func=AF.Identity), reads=[pk], writes=["gd_xp"])
        xs = dst_fp32
        S.op("dve", lambda e: e.tensor_scalar(out=xs[:, 1:1 + T], in0=xp[:, 0:T], scalar1=self.gd_cw[:, 0:1], scalar2=None, op0=ALU.mult),
             reads=["gd_xp", "gd_cw"], writes=["gd_xs"])
        S.op("dve", lambda e: e.scalar_tensor_tensor(out=xs[:, 1:1 + T], in0=xp[:, 1:1 + T], scalar=self.gd_cw[:, 1:2], in1=xs[:, 1:1 + T],
                                                     op0=ALU.mult, op1=ALU.add), reads=["gd_xp", "gd_cw", "gd_xs"], writes=["gd_xs"])
        S.op("dve", lambda e: e.scalar_tensor_tensor(out=xs[:, 1:1 + T], in0=xp[:, 2:2 + T], scalar=self.gd_cw[:, 2:3], in1=xs[:, 1:1 + T],
                                                     op0=ALU.mult, op1=ALU.add), reads=["gd_xp", "gd_cw", "gd_xs"], writes=["gd_xs"])
        S.op("act", lambda e: e.activation(out=xs[:, 1:1 + T], in_=xs[:, 1:1 + T], func=AF.Silu), reads=["gd_xs"], writes=["gd_xs"])

    def gdn_l2norm_to(self, dst_bf, dkey, scale):
        S, T = self.S, self.T
        xs = self.gd_xs
        for (t0, nt) in tok_blocks(T, 512):
            sq, sqk = self.gd_sq.next()
            S.op("act", lambda e: e.activation(out=sq[:, :nt], in_=xs[:, 1 + t0:1 + t0 + nt], func=AF.Square), reads=["gd_xs"], writes=[sqk])
            ps, pk = self.ps.next()
            S.op("pe", lambda e: e.matmul(ps[:, :nt], lhsT=self.ones_b[:], rhs=sq[:, :nt], start=True, stop=True), reads=["ones_b", sqk], writes=[pk])
            r, rk = self.gd_r.next()
            S.op("act", lambda e: e.activation(out=r[:, :nt], in_=ps[:, :nt], func=AF.Ln, bias=1e-6), reads=[pk], writes=[rk])
            S.op("act", lambda e: e.activation(out=r[:, :nt], in_=r[:, :nt], func=AF.Exp, scale=-0.5), reads=[rk], writes=[rk])
            S.op("dve", lambda e: e.scalar_tensor_tensor(out=xs[:, 1 + t0:1 + t0 + nt], in0=xs[:, 1 + t0:1 + t0 + nt], scalar=float(scale),
                                                         in1=r[:, :nt], op0=ALU.mult, op1=ALU.mult), reads=["gd_xs", rk], writes=["gd_xs"])
        S.op("pool", lambda e: e.tensor_copy(out=dst_bf[:, 0:T], in_=xs[:, 1:1 + T]), reads=["gd_xs"], writes=[dkey])

    def gdn_to_tok(self, dst, dkey):
        S, NT = self.S, self.NT
        xs = self.gd_xs
        for n in range(NT):
            ps, pk = self.ps.next()
            S.op("pe", lambda e: e.transpose(out=ps[:, :128], in_=xs[:, 1 + n * 128:1 + (n + 1) * 128], identity=self.ident[:]),
                 reads=["gd_xs", "ident"], writes=[pk])
            S.op("act", lambda e: e.activation(out=dst[:, n, :], in_=ps[:, :128], func=AF.Identity), reads=[pk], writes=[dkey])

    def gdn_inv_gen(self, cx):
        S = self.S
        C = self.gd_c
        tab = self.gd_tab
        h, d, n = cx["h"], cx["d"], cx["n"]
        col = d * 16 + h
        maskA, maskT, nstrict = C[:, 4 + d, :], C[:, 6 + d, :], C[:, 8 + d, :]
        c0 = n * 128
        kc_ = self.gd_k[:, c0:c0 + 128]
        qc_ = self.gd_q[:, c0:c0 + 128]
        gcol = tab["gam"][:, n, col:col + 1]
        kk, kkk = self.gd_kk.next()
        ps, pk = self.ps.next()
        S.op("pe", lambda e: e.matmul(ps[:, 0:128], lhsT=kc_, rhs=kc_, start=True, stop=True), reads=["gd_k"], writes=[pk]); yield
        S.op("pe", lambda e: e.matmul(ps[:, 128:256], lhsT=kc_, rhs=qc_, start=True, stop=True), reads=["gd_k", "gd_q"], writes=[pk]); yield
        S.op("dve", lambda e: e.tensor_tensor(out=kk[:, 0, :], in0=ps[:, 0:128], in1=nstrict, op=ALU.mult), reads=[pk, "gd_c"], writes=[kkk]); yield
        S.op("dve", lambda e: e.tensor_copy(out=kk[:, 1, :], in_=ps[:, 128:256]), reads=[pk], writes=[kkk]); yield
        dg, dgk = self.gd_m.next()
        S.op("dve", lambda e: e.tensor_scalar(out=dg[:], in0=self.ident[:], scalar1=gcol, scalar2=None, op0=ALU.mult),
             reads=["ident", "gd_gam"], writes=[dgk]); yield
        pg, pgk = self.ps.next()
        S.op("pe", lambda e: e.matmul(pg[:, 0:128], lhsT=self.ones_f[:], rhs=dg[:], start=True, stop=True), reads=["ones_f", dgk], writes=[pgk]); yield
        e1, e1k = self.gd_m.next()
        S.op("dve", lambda e: e.scalar_tensor_tensor(out=e1[:], in0=pg[:, 0:128], scalar=gcol, in1=maskA, op0=ALU.subtract, op1=ALU.add),
             reads=[pgk, "gd_gam", "gd_c"], writes=[e1k]); yield
        e2, e2k = self.gd_e.next()
        S.op("dve", lambda e: e.scalar_tensor_tensor(out=e2[:], in0=pg[:, 0:128], scalar=gcol, in1=maskT, op0=ALU.subtract, op1=ALU.add),
             reads=[pgk, "gd_gam", "gd_c"], writes=[e2k]); yield
        eg, egk = self.gd_e.next()
        S.op("dve", lambda e: e.tensor_copy(out=eg[:], in_=pg[:, 0:128]), reads=[pgk], writes=[egk]); yield
        S.op("act", lambda e: e.activation(out=e1[:], in_=e1[:], func=AF.Exp, scale=-1.0), reads=[e1k], writes=[e1k]); yield
        S.op("act", lambda e: e.activation(out=e2[:], in_=e2[:], func=AF.Exp), reads=[e2k], writes=[e2k]); yield
        S.op("act", lambda e: e.activation(out=eg[:], in_=eg[:], func=AF.Exp), reads=[egk], writes=[egk]); yield
        L, Lk = self.gd_m.next()
        S.op("dve", lambda e: e.scalar_tensor_tensor(out=L[:], in0=e1[:], scalar=tab["beta"][:, n, col:col + 1], in1=kk[:, 0, :],
                                                     op0=ALU.mult, op1=ALU.mult), reads=[e1k, "gd_beta", kkk], writes=[Lk]); yield
        pt, ptk = self.ps.next()
        S.op("pe", lambda e: e.transpose(out=pt[:, 0:128], in_=L[:], identity=self.ident[:]), reads=[Lk, "ident"], writes=[ptk]); yield
        M, Mk = self.gd_m.next()
        S.op("act", lambda e: e.activation(out=M[:], in_=pt[:, 0:128], func=AF.Identity), reads=[ptk], writes=[Mk]); yield
        P, Pk = self.gd_m.next()
        S.op("dve", lambda e: e.tensor_tensor(out=P[:], in0=M[:], in1=self.ident[:], op=ALU.add), reads=[Mk, "ident"], writes=[Pk]); yield
        for lvl in range(1, 7):
            p2, p2k = self.ps.next()
            S.op("pe", lambda e: e.matmul(p2[:, 0:128], lhsT=M[:], rhs=L[:], start=True, stop=True), reads=[Mk, Lk], writes=[p2k]); yield
            if lvl < 6:
                S.op("pe", lambda e: e.matmul(p2[:, 128:256], lhsT=L[:], rhs=M[:], start=True, stop=True), reads=[Mk, Lk], writes=[p2k]); yield
            L2, L2k = self.gd_m.next()
            S.op("act", lambda e: e.activation(out=L2[:], in_=p2[:, 0:128], func=AF.Identity), reads=[p2k], writes=[L2k]); yield
            if lvl < 6:
                M2, M2k = self.gd_m.next()
                S.op("act", lambda e: e.activation(out=M2[:], in_=p2[:, 128:256], func=AF.Identity), reads=[p2k], writes=[M2k]); yield
            p3, p3k = self.ps.next()
            S.op("pe", lambda e: e.matmul(p3[:, 0:128], lhsT=L2[:], rhs=P[:], start=True, stop=True), reads=[L2k, Pk], writes=[p3k]); yield
            P2, P2k = self.gd_m.next()
            S.op("dve", lambda e: e.tensor_tensor(out=P2[:], in0=p3[:, 0:128], in1=P[:], op=ALU.add), reads=[p3k, Pk], writes=[P2k]); yield
            P, Pk = P2, P2k
            L, Lk = L2, L2k
            if lvl < 6:
                M, Mk = M2, M2k
        TT, TTk = self.gd_tt.next()
        S.op("pool", lambda e: e.tensor_copy(out=TT[:], in_=P[:]), reads=[Pk], writes=[TTk]); yield
        cx.update(kk=kk, kkk=kkk, e2=e2, e2k=e2k, eg=eg, egk=egk, TT=TT, TTk=TTk)

    def gdn_scan_gen(self, cx):
        S = self.S
        tab = self.gd_tab
        h, d, n = cx["h"], cx["d"], cx["n"]
        col = d * 16 + h
        kk, kkk, e2, e2k, eg, egk, TT, TTk = (cx[k_] for k_ in ("kk", "kkk", "e2", "e2k", "eg", "egk", "TT", "TTk"))
        Sf, Sb = self.gd_S[d], self.gd_Sb[d]
        sk, sbk = "gd_S%d" % d, "gd_Sb%d" % d
        c0 = n * 128
        kc_ = self.gd_k[:, c0:c0 + 128]
        qc_ = self.gd_q[:, c0:c0 + 128]
        pks, pksk = self.ps.next()
        S.op("pe", lambda e: e.matmul(pks[:, 0:128], lhsT=kc_, rhs=Sb[:], start=True, stop=True), reads=["gd_k", sbk], writes=[pksk]); yield
        vb, vbk = self.gd_mb.next()
        S.op("pool", lambda e: e.tensor_scalar(out=vb[:], in0=self.gd_vtok[:, n, :], scalar1=tab["beta"][:, n, col:col + 1], scalar2=None, op0=ALU.mult),
             reads=["gd_vtok", "gd_beta"], writes=[vbk]); yield
        R, Rk = self.gd_mb.next()
        S.op("dve", lambda e: e.scalar_tensor_tensor(out=R[:], in0=pks[:, 0:128], scalar=tab["nb"][:, n, col:col + 1], in1=vb[:],
                                                     op0=ALU.mult, op1=ALU.add), reads=[pksk, "gd_nb", vbk], writes=[Rk]); yield
        pv, pvk = self.ps.next()
        S.op("pe", lambda e: e.matmul(pv[:, 0:128], lhsT=TT[:], rhs=R[:], start=True, stop=True), reads=[TTk, Rk], writes=[pvk]); yield
        VN, VNk = self.gd_mb.next()
        S.op("act", lambda e: e.activation(out=VN[:], in_=pv[:, 0:128], func=AF.Identity), reads=[pvk], writes=[VNk]); yield
        qg, qgk = self.gd_mb.next()
        S.op("dve", lambda e: e.tensor_tensor(out=qg[:], in0=qc_, in1=eg[:], op=ALU.mult), reads=["gd_q", egk], writes=[qgk]); yield
        at, atk = self.gd_mb.next()
        S.op("dve", lambda e: e.tensor_tensor(out=at[:], in0=kk[:, 1, :], in1=e2[:], op=ALU.mult), reads=[kkk, e2k], writes=[atk]); yield
        po, pok = self.ps.next()
        S.op("pe", lambda e: e.matmul(po[:, 0:128], lhsT=Sb[:], rhs=qg[:], start=True, stop=False), reads=[sbk, qgk], writes=[pok]); yield
        S.op("pe", lambda e: e.matmul(po[:, 0:128], lhsT=VN[:], rhs=at[:], start=False, stop=True), reads=[VNk, atk], writes=[pok]); yield
        if d == 0:
            S.op("act", lambda e: e.activation(out=self.gd_o[:, c0:c0 + 128], in_=po[:, 0:128], func=AF.Identity), reads=[pok], writes=["gd_o"]); yield
        else:
            S.op("dve", lambda e: e.tensor_tensor(out=self.gd_o[:, c0:c0 + 128], in0=po[:, 0:128], in1=self.gd_o[:, c0:c0 + 128], op=ALU.add),
                 reads=[pok, "gd_o"], writes=["gd_o"]); yield
        ke, kek = self.gd_mb.next()
        S.op("pool", lambda e: e.tensor_scalar(out=ke[:], in0=self.gd_ktok[:, n, :], scalar1=tab["ksc"][:, n, col:col + 1], scalar2=None, op0=ALU.mult),
             reads=["gd_ktok", "gd_ksc"], writes=[kek]); yield
        pS, pSk = self.ps.next()
        S.op("pe", lambda e: e.matmul(pS[:, 0:128], lhsT=ke[:], rhs=VN[:], start=True, stop=True), reads=[kek, VNk], writes=[pSk]); yield
        S.op("dve", lambda e: e.scalar_tensor_tensor(out=Sf[:], in0=Sf[:], scalar=tab["egl"][:, n, col:col + 1], in1=pS[:, 0:128],
                                                     op0=ALU.mult, op1=ALU.add), reads=[sk, "gd_egl", pSk], writes=[sk]); yield
        S.op("act", lambda e: e.activation(out=Sb[:], in_=Sf[:], func=AF.Identity), reads=[sk], writes=[sbk]); yield

    def mixer_gdn(self, li, j, s):
        nc, S, T, Tp, NT = self.nc, self.S, self.T, self.Tp, self.NT
        C = self.gd_c
        tab = self.gd_tab
        Uf, Ub, SELL, SELF = C[:, 0, :], C[:, 1, :], C[:, 2, :], C[:, 3, :]
        maskA = [C[:, 4, :], C[:, 5, :]]
        maskT = [C[:, 6, :], C[:, 7, :]]
        nstrict = [C[:, 8, :], C[:, 9, :]]
        if s == 0:
            S.dma("sp", C[:], self.dram["c_gdn"], writes=["gd_c"])
            S.dma("sp", self.gd_gain[:], self.dram["gdn_o_norm"][j].rearrange("(p o) -> p o", o=1), writes=["hn_gain"])
            wg_, wgk_ = self.wring.next()
            S.op("dve", lambda e: e.memset(wg_[:], 0.0), writes=[wgk_])
            S.op("dve", lambda e: e.memset(self.gd_gc[:], 0.0), writes=["gd_gc"])
            for q4 in range(4):
                src = self.dram["gdn_w_in"][j][:, 3 * E + q4 * 16:3 * E + (q4 + 1) * 16].rearrange("(kc p) w -> p kc w", p=128)
                S.dma("sp", wg_[:, :, q4 * 32:q4 * 32 + 16], src, writes=[wgk_])
            for d in range(2):
                S.dma("sp", self.gd_gc[d * 32:d * 32 + 16, 0:1], self.dram["gdn_dt_bias"][j][d].rearrange("(p o) -> p o", o=1), writes=["gd_gc"])
                S.dma("sp", self.gd_gc[d * 32:d * 32 + 16, 1:2], self.dram["gdn_a_log"][j][d].rearrange("(p o) -> p o", o=1), writes=["gd_gc"])
            S.op("pool", lambda e: e.tensor_copy(out=self.gd_wgb[:], in_=wg_[:]), reads=[wgk_], writes=["gd_wgb"])
            S.op("act", lambda e: e.activation(out=self.gd_gc[0:64, 1:2], in_=self.gd_gc[0:64, 1:2], func=AF.Exp), reads=["gd_gc"], writes=["gd_gc"])
            S.op("dve", lambda e: e.tensor_scalar(out=self.gd_gc[0:64, 1:2], in0=self.gd_gc[0:64, 1:2], scalar1=-1.0, scalar2=None, op0=ALU.mult),
                 reads=["gd_gc"], writes=["gd_gc"])
        gf = self.gd_xs
        S.op("dve", lambda e: e.memset(gf[:], 0.0), writes=["gd_xs"])
        for (t0, nt) in tok_blocks(T, 512):
            ps, pk = self.proj_fm(self.gd_wgb, "gd_wgb", 128, t0, nt)
            S.op("act", lambda e: e.activation(out=gf[0:64, t0:t0 + nt], in_=ps[0:64, :nt], func=AF.Exp, bias=self.gd_gc[0:64, 0:1]),
                 reads=[pk, "gd_gc"], writes=["gd_xs"])
            S.op("act", lambda e: e.activation(out=gf[64:128, t0:t0 + nt], in_=ps[64:128, :nt], func=AF.Sigmoid), reads=[pk], writes=["gd_xs"])
        S.op("act", lambda e: e.activation(out=gf[0:64, 0:T], in_=gf[0:64, 0:T], func=AF.Ln, bias=1.0), reads=["gd_xs"], writes=["gd_xs"])
        S.op("dve", lambda e: e.tensor_scalar(out=gf[0:64, 0:T], in0=gf[0:64, 0:T], scalar1=self.gd_gc[0:64, 1:2], scalar2=None, op0=ALU.mult),
             reads=["gd_xs", "gd_gc"], writes=["gd_xs"])
        for n in range(NT):
            ps, pk = self.ps.next()
            S.op("pe", lambda e: e.transpose(out=ps[:, :128], in_=gf[:, n * 128:(n + 1) * 128], identity=self.ident[:]),
                 reads=["gd_xs", "ident"], writes=[pk])
            gtn, gtk = self.gd_gtn.next()
            S.op("act", lambda e: e.activation(out=gtn[:], in_=ps[:, :128], func=AF.Identity), reads=[pk], writes=[gtk])
            ps, pk = self.ps.next()
            S.op("pe", lambda e: e.matmul(ps[:, 0:16], lhsT=Uf, rhs=gtn[:, 0:16], start=True, stop=True), reads=["gd_c", gtk], writes=[pk])
            S.op("pe", lambda e: e.matmul(ps[:, 16:32], lhsT=Ub, rhs=gtn[:, 32:48], start=True, stop=True), reads=["gd_c", gtk], writes=[pk])
            S.op("dve", lambda e: e.tensor_copy(out=tab["gam"][:, n, :], in_=ps[:, 0:32]), reads=[pk], writes=["gd_gam"])
            S.op("pool", lambda e: e.tensor_copy(out=tab["beta"][:, n, 0:16], in_=gtn[:, 64:80]), reads=[gtk], writes=["gd_beta"])
            S.op("pool", lambda e: e.tensor_copy(out=tab["beta"][:, n, 16:32], in_=gtn[:, 96:112]), reads=[gtk], writes=["gd_beta"])
        for n in range(NT):
            ps, pk = self.ps.next()
            S.op("pe", lambda e: e.matmul(ps[:, 0:16], lhsT=SELL, rhs=tab["gam"][:, n, 0:16], start=True, stop=True), reads=["gd_c", "gd_gam"], writes=[pk])
            S.op("pe", lambda e: e.matmul(ps[:, 16:32], lhsT=SELF, rhs=tab["gam"][:, n, 16:32], start=True, stop=True), reads=["gd_c", "gd_gam"], writes=[pk])
            S.op("dve", lambda e: e.tensor_copy(out=tab["egl"][:, n, :], in_=ps[:, 0:32]), reads=[pk], writes=["gd_egl"])
        fl = lambda t: t[:].rearrange("p n c -> p (n c)")
        S.op("dve", lambda e: e.tensor_tensor(out=fl(tab["ksc"]), in0=fl(tab["egl"]), in1=fl(tab["gam"]), op=ALU.subtract), reads=["gd_egl", "gd_gam"], writes=["gd_ksc"])
        S.op("act", lambda e: e.activation(out=fl(tab["ksc"]), in_=fl(tab["ksc"]), func=AF.Exp), reads=["gd_ksc"], writes=["gd_ksc"])
        S.op("act", lambda e: e.activation(out=fl(tab["egl"]), in_=fl(tab["egl"]), func=AF.Exp), reads=["gd_egl", "gd_ksc"], writes=["gd_egl"])
        S.op("act", lambda e: e.activation(out=fl(tab["nb"]), in_=fl(tab["gam"]), func=AF.Exp), reads=["gd_gam"], writes=["gd_nb"])
        S.op("dve", lambda e: e.scalar_tensor_tensor(out=fl(tab["nb"]), in0=fl(tab["nb"]), scalar=-1.0, in1=fl(tab["beta"]), op0=ALU.mult, op1=ALU.mult),
             reads=["gd_nb", "gd_beta"], writes=["gd_nb"])
        import os
        dbg = bool(os.environ.get("GDNDBG"))
        if dbg:
            for nm in ("gam", "beta", "nb", "ksc", "egl"):
                d_ = nc.dram_tensor("dbg_" + nm, [128, NT, 32], F32, kind="ExternalOutput").ap()
                S.dma("sp", d_, tab[nm][:], reads=["gd_" + nm], writes=["dbg_" + nm])
        S.op("dve", lambda e: e.memset(self.gd_xs[:], 0.0), writes=["gd_xs"])
        S.op("dve", lambda e: e.memset(self.gd_xp[:], 0.0), writes=["gd_xp"])
        S.op("dve", lambda e: e.memset(self.gd_q[:], 0.0), writes=["gd_q"])
        S.op("dve", lambda e: e.memset(self.gd_k[:], 0.0), writes=["gd_k"])
        tabkeys = ["gd_gam", "gd_beta", "gd_nb", "gd_ksc", "gd_egl"]
        for kh in range(8):
            self.gdn_qkv_fm(j, kh * 128, kh * 128, self.gd_xs)
            self.gdn_l2norm_to(self.gd_q, "gd_q", 128 ** -0.5)
            self.gdn_qkv_fm(j, D + kh * 128, D + kh * 128, self.gd_xs)
            self.gdn_l2norm_to(self.gd_k, "gd_k", 1.0)
            self.gdn_to_tok(self.gd_ktok, "gd_ktok")
            for hv in range(2):
                h = kh * 2 + hv
                self.gdn_qkv_fm(j, 2 * D + h * 128, 2 * D + h * 128, self.gd_xs)
                self.gdn_to_tok(self.gd_vtok, "gd_vtok")
                for d in range(2):
                    S.op("dve", lambda e: e.memset(self.gd_S[d][:], 0.0), writes=["gd_S%d" % d])
                    S.op("dve", lambda e: e.memset(self.gd_Sb[d][:], 0.0), writes=["gd_Sb%d" % d])
                probs = [(0, n) for n in range(NT)] + [(1, n) for n in range(NT - 1, -1, -1)]
                ctxs = [dict(h=h, d=d_, n=n_) for (d_, n_) in probs]
                NP = len(ctxs)
                ia = ib = 0
                inflight = []
                doneA = set()
                Bg = None
                while ib < NP:
                    while len(inflight) < 2 and ia < NP and ia < ib + 3:
                        inflight.append((ia, self.gdn_inv_gen(ctxs[ia])))
                        ia += 1
                    if Bg is None and ib in doneA:
                        Bg = self.gdn_scan_gen(ctxs[ib])
                    for item in list(inflight):
                        try:
                            next(item[1])
                        except StopIteration:
                            doneA.add(item[0])
                            inflight.remove(item)
                    if Bg is not None:
                        try:
                            next(Bg)
                        except StopIteration:
                            Bg = None
                            ib += 1
                if dbg and h in (0, 15):
                    d_ = nc.dram_tensor("dbg_o%d" % h, [128, Tp], F32, kind="ExternalOutput").ap()
                    S.dma("sp", d_, self.gd_o[:], reads=["gd_o"], writes=["dbg_o%d" % h])
                    if h == 0:
                        for nm, t_ in (("q", self.gd_q), ("k", self.gd_k)):
                            d_ = nc.dram_tensor("dbg_" + nm, [128, Tp], BF16, kind="ExternalOutput").ap()
                            S.dma("sp", d_, t_[:], reads=["gd_" + nm], writes=["dbg_" + nm])
                        d_ = nc.dram_tensor("dbg_v", [128, NT, 128], BF16, kind="ExternalOutput").ap()
                        S.dma("sp", d_, self.gd_vtok[:], reads=["gd_vtok"], writes=["dbg_v"])
                wz, wzk = self.load_wcols("gdn_w_in", j, 2 * E + h * 128, 128)
                self.hn_ps = self.ps
                for (t0, nt) in tok_blocks(T, 512):
                    ps, pk = self.proj_fm(wz, wzk, 128, t0, nt)
                    sz, szk = self.gd_sz.next()
                    S.op("act", lambda e: e.activation(out=sz[:, :nt], in_=ps[:, :nt], func=AF.Silu), reads=[pk], writes=[szk])
                    self.head_norm_gate(self.gd_o[:, t0:t0 + nt], "gd_o", nt, self.gd_gain[:, 0:1], sz[:, :nt], szk, s, h * 128, t0, 1e-6, 1.0)

    def setup_conformer(self):
        NE = E // 128
        self.cf_ypad = self.sb("cf_ypad", [128, self.T + 30], BF16)
        self.cf_a = self.ring("cf_a", 2, [128, 512], F32)
        self.cf_sg = self.ring("cf_sg", 2, [128, 512], F32)
        self.cf_sz = self.ring("cf_sz", 2, [128, 512], BF16)
        self.cf_cz = self.ring("cf_cz", 2, [128, 512], F32)
        self.cf_diag = self.sb("cf_diag", [128, 31, 128], BF16)
        self.cf_dw = self.sb("cf_dw", [128, 31], F32)
        self.cf_vec = self.sb("cf_vec", [128, 8, NE], F32)
        self.CZ = self.dt("cf_cz_scr", [self.nseq, E, self.T], F32)
        self.SZ = self.dt("cf_sz_scr", [self.nseq, E, self.T], BF16)
        self.cf_czall = self.ring("cf_czall", 1, [128, NE, 512], F32)
        self.cf_szall = self.ring("cf_szall", 1, [128, NE, 512], BF16)
        self.cf_tmpb = self.ring("cf_tmpb", 2, [128, 512], BF16)
        self.cf_stat = self.ring("cf_stat", 2, [128, 3, 512], F32)
        self.cf_n = self.ring("cf_n", 2, [128, 512], F32)
        self.cf_g = self.ring("cf_g", 2, [128, 512], BF16)

    def mixer_conformer(self, li, j, s):
        nc, S, T = self.nc, self.S, self.T
        NE = E // 128
        vec = self.cf_vec
        if s == 0:
            def ldv(slot, ap):
                S.dma("sp", vec[:, slot, :], ap.rearrange("(c p) -> p c", p=128), writes=["cf_vec"],
                      allow_slow_non_contiguous=True)
            ldv(0, self.dram["cf_b_in"][j][0:E])
            ldv(1, self.dram["cf_b_in"][j][E:2 * E])
            ldv(2, self.dram["cf_b_in"][j][2 * E:3 * E])
            ldv(3, self.dram["cf_dw_b"][j])
            ldv(4, self.dram["cf_ln_g"][j])
            ldv(5, self.dram["cf_ln_b"][j])
            S.op("dve", lambda e: e.memset(self.cf_ypad[:], 0.0), writes=["cf_ypad"])
        blocks = tok_blocks(T, 512)
        for c in range(NE):
            S.dma("sp", self.cf_dw[:], self.dram["cf_dw_w"][j][:, c * 128:(c + 1) * 128].rearrange("k p -> p k"),
                  writes=["cf_dw"], allow_slow_non_contiguous=True)
            for k in range(31):
                S.op("pool", lambda e: e.tensor_scalar(out=self.cf_diag[:, k, :], in0=self.ident[:], scalar1=self.cf_dw[:, k:k + 1],
                                                       scalar2=None, op0=ALU.mult),
                     reads=["ident", "cf_dw"], writes=["cf_diag"])
            wa, wak = self.load_wcols("cf_w_in", j, c * 128, 128)
            wg, wgk = self.load_wcols("cf_w_in", j, E + c * 128, 128)
            for (t0, nt) in blocks:
                ps, pk = self.proj_fm(wa, wak, 128, t0, nt)
                a, ak = self.cf_a.next()
                S.op("act", lambda e: e.activation(out=a[:, :nt], in_=ps[:, :nt], func=AF.Identity,
                                                   bias=vec[:, 0, c:c + 1]), reads=[pk, "cf_vec"], writes=[ak])
                ps, pk = self.proj_fm(wg, wgk, 128, t0, nt)
                sg, sgk = self.cf_sg.next()
                S.op("act", lambda e: e.activation(out=sg[:, :nt], in_=ps[:, :nt], func=AF.Sigmoid,
                                                   bias=vec[:, 1, c:c + 1]), reads=[pk, "cf_vec"], writes=[sgk])
                S.op("dve", lambda e: e.tensor_tensor(out=self.cf_ypad[:, 15 + t0:15 + t0 + nt], in0=a[:, :nt], in1=sg[:, :nt], op=ALU.mult),
                     reads=[ak, sgk], writes=["cf_ypad"])
            wz, wzk = self.load_wcols("cf_w_in", j, 2 * E + c * 128, 128)
            for (t0, nt) in blocks:
                ps, pk = self.proj_fm(wz, wzk, 128, t0, nt)
                sz, szk = self.cf_sz.next()
                S.op("act", lambda e: e.activation(out=sz[:, :nt], in_=ps[:, :nt], func=AF.Silu,
                                                   bias=vec[:, 2, c:c + 1]), reads=[pk, "cf_vec"], writes=[szk])
                S.dma("sp", self.SZ[s, c * 128:(c + 1) * 128, t0:t0 + nt], sz[:, :nt], reads=[szk], writes=[("SZ", s)])
            for (t0, nt) in blocks:
                ps, pk = self.ps.next()
                for k in range(31):
                    S.op("pe", lambda e: e.matmul(ps[:, :nt], lhsT=self.cf_diag[:, k, :], rhs=self.cf_ypad[:, t0 + k:t0 + k + nt],
                                                  start=(k == 0), stop=(k == 30)),
                         reads=["cf_diag", "cf_ypad"], writes=[pk])
                cz, czk = self.cf_cz.next()
                S.op("act", lambda e: e.activation(out=cz[:, :nt], in_=ps[:, :nt], func=AF.Identity,
                                                   bias=vec[:, 3, c:c + 1]), reads=[pk, "cf_vec"], writes=[czk])
                S.dma("sp", self.CZ[s, c * 128:(c + 1) * 128, t0:t0 + nt], cz[:, :nt], reads=[czk], writes=[("CZ", s)])
        for (t0, nt) in blocks:
            ca, cak = self.cf_czall.next()
            sa, sak = self.cf_szall.next()
            S.dma("sp", ca[:, :, :nt], self.CZ[s, :, t0:t0 + nt].rearrange("(c p) t -> p c t", p=128),
                  reads=[("CZ", s)], writes=[cak])
            S.dma("sp", sa[:, :, :nt], self.SZ[s, :, t0:t0 + nt].rearrange("(c p) t -> p c t", p=128),
                  reads=[("SZ", s)], writes=[sak])
            p1, p1k = self.ps.next()
            p2, p2k = self.ps.next()
            for c in range(NE):
                tb, tbk = self.cf_tmpb.next()
                S.op("dve", lambda e: e.tensor_copy(out=tb[:, :nt], in_=ca[:, c, :nt]), reads=[cak], writes=[tbk])
                S.op("pe", lambda e: e.matmul(p1[:, :nt], lhsT=self.ones_b[:], rhs=tb[:, :nt], start=(c == 0), stop=(c == NE - 1)),
                     reads=["ones_b", tbk], writes=[p1k])
                tb2, tb2k = self.cf_tmpb.next()
                S.op("act", lambda e: e.activation(out=tb2[:, :nt], in_=ca[:, c, :nt], func=AF.Square), reads=[cak], writes=[tb2k])
                S.op("pe", lambda e: e.matmul(p2[:, :nt], lhsT=self.ones_b[:], rhs=tb2[:, :nt], start=(c == 0), stop=(c == NE - 1)),
                     reads=["ones_b", tb2k], writes=[p2k])
            st, stk = self.cf_stat.next()
            S.op("dve", lambda e: e.tensor_scalar(out=st[:, 0, :nt], in0=p1[:, :nt], scalar1=1.0 / E, scalar2=None, op0=ALU.mult),
                 reads=[p1k], writes=[stk])
            S.op("dve", lambda e: e.tensor_tensor(out=st[:, 1, :nt], in0=st[:, 0, :nt], in1=st[:, 0, :nt], op=ALU.mult),
                 reads=[stk], writes=[stk])
            S.op("dve", lambda e: e.scalar_tensor_tensor(out=st[:, 1, :nt], in0=p2[:, :nt], scalar=1.0 / E, in1=st[:, 1, :nt],
                                                         op0=ALU.mult, op1=ALU.subtract), reads=[p2k, stk], writes=[stk])
            S.op("act", lambda e: e.activation(out=st[:, 2, :nt], in_=st[:, 1, :nt], func=AF.Ln, bias=1e-5), reads=[stk], writes=[stk])
            S.op("act", lambda e: e.activation(out=st[:, 2, :nt], in_=st[:, 2, :nt], func=AF.Exp, scale=-0.5), reads=[stk], writes=[stk])
            for c in range(NE):
                n, nk = self.cf_n.next()
                S.op("dve", lambda e: e.tensor_tensor(out=n[:, :nt], in0=ca[:, c, :nt], in1=st[:, 0, :nt], op=ALU.subtract),
                     reads=[cak, stk], writes=[nk])
                S.op("dve", lambda e: e.tensor_tensor(out=n[:, :nt], in0=n[:, :nt], in1=st[:, 2, :nt], op=ALU.mult),
                     reads=[nk, stk], writes=[nk])
                S.op("act", lambda e: e.activation(out=n[:, :nt], in_=n[:, :nt], func=AF.Silu, scale=vec[:, 4, c:c + 1],
                                                   bias=vec[:, 5, c:c + 1]), reads=[nk, "cf_vec"], writes=[nk])
                g, gk = self.cf_g.next()
                S.op("dve", lambda e: e.tensor_tensor(out=g[:, :nt], in0=n[:, :nt], in1=sa[:, c, :nt], op=ALU.mult),
                     reads=[nk, sak], writes=[gk])
                S.dma("sp", self.G[s, c * 128:(c + 1) * 128, t0:t0 + nt], g[:, :nt], reads=[gk], writes=[("G", s)])

    def build(self):
        self.setup_common()
        self.init_h()
        nl = len(self.layers)
        WOUT = {0: ("hy_w_out", None), 1: ("da_w_out", None), 2: ("gdn_w_out", None), 3: ("cf_w_out", "cf_b_out")}
        for li, m in enumerate(self.layers):
            j = 0
            if m == 0:
                self.hy_dims()
                self.open_scope()
                self.hyena_filters(j)
                self.close_scope()
            self.open_scope()
            self.setup_am()
            if m == 0:
                self.setup_hyena_proj()
            if m == 1:
                self.setup_attention()
            elif m == 2:
                self.setup_gdn()
            elif m == 3:
                self.setup_conformer()
            for s in range(self.nseq):
                self.phase_a(li, s)
                if m == 0:
                    self.hyena_proj(j, s)
                elif m == 1:
                    self.mixer_attention(li, j, s, self.layer_index(li))
                elif m == 2:
                    self.mixer_gdn(li, j, s)
                elif m == 3:
                    self.mixer_conformer(li, j, s)
            self.close_scope()
            if m == 0:
                self.open_scope()
                self.hyena_spectral(j)
                self.close_scope()
            self.open_scope()
            self.setup_z()
            for s in range(self.nseq):
                self.phase_z(li, s, WOUT[m][0], j, bias_name=WOUT[m][1], final=(li == nl - 1))
            self.close_scope()
        self.S.finish("sp")
        self.stack.close()
        return self.nc

    def layer_index(self, li):
        return 1 if self.layers != [0, 1, 2, 3] else li


def rope_consts(T):
    inv = (10000.0 ** (-np.arange(0, 64, 2, dtype=np.float32) / np.float32(64))).astype(np.float32)
    ang = (np.arange(T, dtype=np.float32)[:, None] * inv[None, :]).astype(np.float32)
    cos = np.cos(ang).astype(np.float32)
    sin = np.sin(ang).astype(np.float32)
    idx = np.arange(128) % 32
    C = np.ascontiguousarray(cos[:, idx].T)
    Sg = np.ascontiguousarray(sin[:, idx].T)
    P = np.zeros((128, 128), np.float32)
    for p in range(128):
        r = p % 64
        if r < 32:
            P[p, p + 32] = -1.0
        else:
            P[p, p - 32] = 1.0
    return C, Sg, np.ascontiguousarray(P.T)


def gdn_consts():
    i = np.arange(128)[:, None]
    j = np.arange(128)[None, :]
    c = np.zeros((10, 128, 128), np.float32)
    c[0] = (i <= j)
    c[1] = (i >= j)
    c[2][127, :] = 1.0
    c[3][0, :] = 1.0
    big = 1.0e4
    c[4] = np.where(i > j, 0.0, big)
    c[5] = np.where(i < j, 0.0, big)
    c[6] = np.where(j >= i, 0.0, -big)
    c[7] = np.where(j <= i, 0.0, -big)
    c[8] = np.where(i > j, -1.0, 0.0)
    c[9] = np.where(i < j, -1.0, 0.0)
    return np.ascontiguousarray(c.transpose(1, 0, 2))


def hyena_consts(T):
    import ml_dtypes
    N = 2 * T
    NF = T + 1
    NFC = (NF + 127) // 128
    t = np.arange(T, dtype=np.int64)
    k = np.arange(NFC * 128, dtype=np.int64)
    valid = (k < NF)
    ang = 2.0 * np.pi * ((t[:, None] * k[None, :]) % N).astype(np.float64) / N
    Fc = np.cos(ang) * valid[None, :]
    Fs = -np.sin(ang) * valid[None, :]
    F = np.concatenate([Fc, Fs], axis=1)
    ck = np.where((k == 0) | (k == N // 2), 1.0, 2.0) * valid / N
    Gc = (np.cos(ang) * ck[None, :]).T
    Gs = (-np.sin(ang) * ck[None, :]).T
    G = np.concatenate([Gc, Gs], axis=0)
    tl = np.linspace(0.0, 1.0, T, dtype=np.float32)[:, None]
    w = (2.0 * np.float32(math.pi) * np.arange(T, dtype=np.float32)[:, None] / np.float32(T)).astype(np.float32)
    bands = np.linspace(1e-4, 15, 16, dtype=np.float32)[None, :]
    z = np.concatenate([tl, np.cos(bands * w), -np.sin(bands * w)], axis=-1).astype(np.float32)
    max_decay = math.log(1e-2) / 0.3
    min_decay = math.log(1e-2) / 1.5
    deltas = np.abs(np.linspace(min_decay, max_decay, E, dtype=np.float32)).astype(np.float32)
    NT = (T + 127) // 128
    tlp = np.zeros((NT * 128,), np.float32)
    tlp[:T] = -tl[:, 0]
    return {"c_dft_f": np.ascontiguousarray(F).astype(ml_dtypes.bfloat16),
            "c_dft_g": np.ascontiguousarray(G).astype(ml_dtypes.bfloat16),
            "c_hy_z": np.ascontiguousarray(z.T), "c_hy_delta": deltas,
            "c_hy_tl": np.ascontiguousarray(tlp.reshape(NT, 128).T)}


def const_inputs(T):
    C, Sg, PT = rope_consts(T)
    return {"c_ident": np.eye(128, dtype=np.float32), "c_rope_cos": C, "c_rope_sin": Sg, "c_rope_perm": PT,
            "c_gdn": gdn_consts(), **hyena_consts(T)}


def build_program(T, nseq, layers, inputs):
    shapes = {k: v.shape for k, v in inputs.items() if k != "x"}
    b = Builder(T, nseq, layers, shapes)
    nc = b.build()
    return nc, b


def kernel(**inputs):
    ncores = 8
    x = np.ascontiguousarray(inputs["x"], dtype=np.float32)
    B, L, _ = x.shape
    nseq = B // ncores
    T = L + NMETA
    consts = const_inputs(T)
    params = {k: np.ascontiguousarray(v, dtype=np.float32) for k, v in inputs.items() if k != "x"}
    params.update(consts)
    nc, b = build_program(T, nseq, [0, 1, 2, 3], dict(params, x=x))
    in_maps = []
    for c in range(ncores):
        m = dict(params)
        m["x"] = x[c * nseq:(c + 1) * nseq]
        in_maps.append(m)
    res = run_bass_kernel_spmd(nc, in_maps, core_ids=list(range(ncores)))
    return np.concatenate([r["out"] for r in res.results], axis=0)
```

```python
import math
from contextlib import ExitStack
import numpy as np
import concourse.bass as bass
import concourse.mybir as mybir
from concourse.bass_utils import run_bass_kernel_spmd

F32 = mybir.dt.float32
BF16 = mybir.dt.bfloat16
AF = mybir.ActivationFunctionType
ALU = mybir.AluOpType
AX = mybir.AxisListType

D = 1024
E = 2048
NMETA = 16
KC = D // 128


class _Eng:
    def __init__(self, name, eng, sem):
        self.name = name
        self.eng = eng
        self.sem = sem
        self.count = 0
        self.waited = {}


class Sched:
    def __init__(self, nc, stack, n_dma_sems=24):
        self.nc = nc
        self.E = {}
        for name, eng in (("pe", nc.tensor), ("act", nc.scalar), ("dve", nc.vector),
                          ("pool", nc.gpsimd), ("sp", nc.sync)):
            sem = stack.enter_context(nc.semaphore("s_" + name))
            self.E[name] = _Eng(name, eng, sem)
        self.dma_sems = [stack.enter_context(nc.semaphore("s_dma%d" % i)) for i in range(n_dma_sems)]
        self.dma_issued = [0] * n_dma_sems
        self.dma_rr = 0
        self.dma_rr2 = 0
        self.last_write = {}
        self.readers = {}
        self.ninstr = 0
        self.strict = True

    def _wait(self, e, ev):
        if ev is None:
            return
        if ev[0] == "e":
            if ev[1] == e.name and not (self.strict and e.name in ("act", "dve", "pool")):
                return
            src = self.E[ev[1]]
            key = ("e", ev[1])
            val = ev[2]
            sem = src.sem
        else:
            idx = ev[1]
            key = ("d", idx)
            val = 16 * self.dma_issued[idx]
            sem = self.dma_sems[idx]
        if e.waited.get(key, 0) >= val:
            return
        e.waited[key] = val
        e.eng.wait_ge(sem, val)
        self.ninstr += 1

    def _deps(self, e, reads, writes):
        for k in reads:
            self._wait(e, self.last_write.get(k))
            if isinstance(k, str) and k[0] == "P":
                for ev in self.readers.get(k, {}).values():
                    if ev[0] == "e" and ev[1] != e.name:
                        self._wait(e, ev)
        for k in writes:
            self._wait(e, self.last_write.get(k))
            for ev in self.readers.get(k, {}).values():
                self._wait(e, ev)

    def _record(self, ev, reads, writes):
        for k in reads:
            d = self.readers.setdefault(k, {})
            d[ev[:2]] = ev
        for k in writes:
            self.last_write[k] = ev
            self.readers[k] = {}

    def op(self, engname, emit, reads=(), writes=()):
        e = self.E[engname]
        self._deps(e, reads, writes)
        ins = emit(e.eng)
        e.count += 1
        ins.then_inc(e.sem, 1)
        self.ninstr += 1
        self._record(("e", engname, e.count), reads, writes)
        return ins

    def dma(self, qname, out, in_, reads=(), writes=(), semgroup=None, **kw):
        e = self.E[qname]
        self._deps(e, reads, writes)
        if qname == "sp":
            idx = self.dma_rr % 16
            self.dma_rr += 1
        else:
            idx = 16 + self.dma_rr2 % 8
            self.dma_rr2 += 1
        ins = e.eng.dma_start(out=out, in_=in_, **kw)
        ins.then_inc(self.dma_sems[idx], 16)
        self.dma_issued[idx] += 1
        self.ninstr += 1
        self._record(("d", idx), reads, writes)
        return ins

    def barrier(self):
        for name in self.E:
            self.finish(name)

    def finish(self, engname="sp"):
        e = self.E[engname]
        for idx in range(len(self.dma_sems)):
            if self.dma_issued[idx]:
                self._wait(e, ("d", idx))
        for name, src in self.E.items():
            if name != engname and src.count:
                self._wait(e, ("e", name, src.count))


class Ring:
    def __init__(self, name, bufs, keys=None):
        self.name = name
        self.bufs = bufs
        self.keys = keys if keys is not None else ["%s#%d" % (name, j) for j in range(len(bufs))]
        self.i = 0

    def next(self):
        j = self.i % len(self.bufs)
        self.i += 1
        return self.bufs[j], self.keys[j]


def tok_blocks(T, bs):
    out = []
    t = 0
    while t < T:
        n = min(bs, T - t)
        out.append((t, n))
        t += n
    return out


class Builder:
    def __init__(self, T, nseq, layers, params_shapes):
        self.T = T
        self.nseq = nseq
        self.layers = layers
        self.nc = bass.Bass("TRN2", target_bir_lowering=False)
        self.stack = ExitStack()
        nc = self.nc
        self.S = Sched(nc, self.stack)
        self.dram = {}
        for name, shp in params_shapes.items():
            dt_ = BF16 if name.startswith("c_dft") else F32
            self.dram[name] = nc.dram_tensor(name, list(shp), dt_, kind="ExternalInput").ap()
        self.x = nc.dram_tensor("x", [nseq, T - NMETA, D], F32, kind="ExternalInput").ap()
        self.out = nc.dram_tensor("out", [nseq, T - NMETA, D], F32, kind="ExternalOutput").ap()
        self.h = nc.dram_tensor("h_scr", [nseq, T, D], F32).ap()
        self.G = nc.dram_tensor("g_scr", [nseq, E, T], BF16).ap()
        self.uid = 0
        self.cur = self.stack

    def sb(self, name, shape, dtype=F32):
        self.uid += 1
        return self.cur.enter_context(self.nc.sbuf_tensor("%s_%d" % (name, self.uid), list(shape), dtype))

    def open_scope(self):
        self.cur = ExitStack()

    def close_scope(self):
        self.S.barrier()
        self.cur.close()
        self.cur = self.stack

    def ring(self, name, n, shape, dtype=F32):
        return Ring(name, [self.sb("%s_%d" % (name, i), shape, dtype) for i in range(n)])

    def dt(self, name, shape, dtype=F32):
        return self.nc.dram_tensor(name, list(shape), dtype).ap()

    def setup_common(self):
        nc, S = self.nc, self.S
        pb = [self.stack.enter_context(nc.psum_tensor("ps%d" % i, [128, 512], F32)) for i in range(6)]
        self.PB = pb
        self.PBK = ["P%d" % i for i in range(6)]
        self.PW = self.stack.enter_context(nc.psum_tensor("pw", [128, 1024], F32))
        self.ps = Ring("ps", pb, self.PBK)
        self.ident = self.sb("ident", [128, 128], F32)
        self.identb = self.sb("identb", [128, 128], BF16)
        self.ones_b = self.sb("ones_b", [128, 128], BF16)
        self.ones_f = self.sb("ones_f", [128, 128], F32)
        S.dma("sp", self.ident[:], self.dram["c_ident"], writes=["ident"])
        S.op("dve", lambda e: e.tensor_copy(out=self.identb[:], in_=self.ident[:]), reads=["ident"], writes=["identb"])
        S.op("dve", lambda e: e.memset(self.ones_b[:], 1.0), writes=["ones_b"])
        S.op("dve", lambda e: e.memset(self.ones_f[:], 1.0), writes=["ones_f"])
        self.ht = self.ring("ht", 1, [128, D], F32)
        self.junk = self.ring("junk", 1, [128, D], F32)
        self.col = self.ring("col", 4, [128, 1], F32)

    def setup_am(self):
        self.yT = self.sb("yT", [128, KC, self.T], BF16)
        self.wring = self.ring("wf", 2, [128, KC, 128], F32)
        self.wbring = self.ring("wb", 2, [128, KC, 128], BF16)
        self.gpre = self.sb("gpre", [128, KC], F32)

    def setup_z(self):
        self.gpost = self.sb("gpost", [128, D], F32)
        self.bout = self.sb("bout", [128, D], F32)
        self.wout = self.sb("wout", [128, E // 128, D], BF16)
        self.woutf = self.ring("woutf", 2, [128, D], F32)
        self.gt = self.ring("gt", 2, [128, E // 128, 128], BF16)
        self.ot = self.ring("ot", 2, [128, D], F32)

    def init_h(self):
        S = self.S
        for s in range(self.nseq):
            S.dma("sp", self.h[s, 0:NMETA, :], self.dram["meta"], writes=[("h", s)])
            S.dma("sp", self.h[s, NMETA:, :], self.x[s], writes=[("h", s)])

    def phase_a(self, li, s):
        nc, S, T = self.nc, self.S, self.T
        S.dma("sp", self.gpre[:], self.dram["norm_pre"][li].rearrange("(kc p) -> p kc", p=128),
              reads=[], writes=["gpre"], allow_slow_non_contiguous=True)
        pw = self.PW
        pwk = ["PWa", "PWb"]
        for (t0, nt) in tok_blocks(T, 128):
            ht, hk = self.ht.next()
            S.dma("sp", ht[:nt, :], self.h[s, t0:t0 + nt, :], reads=[("h", s)], writes=[hk])
            jk, jkk = self.junk.next()
            cs, ck = self.col.next()
            S.op("dve", lambda e: e.memset(cs[:nt, :], 0.0), writes=[ck])
            S.op("act", lambda e: e.activation(out=jk[:nt, :], in_=ht[:nt, :], func=AF.Square, accum_out=cs[:nt, :]),
                 reads=[hk, ck], writes=[jkk, ck])
            S.op("act", lambda e: e.activation(out=cs[:nt, :], in_=cs[:nt, :], func=AF.Ln, scale=1.0 / D, bias=1e-6),
                 reads=[ck], writes=[ck])
            S.op("act", lambda e: e.activation(out=cs[:nt, :], in_=cs[:nt, :], func=AF.Exp, scale=-0.5),
                 reads=[ck], writes=[ck])
            S.op("act", lambda e: e.activation(out=jk[:nt, :], in_=ht[:nt, :], func=AF.Identity, scale=cs[:nt, :]),
                 reads=[hk, ck], writes=[jkk])
            for kc in range(KC):
                S.op("pe", lambda e: e.transpose(out=pw[:, kc * 128:kc * 128 + nt], in_=jk[:nt, kc * 128:(kc + 1) * 128],
                                                 identity=self.ident[:nt, :nt]),
                     reads=[jkk, "ident"], writes=pwk)
            pv = pw[:].rearrange("p (kc t) -> p kc t", kc=KC)[:, :, :nt]
            S.op("dve", lambda e: e.tensor_tensor(out=self.yT[:, :, t0:t0 + nt], in0=pv,
                                                  in1=self.gpre[:].unsqueeze(2).to_broadcast([128, KC, nt]), op=ALU.mult),
                 reads=pwk + ["gpre"], writes=["yT"])

    def load_wcols(self, wname, li_idx, col0, width):
        S = self.S
        wf, wfk = self.wring.next()
        wb, wbk = self.wbring.next()
        src = self.dram[wname][li_idx][:, col0:col0 + width].rearrange("(kc p) w -> p kc w", p=128)
        S.dma("pool", wf[:, :, :width], src, writes=[wfk])
        S.op("pool", lambda e: e.tensor_copy(out=wb[:, :, :width], in_=wf[:, :, :width]), reads=[wfk], writes=[wbk])
        return wb, wbk

    def proj_fm(self, wb, wbk, width, t0, nt, ring=None):
        S = self.S
        ps, pk = (ring or self.ps).next()
        for kc in range(KC):
            S.op("pe", lambda e: e.matmul(ps[:width, :nt], lhsT=wb[:, kc, :width], rhs=self.yT[:, kc, t0:t0 + nt],
                                          start=(kc == 0), stop=(kc == KC - 1)),
                 reads=[wbk, "yT"], writes=[pk])
        return ps, pk

    def phase_z(self, li, s, wout_name, j, bias_name=None, final=False):
        nc, S, T = self.nc, self.S, self.T
        pw = self.PW
        pwk = ["PWa", "PWb"]
        if s == 0:
            S.dma("sp", self.gpost[:], self.dram["norm_post"][li].partition_broadcast(128), writes=["gpost"])
            if bias_name is not None:
                S.dma("sp", self.bout[:], self.dram[bias_name][j].partition_broadcast(128), writes=["bout"])
            for ec in range(E // 128):
                wf, wfk = self.woutf.next()
                S.dma("pool", wf[:], self.dram[wout_name][j][ec * 128:(ec + 1) * 128, :], writes=[wfk])
                S.op("pool", lambda e: e.tensor_copy(out=self.wout[:, ec, :], in_=wf[:]), reads=[wfk], writes=["wout"])
        for (t0, nt) in tok_blocks(T, 128):
            gt, gk = self.gt.next()
            S.dma("sp", gt[:, :, :nt], self.G[s, :, t0:t0 + nt].rearrange("(ec p) t -> p ec t", p=128),
                  reads=[("G", s)], writes=[gk])
            ht, hk = self.ht.next()
            S.dma("sp", ht[:nt, :], self.h[s, t0:t0 + nt, :], reads=[("h", s)], writes=[hk])
            for half in range(2):
                for ec in range(E // 128):
                    S.op("pe", lambda e: e.matmul(pw[:nt, half * 512:(half + 1) * 512], lhsT=gt[:, ec, :nt],
                                                  rhs=self.wout[:, ec, half * 512:(half + 1) * 512],
                                                  start=(ec == 0), stop=(ec == E // 128 - 1)),
                         reads=[gk, "wout"], writes=pwk)
            ot, ok = self.ot.next()
            if bias_name is not None:
                S.op("dve", lambda e: e.tensor_tensor(out=ot[:nt, :], in0=pw[:nt, :], in1=self.bout[:nt, :], op=ALU.add),
                     reads=pwk + ["bout"], writes=[ok])
            else:
                S.op("dve", lambda e: e.tensor_copy(out=ot[:nt, :], in_=pw[:nt, :]), reads=pwk, writes=[ok])
            jk, jkk = self.junk.next()
            cs, ck = self.col.next()
            S.op("dve", lambda e: e.memset(cs[:nt, :], 0.0), writes=[ck])
            S.op("act", lambda e: e.activation(out=jk[:nt, :], in_=ot[:nt, :], func=AF.Square, accum_out=cs[:nt, :]),
                 reads=[ok, ck], writes=[jkk, ck])
            S.op("act", lambda e: e.activation(out=cs[:nt, :], in_=cs[:nt, :], func=AF.Ln, scale=1.0 / D, bias=1e-6),
                 reads=[ck], writes=[ck])
            S.op("act", lambda e: e.activation(out=cs[:nt, :], in_=cs[:nt, :], func=AF.Exp, scale=-0.5),
                 reads=[ck], writes=[ck])
            S.op("dve", lambda e: e.scalar_tensor_tensor(out=ot[:nt, :], in0=ot[:nt, :], scalar=cs[:nt, :],
                                                         in1=self.gpost[:nt, :], op0=ALU.mult, op1=ALU.mult),
                 reads=[ok, ck, "gpost"], writes=[ok])
            S.op("dve", lambda e: e.tensor_tensor(out=ot[:nt, :], in0=ot[:nt, :], in1=ht[:nt, :], op=ALU.add),
                 reads=[ok, hk], writes=[ok])
            if not final:
                S.dma("sp", self.h[s, t0:t0 + nt, :], ot[:nt, :], reads=[ok], writes=[("h", s)])
            else:
                if t0 == 0:
                    S.dma("sp", self.out[s, 0:nt - NMETA, :], ot[NMETA:nt, :], reads=[ok], writes=[("out", s)])
                else:
                    S.dma("sp", self.out[s, t0 - NMETA:t0 - NMETA + nt, :], ot[:nt, :], reads=[ok], writes=[("out", s)])

    def head_norm_gate(self, o, okey, nt, gain_col, gate, gatekey, s, row0, t0, eps, extra_scale):
        S = self.S
        sq, sqk = self.hn_sq.next()
        S.op("act", lambda e: e.activation(out=sq[:, :nt], in_=o[:, :nt], func=AF.Square), reads=[okey], writes=[sqk])
        ps, pk = self.hn_ps.next()
        S.op("pe", lambda e: e.matmul(ps[:, :nt], lhsT=self.ones_b[:], rhs=sq[:, :nt], start=True, stop=True),
             reads=["ones_b", sqk], writes=[pk])
        r, rk = self.hn_r.next()
        S.op("act", lambda e: e.activation(out=r[:, :nt], in_=ps[:, :nt], func=AF.Ln, scale=1.0 / 128, bias=eps), reads=[pk], writes=[rk])
        S.op("act", lambda e: e.activation(out=r[:, :nt], in_=r[:, :nt], func=AF.Exp, scale=-0.5), reads=[rk], writes=[rk])
        S.op("dve", lambda e: e.scalar_tensor_tensor(out=r[:, :nt], in0=o[:, :nt], scalar=gain_col, in1=r[:, :nt],
                                                     op0=ALU.mult, op1=ALU.mult), reads=[okey, rk, "hn_gain"], writes=[rk])
        g, gk = self.hn_g.next()
        S.op("dve", lambda e: e.scalar_tensor_tensor(out=g[:, :nt], in0=r[:, :nt], scalar=float(extra_scale), in1=gate,
                                                     op0=ALU.mult, op1=ALU.mult), reads=[rk, gatekey], writes=[gk])
        S.dma("sp", self.G[s, row0:row0 + 128, t0:t0 + nt], g[:, :nt], reads=[gk], writes=[("G", s)])

    def setup_head_norm(self):
        self.hn_sq = self.ring("hn_sq", 2, [128, 512], BF16)
        self.hn_r = self.ring("hn_r", 2, [128, 512], F32)
        self.hn_g = self.ring("hn_g", 2, [128, 512], BF16)

    def setup_attention(self):
        T = self.T
        self.setup_head_norm()
        self.da_cos = self.sb("da_cos", [128, T], F32)
        self.da_sin = self.sb("da_sin", [128, T], F32)
        self.da_perm = self.sb("da_perm", [128, 128], BF16)
        self.da_permf = self.sb("da_permf", [128, 128], F32)
        self.da_q = self.sb("da_q", [128, T], BF16)
        self.da_k = self.sb("da_k", [128, T], BF16)
        self.da_v = self.sb("da_v", [128, (T + 127) // 128, 128], BF16)
        self.da_sz = self.sb("da_sz", [128, T], BF16)
        self.da_xb = self.ring("da_xb", 2, [128, 512], BF16)
        self.da_t = self.ring("da_t", 6, [128, 512], F32)
        self.da_p = self.ring("da_p", 4, [128, 512], BF16)
        self.da_lv = self.sb("da_lv", [128, 4, 64], F32)
        self.da_lam = self.sb("da_lam", [128, 4], F32)
        self.da_gain = self.sb("da_gain", [128, 1], F32)
        self.da_o = self.ring("da_o", 2, [128, 512], F32)
        self.da_den = [self.sb("da_den%d" % i, [128, 512], F32) for i in range(2)]

    def mixer_attention(self, li, j, s, layer_idx):
        nc, S, T = self.nc, self.S, self.T
        lam_init = 0.8 - 0.6 * math.exp(-0.3 * layer_idx)
        if s == 0:
            S.dma("sp", self.da_cos[:], self.dram["c_rope_cos"], writes=["da_cos"])
            S.dma("sp", self.da_sin[:], self.dram["c_rope_sin"], writes=["da_sin"])
            S.dma("sp", self.da_permf[:], self.dram["c_rope_perm"], writes=["da_permf"])
            S.op("dve", lambda e: e.tensor_copy(out=self.da_perm[:], in_=self.da_permf[:]), reads=["da_permf"], writes=["da_perm"])
            S.dma("sp", self.da_gain[:], self.dram["da_subln"][j].rearrange("(p o) -> p o", o=1), writes=["hn_gain"])
            S.dma("sp", self.da_lv[:].rearrange("p a b -> p (a b)"),
                  self.dram["da_lambda"][j].rearrange("a b -> (a b)").partition_broadcast(128), writes=["da_lv"])
            lam = self.da_lam
            S.op("dve", lambda e: e.tensor_tensor(out=self.da_lv[:, 0, :], in0=self.da_lv[:, 0, :], in1=self.da_lv[:, 1, :], op=ALU.mult),
                 reads=["da_lv"], writes=["da_lv"])
            S.op("dve", lambda e: e.tensor_tensor(out=self.da_lv[:, 2, :], in0=self.da_lv[:, 2, :], in1=self.da_lv[:, 3, :], op=ALU.mult),
                 reads=["da_lv"], writes=["da_lv"])
            S.op("dve", lambda e: e.tensor_reduce(out=lam[:, 0:1], in_=self.da_lv[:, 0, :], axis=AX.X, op=ALU.add), reads=["da_lv"], writes=["da_lam"])
            S.op("dve", lambda e: e.tensor_reduce(out=lam[:, 1:2], in_=self.da_lv[:, 2, :], axis=AX.X, op=ALU.add), reads=["da_lv"], writes=["da_lam"])
            S.op("act", lambda e: e.activation(out=lam[:, 0:2], in_=lam[:, 0:2], func=AF.Exp), reads=["da_lam"], writes=["da_lam"])
            S.op("dve", lambda e: e.tensor_tensor(out=lam[:, 2:3], in0=lam[:, 1:2], in1=lam[:, 0:1], op=ALU.subtract), reads=["da_lam"], writes=["da_lam"])
            S.op("dve", lambda e: e.tensor_scalar(out=lam[:, 2:3], in0=lam[:, 2:3], scalar1=-lam_init, scalar2=None, op0=ALU.add),
                 reads=["da_lam"], writes=["da_lam"])
        blocks = tok_blocks(T, 512)
        tiles = tok_blocks(T, 128)
        acc = self.PB[0:4]
        acck = self.PBK[0:4]
        sring = Ring("sc", [self.PB[4], self.PB[5], self.PW[:, 0:512], self.PW[:, 512:1024]], ["P4", "P5", "PWa", "PWb"])
        self.hn_ps = Ring("hnps", [self.PB[4], self.PB[5]], ["P4", "P5"])
        import os
        cut = int(os.environ.get("ATTCUT", "9"))
        if os.environ.get("ATTRING"):
            sring = Ring("sc", [self.PB[4], self.PB[5]], ["P4", "P5"])
        for hd in range(16):
            if cut <= 0:
                continue
            for which, dst, dkey, col0 in (("q", self.da_q, "da_q", hd * 128), ("k", self.da_k, "da_k", E + hd * 128)):
                wb, wbk = self.load_wcols("da_w_in", j, col0, 128)
                for (t0, nt) in blocks:
                    ps, pk = self.proj_fm(wb, wbk, 128, t0, nt, ring=sring)
                    xf, xfk = self.da_t.next()
                    S.op("act", lambda e: e.activation(out=xf[:, :nt], in_=ps[:, :nt], func=AF.Identity), reads=[pk], writes=[xfk])
                    xb, xbk = self.da_xb.next()
                    S.op("pool", lambda e: e.tensor_copy(out=xb[:, :nt], in_=xf[:, :nt]), reads=[xfk], writes=[xbk])
                    pr, prk = sring.next()
                    S.op("pe", lambda e: e.matmul(pr[:, :nt], lhsT=self.da_perm[:], rhs=xb[:, :nt], start=True, stop=True),
                         reads=["da_perm", xbk], writes=[prk])
                    t1, t1k = self.da_t.next()
                    t2, t2k = self.da_t.next()
                    S.op("dve", lambda e: e.tensor_tensor(out=t1[:, :nt], in0=xf[:, :nt], in1=self.da_cos[:, t0:t0 + nt], op=ALU.mult),
                         reads=[xfk, "da_cos"], writes=[t1k])
                    S.op("dve", lambda e: e.tensor_tensor(out=t2[:, :nt], in0=pr[:, :nt], in1=self.da_sin[:, t0:t0 + nt], op=ALU.mult),
                         reads=[prk, "da_sin"], writes=[t2k])
                    S.op("dve", lambda e: e.tensor_tensor(out=dst[:, t0:t0 + nt], in0=t1[:, :nt], in1=t2[:, :nt], op=ALU.add),
                         reads=[t1k, t2k], writes=[dkey])
            if cut <= 1:
                continue
            wb, wbk = self.load_wcols("da_w_in", j, 2 * E + hd * 128, 128)
            for ti, (t0, nt) in enumerate(tiles):
                ps, pk = sring.next()
                for kc in range(KC):
                    S.op("pe", lambda e: e.matmul(ps[:nt, :128], lhsT=self.yT[:, kc, t0:t0 + nt], rhs=wb[:, kc, :],
                                                  start=(kc == 0), stop=(kc == KC - 1)), reads=[wbk, "yT"], writes=[pk])
                S.op("act", lambda e: e.activation(out=self.da_v[:nt, ti, :], in_=ps[:nt, :128], func=AF.Identity),
                     reads=[pk], writes=["da_v"])
            wb, wbk = self.load_wcols("da_w_in", j, 3 * E + hd * 128, 128)
            for (t0, nt) in blocks:
                ps, pk = self.proj_fm(wb, wbk, 128, t0, nt, ring=sring)
                S.op("act", lambda e: e.activation(out=self.da_sz[:, t0:t0 + nt], in_=ps[:, :nt], func=AF.Silu),
                     reads=[pk], writes=["da_sz"])
            if cut <= 2:
                continue
            for (q0, nq) in blocks:
                steps = [(ti, k0, nk, m) for ti, (k0, nk) in enumerate(tiles) for m in range(2)]

                def score(st):
                    ti, k0, nk, m = st
                    ps, pk = sring.next()
                    S.op("pe", lambda e: e.matmul(ps[:nk, :nq], lhsT=self.da_k[m * 64:(m + 1) * 64, k0:k0 + nk],
                                                  rhs=self.da_q[m * 64:(m + 1) * 64, q0:q0 + nq], start=True, stop=True),
                         reads=["da_k", "da_q"], writes=[pk])
                    return ps, pk
                pend = score(steps[0])
                for i, st in enumerate(steps):
                    ti, k0, nk, m = st
                    ps, pk = pend
                    if i + 1 < len(steps):
                        pend = score(steps[i + 1])
                    p, pkk = self.da_p.next()
                    S.op("act", lambda e: e.activation(out=p[:nk, :nq], in_=ps[:nk, :nq], func=AF.Exp, scale=0.125),
                         reads=[pk], writes=[pkk])
                    first = (ti == 0)
                    last = (ti == len(tiles) - 1)
                    S.op("pe", lambda e: e.matmul(acc[m][:, :nq], lhsT=self.da_v[:nk, ti, :], rhs=p[:nk, :nq], start=first, stop=last),
                         reads=["da_v", pkk], writes=[acck[m]])
                    dacc, dacck = self.da_den[m], "da_den%d" % m
                    deng = "dve" if m == 0 else "pool"
                    if first:
                        S.op(deng, lambda e: e.tensor_copy(out=dacc[:nk, :nq], in_=p[:nk, :nq]), reads=[pkk], writes=[dacck])
                    else:
                        S.op(deng, lambda e: e.tensor_tensor(out=dacc[:nk, :nq], in0=dacc[:nk, :nq], in1=p[:nk, :nq], op=ALU.add),
                             reads=[pkk, dacck], writes=[dacck])
                    if last:
                        S.op("pe", lambda e: e.matmul(acc[2 + m][:, :nq], lhsT=self.ones_f[:, :], rhs=dacc[:, :nq], start=True, stop=True),
                             reads=["ones_f", dacck], writes=[acck[2 + m]])
                if cut <= 3:
                    continue
                r0, r0k = self.da_t.next()
                r1, r1k = self.da_t.next()
                S.op("dve", lambda e: e.reciprocal(out=r0[:, :nq], in_=acc[2][:, :nq]), reads=[acck[2]], writes=[r0k])
                S.op("dve", lambda e: e.reciprocal(out=r1[:, :nq], in_=acc[3][:, :nq]), reads=[acck[3]], writes=[r1k])
                S.op("dve", lambda e: e.tensor_tensor(out=r0[:, :nq], in0=acc[0][:, :nq], in1=r0[:, :nq], op=ALU.mult),
                     reads=[acck[0], r0k], writes=[r0k])
                S.op("dve", lambda e: e.tensor_tensor(out=r1[:, :nq], in0=acc[1][:, :nq], in1=r1[:, :nq], op=ALU.mult),
                     reads=[acck[1], r1k], writes=[r1k])
                o, ok = self.da_o.next()
                S.op("dve", lambda e: e.scalar_tensor_tensor(out=o[:, :nq], in0=r1[:, :nq], scalar=self.da_lam[:, 2:3], in1=r0[:, :nq],
                                                             op0=ALU.mult, op1=ALU.add), reads=[r0k, r1k, "da_lam"], writes=[ok])
                self.head_norm_gate(o, ok, nq, self.da_gain[:, 0:1], self.da_sz[:, q0:q0 + nq], "da_sz", s, hd * 128, q0,
                                    1e-5, 1.0 - lam_init)

    def hy_dims(self):
        T = self.T
        self.NT = (T + 127) // 128
        self.NFC = (T + 1 + 127) // 128
        self.NK = 2 * self.NFC

    def sin_reduce(self, x, xk, kf, kfk, ki, kik, rows, n):
        S = self.S
        xx, k_, i_ = x[:rows, :n], kf[:rows, :n], ki[:rows, :n]
        S.op("dve", lambda e: e.tensor_scalar(out=k_, in0=xx, scalar1=1.0 / (2 * math.pi), scalar2=None, op0=ALU.mult), reads=[xk], writes=[kfk])
        S.op("dve", lambda e: e.tensor_copy(out=i_, in_=k_), reads=[kfk], writes=[kik])
        S.op("dve", lambda e: e.tensor_copy(out=k_, in_=i_), reads=[kik], writes=[kfk])
        S.op("dve", lambda e: e.scalar_tensor_tensor(out=xx, in0=k_, scalar=-2 * math.pi, in1=xx, op0=ALU.mult, op1=ALU.add), reads=[kfk, xk], writes=[xk])
        S.op("dve", lambda e: e.tensor_scalar(out=k_, in0=xx, scalar1=math.pi, scalar2=-2 * math.pi, op0=ALU.is_gt, op1=ALU.mult), reads=[xk], writes=[kfk])
        S.op("dve", lambda e: e.tensor_tensor(out=xx, in0=xx, in1=k_, op=ALU.add), reads=[xk, kfk], writes=[xk])
        S.op("dve", lambda e: e.tensor_scalar(out=k_, in0=xx, scalar1=-math.pi, scalar2=2 * math.pi, op0=ALU.is_lt, op1=ALU.mult), reads=[xk], writes=[kfk])
        S.op("dve", lambda e: e.tensor_tensor(out=xx, in0=xx, in1=k_, op=ALU.add), reads=[xk, kfk], writes=[xk])

    def hyena_filters(self, j):
        nc, S, T, NT = self.nc, self.S, self.T, self.NT
        self.HF = self.dt("hy_hf", [2, T, E], BF16)
        self.RN = self.dt("hy_rn", [128, E], F32)
        zT = self.sb("hy_zT", [33, T], F32)
        hA = self.sb("hy_hA", [64, T], F32)
        hB = self.sb("hy_hB", [64, T], F32)
        kf = self.sb("hy_kf", [64, T], F32)
        ki = self.sb("hy_ki", [64, T], mybir.dt.int32)
        w1 = self.sb("hy_w1", [33, 64], F32)
        w2 = self.sb("hy_w2", [64, 64], F32)
        w3 = self.sb("hy_w3", [64, 64], F32)
        w4 = self.sb("hy_w4", [64, 2 * E], F32)
        vec = self.sb("hy_fv", [64, 8], F32)
        delta = self.sb("hy_delta", [128, E], F32)
        tl = self.sb("hy_tl", [128, NT], F32)
        ssum = self.sb("hy_ssum", [128, 2 * E], F32)
        dec = self.ring("hy_dec", 2, [128, 512], F32)
        ft = self.ring("hy_ft", 2, [128, 512], F32)
        fa = self.ring("hy_fa", 2, [128, 512], F32)
        fb = self.ring("hy_fb", 2, [128, 512], BF16)
        S.dma("sp", zT[:], self.dram["c_hy_z"], writes=["hy_zT"])
        S.dma("sp", w1[:], self.dram["hy_f_w1"][j], writes=["hy_w"])
        S.dma("sp", w2[:], self.dram["hy_f_w2"][j], writes=["hy_w"])
        S.dma("sp", w3[:], self.dram["hy_f_w3"][j], writes=["hy_w"])
        S.dma("sp", w4[:], self.dram["hy_f_w4"][j], writes=["hy_w"])
        for i_, nm in enumerate(("hy_f_b1", "hy_f_b2", "hy_f_b3", "hy_f_freq")):
            S.dma("sp", vec[:, i_:i_ + 1], self.dram[nm][j].rearrange("(p o) -> p o", o=1), writes=["hy_fv"])
        S.dma("sp", delta[:], self.dram["c_hy_delta"].partition_broadcast(128), writes=["hy_delta"])
        S.dma("sp", tl[:], self.dram["c_hy_tl"], writes=["hy_tl"])
        for i_ in range(3):
            S.op("dve", lambda e: e.tensor_tensor(out=vec[:, 4 + i_:5 + i_], in0=vec[:, i_:i_ + 1], in1=vec[:, 3:4], op=ALU.mult), reads=["hy_fv"], writes=["hy_fv"])
        src, srck, krows = zT, "hy_zT", 33
        for li_, (w_, dst, dk_) in enumerate(((w1, hA, "hy_hA"), (w2, hB, "hy_hB"), (w3, hA, "hy_hA"))):
            for (t0, nt) in tok_blocks(T, 512):
                ps, pk = self.ps.next()
                S.op("pe", lambda e: e.matmul(ps[:64, :nt], lhsT=w_[:krows, :], rhs=src[:krows, t0:t0 + nt], start=True, stop=True),
                     reads=["hy_w", srck], writes=[pk])
                S.op("act", lambda e: e.activation(out=dst[:, t0:t0 + nt], in_=ps[:64, :nt], func=AF.Identity, scale=vec[:, 3:4],
                                                   bias=vec[:, 4 + li_:5 + li_]), reads=[pk, "hy_fv"], writes=[dk_])
            self.sin_reduce(dst, dk_, kf, "hy_kf", ki, "hy_ki", 64, T)
            S.op("act", lambda e: e.activation(out=dst[:, :], in_=dst[:, :], func=AF.Sin), reads=[dk_], writes=[dk_])
            src, srck, krows = dst, dk_, 64
        h3 = src
        for cb in range(2 * E // 512):
            half = cb // (E // 512)
            c0 = (cb % (E // 512)) * 512
            pacc, pacck = self.ps.next()
            for n in range(NT):
                t0 = n * 128
                nt = min(128, T - t0)
                ps, pk = self.ps.next()
                if ps is pacc:
                    ps, pk = self.ps.next()
                S.op("pe", lambda e: e.matmul(ps[:nt, :], lhsT=h3[:, t0:t0 + nt], rhs=w4[:, cb * 512:(cb + 1) * 512], start=True, stop=True),
                     reads=["hy_hA", "hy_w"], writes=[pk])
                d_, dk2 = dec.next()
                S.op("act", lambda e: e.activation(out=d_[:nt, :], in_=delta[:nt, c0:c0 + 512], func=AF.Exp, scale=tl[:nt, n:n + 1]),
                     reads=["hy_delta", "hy_tl"], writes=[dk2])
                f_, fk = ft.next()
                S.op("dve", lambda e: e.tensor_tensor(out=f_[:nt, :], in0=ps[:nt, :], in1=d_[:nt, :], op=ALU.mult), reads=[pk, dk2], writes=[fk])
                a_, ak = fa.next()
                S.op("act", lambda e: e.activation(out=a_[:nt, :], in_=f_[:nt, :], func=AF.Abs), reads=[fk], writes=[ak])
                S.op("pe", lambda e: e.matmul(pacc[:, :], lhsT=self.ones_f[:nt, :], rhs=a_[:nt, :], start=(n == 0), stop=(n == NT - 1)),
                     reads=["ones_f", ak], writes=[pacck])
                b_, bk = fb.next()
                S.op("pool", lambda e: e.tensor_copy(out=b_[:nt, :], in_=f_[:nt, :]), reads=[fk], writes=[bk])
                if half == 1 and n == 0:
                    S.op("pool", lambda e: e.memset(b_[0:1, :], 0.0), reads=[bk], writes=[bk])
                S.dma("sp", self.HF[half, t0:t0 + nt, c0:c0 + 512], b_[:nt, :], reads=[bk], writes=["HF"])
            S.op("dve", lambda e: e.tensor_copy(out=ssum[:, cb * 512:(cb + 1) * 512], in_=pacc[:, :]), reads=[pacck], writes=["hy_ssum"])
        S.op("dve", lambda e: e.tensor_tensor(out=ssum[:, 0:E], in0=ssum[:, 0:E], in1=ssum[:, E:2 * E], op=ALU.add), reads=["hy_ssum"], writes=["hy_ssum"])
        S.op("dve", lambda e: e.tensor_scalar(out=ssum[:, 0:E], in0=ssum[:, 0:E], scalar1=1e-6, scalar2=None, op0=ALU.add), reads=["hy_ssum"], writes=["hy_ssum"])
        S.op("dve", lambda e: e.reciprocal(out=ssum[:, 0:E], in_=ssum[:, 0:E]), reads=["hy_ssum"], writes=["hy_ssum"])
        S.dma("sp", self.RN, ssum[:, 0:E], reads=["hy_ssum"], writes=["RN"])

    def setup_hyena_proj(self):
        T = self.T
        self.hy_xp = self.sb("hy_xp", [128, T + 2], F32)
        self.hy_st = [self.sb("hy_st%d" % i, [128, T], F32) for i in range(3)]
        self.hy_pv = self.sb("hy_pvec", [128, 8], F32)
        self.hy_bf = self.ring("hy_bf", 2, [128, T], BF16)
        self.hy_tb = self.ring("hy_tb", 2, [128, 128], BF16)
        self.hy_sz = self.ring("hy_sz", 2, [128, 512], F32)
        self.VX = self.dt("hy_vx", [self.nseq, E, T], BF16)
        self.GX = self.dt("hy_gx", [self.nseq, E, T], BF16)
        self.VXT = self.dt("hy_vxt", [self.nseq, T, E], BF16)

    def hyena_proj(self, j, s):
        nc, S, T, NT = self.nc, self.S, self.T, self.NT
        xp, pv = self.hy_xp, self.hy_pv
        if s == 0:
            S.op("dve", lambda e: e.memset(xp[:], 0.0), writes=["hy_xp"])
        col1 = lambda ap: ap.rearrange("(p o) -> p o", o=1)
        for c in range(E // 128):
            for si in range(3):
                col0 = si * E + c * 128
                wb, wbk = self.load_wcols("hy_w_in", j, col0, 128)
                S.dma("sp", pv[:, 0:1], col1(self.dram["hy_b_in"][j][col0:col0 + 128]), writes=["hy_pv"])
                S.dma("sp", pv[:, 1:4], self.dram["hy_conv_w"][j][:, col0:col0 + 128].rearrange("k p -> p k"), writes=["hy_pv"],
                      allow_slow_non_contiguous=True)
                S.dma("sp", pv[:, 4:5], col1(self.dram["hy_conv_b"][j][col0:col0 + 128]), writes=["hy_pv"])
                for (t0, nt) in tok_blocks(T, 512):
                    ps, pk = self.proj_fm(wb, wbk, 128, t0, nt)
                    S.op("act", lambda e: e.activation(out=xp[:, 1 + t0:1 + t0 + nt], in_=ps[:, :nt], func=AF.Identity, bias=pv[:, 0:1]),
                         reads=[pk, "hy_pv"], writes=["hy_xp"])
                st, stk = self.hy_st[si], "hy_st%d" % si
                S.op("dve", lambda e: e.tensor_scalar(out=st[:], in0=xp[:, 0:T], scalar1=pv[:, 1:2], scalar2=pv[:, 4:5], op0=ALU.mult, op1=ALU.add),
                     reads=["hy_xp", "hy_pv"], writes=[stk])
                S.op("dve", lambda e: e.scalar_tensor_tensor(out=st[:], in0=xp[:, 1:1 + T], scalar=pv[:, 2:3], in1=st[:], op0=ALU.mult, op1=ALU.add),
                     reads=["hy_xp", "hy_pv", stk], writes=[stk])
                S.op("dve", lambda e: e.scalar_tensor_tensor(out=st[:], in0=xp[:, 2:2 + T], scalar=pv[:, 3:4], in1=st[:], op0=ALU.mult, op1=ALU.add),
                     reads=["hy_xp", "hy_pv", stk], writes=[stk])
            x0, x1, v = self.hy_st
            S.op("dve", lambda e: e.tensor_tensor(out=v[:], in0=v[:], in1=x1[:], op=ALU.mult), reads=["hy_st2", "hy_st1"], writes=["hy_st2"])
            vb, vbk = self.hy_bf.next()
            S.op("pool", lambda e: e.tensor_copy(out=vb[:], in_=v[:]), reads=["hy_st2"], writes=[vbk])
            S.dma("sp", self.VX[s, c * 128:(c + 1) * 128, :], vb[:], reads=[vbk], writes=[("VX", s)])
            for n in range(NT):
                t0 = n * 128
                nt = min(128, T - t0)
                ps, pk = self.ps.next()
                S.op("pe", lambda e: e.transpose(out=ps[:nt, :128], in_=v[:, t0:t0 + nt], identity=self.ident[:]), reads=["hy_st2", "ident"], writes=[pk])
                tb, tbk = self.hy_tb.next()
                S.op("act", lambda e: e.activation(out=tb[:nt, :], in_=ps[:nt, :128], func=AF.Identity), reads=[pk], writes=[tbk])
                S.dma("sp", self.VXT[s, t0:t0 + nt, c * 128:(c + 1) * 128], tb[:nt, :], reads=[tbk], writes=[("VXT", s)])
            col0 = 3 * E + c * 128
            wb, wbk = self.load_wcols("hy_w_in", j, col0, 128)
            S.dma("sp", pv[:, 5:6], col1(self.dram["hy_b_in"][j][col0:col0 + 128]), writes=["hy_pv"])
            for (t0, nt) in tok_blocks(T, 512):
                ps, pk = self.proj_fm(wb, wbk, 128, t0, nt)
                sz, szk = self.hy_sz.next()
                S.op("act", lambda e: e.activation(out=sz[:, :nt], in_=ps[:, :nt], func=AF.Silu, bias=pv[:, 5:6]), reads=[pk, "hy_pv"], writes=[szk])
                S.op("dve", lambda e: e.tensor_tensor(out=x0[:, t0:t0 + nt], in0=x0[:, t0:t0 + nt], in1=sz[:, :nt], op=ALU.mult),
                     reads=["hy_st0", szk], writes=["hy_st0"])
            gb, gbk = self.hy_bf.next()
            S.op("pool", lambda e: e.tensor_copy(out=gb[:], in_=x0[:]), reads=["hy_st0"], writes=[gbk])
            S.dma("sp", self.GX[s, c * 128:(c + 1) * 128, :], gb[:], reads=[gbk], writes=[("GX", s)])

    def load_tokmajor(self, dst, dkey, src2d, c0, ncols, rkeys):
        S, T, NT = self.S, self.T, self.NT
        nfull = T // 128
        if nfull:
            S.dma("sp", dst[:, 0:nfull, :ncols], src2d[0:nfull * 128, c0:c0 + ncols].rearrange("(n p) c -> p n c", p=128),
                  reads=rkeys, writes=[dkey])
        rem = T - nfull * 128
        if rem:
            S.dma("sp", dst[:rem, nfull, :ncols], src2d[nfull * 128:T, c0:c0 + ncols], reads=rkeys, writes=[dkey])

    def hyena_spectral(self, j):
        nc, S, T, NT, NFC, NK = self.nc, self.S, self.T, self.NT, self.NFC, self.NK
        Fm = self.dram["c_dft_f"]
        Gm = self.dram["c_dft_g"]
        self.HS = self.dt("hy_hs", [NK * 128, E], F32)
        fch = self.ring("hy_fch", 2, [128, NT, 128], BF16)
        rn = self.sb("hy_rnb", [128, E], F32)
        S.dma("sp", rn[:], self.RN, reads=["RN"], writes=["hy_rnb"])
        dat = [self.sb("hy_dat%d" % i, [128, NT, 512], BF16) for i in range(3)]
        hs_t = self.ring("hy_hst", 2, [128, 512], F32)
        for i_ in range(3):
            S.op("pool", lambda e: e.memset(dat[i_][:], 0.0), writes=["hy_dat%d" % i_])

        def load_f(kc):
            f_, fk = fch.next()
            self.load_tokmajor(f_, fk, Fm, kc * 128, 128, [])
            return f_, fk

        def fwd(kc, srcs, ps, pk):
            f_, fk = load_f(kc)
            tot = len(srcs) * NT
            i_ = 0
            for (d_, dk_) in srcs:
                for n in range(NT):
                    nt = min(128, T - n * 128)
                    S.op("pe", lambda e: e.matmul(ps[:, :], lhsT=f_[:nt, n, :], rhs=d_[:nt, n, :], start=(i_ == 0), stop=(i_ == tot - 1)),
                         reads=[fk, dk_], writes=[pk])
                    i_ += 1
        for cb in range(E // 512):
            c0 = cb * 512
            self.load_tokmajor(dat[0], "hy_dat0", self.HF[0], c0, 512, ["HF"])
            self.load_tokmajor(dat[1], "hy_dat1", self.HF[1], c0, 512, ["HF"])
            S.op("pool", lambda e: e.tensor_scalar(out=dat[2][:].rearrange("p n c -> p (n c)"), in0=dat[1][:].rearrange("p n c -> p (n c)"),
                                                   scalar1=-1.0, scalar2=None, op0=ALU.mult), reads=["hy_dat1"], writes=["hy_dat2"])
            for kc in range(NK):
                ps, pk = self.ps.next()
                second = (dat[1], "hy_dat1") if kc < NFC else (dat[2], "hy_dat2")
                fwd(kc, [(dat[0], "hy_dat0"), second], ps, pk)
                h_, hk = hs_t.next()
                S.op("dve", lambda e: e.tensor_tensor(out=h_[:], in0=ps[:, :], in1=rn[:, c0:c0 + 512], op=ALU.mult), reads=[pk, "hy_rnb"], writes=[hk])
                S.dma("sp", self.HS[kc * 128:(kc + 1) * 128, c0:c0 + 512], h_[:], reads=[hk], writes=["HS"])
        self.close_scope()
        self.open_scope()
        fch = self.ring("hy_fch2", 2, [128, NT, 128], BF16)
        skip = self.sb("hy_skip", [128, E // 128], F32)
        S.dma("sp", skip[:], self.dram["hy_skip"][j].rearrange("(c p) -> p c", p=128), writes=["hy_skip"], allow_slow_non_contiguous=True)
        dat = [self.sb("hy_dat0b", [128, NT, 512], BF16)]
        S.op("pool", lambda e: e.memset(dat[0][:], 0.0), writes=["hy_dat0"])
        zt = self.sb("hy_zt", [128, NK, 512], BF16)
        hr_t = self.ring("hy_hr", 2, [128, 512], F32)
        hi_t = self.ring("hy_hi", 2, [128, 512], F32)
        tmp = self.ring("hy_tmp", 4, [128, 512], F32)
        gti = self.ring("hy_gt", 6, [128, 512], BF16)
        vxb = self.ring("hy_vxb", 2, [128, 512], BF16)
        gxb = self.ring("hy_gxb", 2, [128, 512], BF16)
        yo = self.ring("hy_yo", 2, [128, 512], F32)
        go = self.ring("hy_go", 2, [128, 512], BF16)
        acc = Ring("hacc", self.PB[0:4], self.PBK[0:4])
        for s in range(self.nseq):
            for cb in range(E // 512):
                c0 = cb * 512
                self.load_tokmajor(dat[0], "hy_dat0", self.VXT[s], c0, 512, [("VXT", s)])
                for i_ in range(NFC):
                    pr, prk = self.PB[4], self.PBK[4]
                    pi, pik = self.PB[5], self.PBK[5]
                    fwd(i_, [(dat[0], "hy_dat0")], pr, prk)
                    fwd(NFC + i_, [(dat[0], "hy_dat0")], pi, pik)
                    hr, hrk = hr_t.next()
                    hi, hik = hi_t.next()
                    S.dma("sp", hr[:], self.HS[i_ * 128:(i_ + 1) * 128, c0:c0 + 512], reads=["HS"], writes=[hrk])
                    S.dma("sp", hi[:], self.HS[(NFC + i_) * 128:(NFC + i_ + 1) * 128, c0:c0 + 512], reads=["HS"], writes=[hik])
                    t1, t1k = tmp.next()
                    t2, t2k = tmp.next()
                    S.op("dve", lambda e: e.tensor_tensor(out=t1[:], in0=pr[:, :], in1=hr[:], op=ALU.mult), reads=[prk, hrk], writes=[t1k])
                    S.op("dve", lambda e: e.tensor_tensor(out=t2[:], in0=pi[:, :], in1=hi[:], op=ALU.mult), reads=[pik, hik], writes=[t2k])
                    S.op("dve", lambda e: e.tensor_tensor(out=zt[:, i_, :], in0=t1[:], in1=t2[:], op=ALU.subtract), reads=[t1k, t2k], writes=["hy_zt"])
                    t3, t3k = tmp.next()
                    t4, t4k = tmp.next()
                    S.op("dve", lambda e: e.tensor_tensor(out=t3[:], in0=pr[:, :], in1=hi[:], op=ALU.mult), reads=[prk, hik], writes=[t3k])
                    S.op("dve", lambda e: e.tensor_tensor(out=t4[:], in0=pi[:, :], in1=hr[:], op=ALU.mult), reads=[pik, hrk], writes=[t4k])
                    S.op("dve", lambda e: e.tensor_tensor(out=zt[:, NFC + i_, :], in0=t3[:], in1=t4[:], op=ALU.add), reads=[t3k, t4k], writes=["hy_zt"])
                for (t0, nt) in tok_blocks(T, 512):
                    for kc in range(NK):
                        g_, gk = gti.next()
                        S.dma("sp" if kc % 2 == 0 else "pool", g_[:, :nt], Gm[kc * 128:(kc + 1) * 128, t0:t0 + nt], writes=[gk])
                        for cs in range(4):
                            S.op("pe", lambda e: e.matmul(acc.bufs[cs][:, :nt], lhsT=zt[:, kc, cs * 128:(cs + 1) * 128], rhs=g_[:, :nt],
                                                          start=(kc == 0), stop=(kc == NK - 1)), reads=["hy_zt", gk], writes=[acc.keys[cs]])
                    for cs in range(4):
                        r0 = c0 + cs * 128
                        vx_, vxk = vxb.next()
                        gx_, gxk = gxb.next()
                        S.dma("pool", vx_[:, :nt], self.VX[s, r0:r0 + 128, t0:t0 + nt], reads=[("VX", s)], writes=[vxk])
                        S.dma("pool", gx_[:, :nt], self.GX[s, r0:r0 + 128, t0:t0 + nt], reads=[("GX", s)], writes=[gxk])
                        y_, yk = yo.next()
                        S.op("dve", lambda e: e.scalar_tensor_tensor(out=y_[:, :nt], in0=vx_[:, :nt], scalar=skip[:, r0 // 128:r0 // 128 + 1],
                                                                     in1=acc.bufs[cs][:, :nt], op0=ALU.mult, op1=ALU.add),
                             reads=[vxk, "hy_skip", acc.keys[cs]], writes=[yk])
                        g2, g2k = go.next()
                        S.op("dve", lambda e: e.tensor_tensor(out=g2[:, :nt], in0=y_[:, :nt], in1=gx_[:, :nt], op=ALU.mult), reads=[yk, gxk], writes=[g2k])
                        S.dma("sp", self.G[s, r0:r0 + 128, t0:t0 + nt], g2[:, :nt], reads=[g2k], writes=[("G", s)])

    def setup_gdn(self):
        T = self.T
        self.NT = (T + 127) // 128
        Tp = self.NT * 128
        self.Tp = Tp
        NT = self.NT
        self.setup_head_norm()
        self.gd_c = self.sb("gd_c", [128, 10, 128], F32)
        self.gd_xp = self.sb("gd_xp", [128, Tp + 2], BF16)
        self.gd_xs = self.sb("gd_xs", [128, Tp + 2], F32)
        self.gd_q = self.sb("gd_q", [128, Tp], BF16)
        self.gd_k = self.sb("gd_k", [128, Tp], BF16)
        self.gd_ktok = self.sb("gd_ktok", [128, NT, 128], BF16)
        self.gd_vtok = self.sb("gd_vtok", [128, NT, 128], BF16)
        self.gd_o = self.sb("gd_o", [128, Tp], BF16)
        self.gd_tab = {n: self.sb("gd_" + n, [128, NT, 32], F32) for n in ("gam", "beta", "nb", "ksc", "egl")}
        self.gd_cw = self.sb("gd_cw", [128, 3], F32)
        self.gd_gc = self.sb("gd_gc", [128, 2], F32)
        self.gd_gain = self.sb("gd_gain", [128, 1], F32)
        self.gd_wgb = self.sb("gd_wgb", [128, KC, 128], BF16)
        self.gd_sq = self.hn_sq
        self.gd_r = self.hn_r
        self.gd_m = self.ring("gd_m", 16, [128, 128], F32)
        self.gd_e = self.ring("gd_e", 6, [128, 128], F32)
        self.gd_tt = self.ring("gd_tt", 3, [128, 128], BF16)
        self.gd_gtn = self.gd_m
        self.gd_mb = self.ring("gd_mb", 8, [128, 128], BF16)
        self.gd_kk = self.ring("gd_kk", 3, [128, 2, 128], F32)
        self.gd_S = [self.sb("gd_S%d" % i, [128, 128], F32) for i in range(2)]
        self.gd_Sb = [self.sb("gd_Sb%d" % i, [128, 128], BF16) for i in range(2)]
        self.gd_sz = self.ring("gd_sz", 1, [128, 512], BF16)

    def gdn_qkv_fm(self, j, col0, cw_row0, dst_fp32):
        S, T, Tp = self.S, self.T, self.Tp
        wb, wbk = self.load_wcols("gdn_w_in", j, col0, 128)
        S.dma("sp", self.gd_cw[:], self.dram["gdn_conv_w"][j][:, cw_row0:cw_row0 + 128].rearrange("k p -> p k"),
              writes=["gd_cw"], allow_slow_non_contiguous=True)
        xp = self.gd_xp
        for (t0, nt) in tok_blocks(T, 512):
            ps, pk = self.proj_fm(wb, wbk, 128, t0, nt)
            S.op("act", lambda e: e.activation(out=xp[:, 1 + t0:1 + t0 + nt], in_=ps[:, :nt], func=AF.Identity), reads=[pk], writes=["gd_xp"])
        xs = dst_fp32
        S.op("dve", lambda e: e.tensor_scalar(out=xs[:, 1:1 + T], in0=xp[:, 0:T], scalar1=self.gd_cw[:, 0:1], scalar2=None, op0=ALU.mult),
             reads=["gd_xp", "gd_cw"], writes=["gd_xs"])
        S.op("dve", lambda e: e.scalar_tensor_tensor(out=xs[:, 1:1 + T], in0=xp[:, 1:1 + T], scalar=self.gd_cw[:, 1:2], in1=xs[:, 1:1 + T],
                                                     op0=ALU.mult, op1=ALU.add), reads=["gd_xp", "gd_cw", "gd_xs"], writes=["gd_xs"])
        S.op("dve", lambda e: e.scalar_tensor_tensor(out=xs[:, 1:1 + T], in0=xp[:, 2:2 + T], scalar=self.gd_cw[:, 2:3], in1=xs[:, 1:1 + T],
                                                     op0=ALU.mult, op1=ALU.add), reads=["gd_xp", "gd_cw", "gd_xs"], writes=["gd_xs"])
        S.op("act", lambda e: e.activation(out=xs[:, 1:1 + T], in_=xs[:, 1:1 + T], func=AF.Silu), reads=["gd_xs"], writes=["gd_xs"])

    def gdn_l2norm_to(self, dst_bf, dkey, scale):
        S, T = self.S, self.T
        xs = self.gd_xs
        for (t0, nt) in tok_blocks(T, 512):
            sq, sqk = self.gd_sq.next()
            S.op("act", lambda e: e.activation(out=sq[:, :nt], in_=xs[:, 1 + t0:1 + t0 + nt], func=AF.Square), reads=["gd_xs"], writes=[sqk])
            ps, pk = self.ps.next()
            S.op("pe", lambda e: e.matmul(ps[:, :nt], lhsT=self.ones_b[:], rhs=sq[:, :nt], start=True, stop=True), reads=["ones_b", sqk], writes=[pk])
            r, rk = self.gd_r.next()
            S.op("act", lambda e: e.activation(out=r[:, :nt], in_=ps[:, :nt], func=AF.Ln, bias=1e-6), reads=[pk], writes=[rk])
            S.op("act", lambda e: e.activation(out=r[:, :nt], in_=r[:, :nt], func=AF.Exp, scale=-0.5), reads=[rk], writes=[rk])
            S.op("dve", lambda e: e.scalar_tensor_tensor(out=xs[:, 1 + t0:1 + t0 + nt], in0=xs[:, 1 + t0:1 + t0 + nt], scalar=float(scale),
                                                         in1=r[:, :nt], op0=ALU.mult, op1=ALU.mult), reads=["gd_xs", rk], writes=["gd_xs"])
        S.op("pool", lambda e: e.tensor_copy(out=dst_bf[:, 0:T], in_=xs[:, 1:1 + T]), reads=["gd_xs"], writes=[dkey])

    def gdn_to_tok(self, dst, dkey):
        S, NT = self.S, self.NT
        xs = self.gd_xs
        for n in range(NT):
            ps, pk = self.ps.next()
            S.op("pe", lambda e: e.transpose(out=ps[:, :128], in_=xs[:, 1 + n * 128:1 + (n + 1) * 128], identity=self.ident[:]),
                 reads=["gd_xs", "ident"], writes=[pk])
            S.op("act", lambda e: e.activation(out=dst[:, n, :], in_=ps[:, :128], func=AF.Identity), reads=[pk], writes=[dkey])

    def gdn_inv_gen(self, cx):
        S = self.S
        C = self.gd_c
        tab = self.gd_tab
        h, d, n = cx["h"], cx["d"], cx["n"]
        col = d * 16 + h
        maskA, maskT, nstrict = C[:, 4 + d, :], C[:, 6 + d, :], C[:, 8 + d, :]
        c0 = n * 128
        kc_ = self.gd_k[:, c0:c0 + 128]
        qc_ = self.gd_q[:, c0:c0 + 128]
        gcol = tab["gam"][:, n, col:col + 1]
        kk, kkk = self.gd_kk.next()
        ps, pk = self.ps.next()
        S.op("pe", lambda e: e.matmul(ps[:, 0:128], lhsT=kc_, rhs=kc_, start=True, stop=True), reads=["gd_k"], writes=[pk]); yield
        S.op("pe", lambda e: e.matmul(ps[:, 128:256], lhsT=kc_, rhs=qc_, start=True, stop=True), reads=["gd_k", "gd_q"], writes=[pk]); yield
        S.op("dve", lambda e: e.tensor_tensor(out=kk[:, 0, :], in0=ps[:, 0:128], in1=nstrict, op=ALU.mult), reads=[pk, "gd_c"], writes=[kkk]); yield
        S.op("dve", lambda e: e.tensor_copy(out=kk[:, 1, :], in_=ps[:, 128:256]), reads=[pk], writes=[kkk]); yield
        dg, dgk = self.gd_m.next()
        S.op("dve", lambda e: e.tensor_scalar(out=dg[:], in0=self.ident[:], scalar1=gcol, scalar2=None, op0=ALU.mult),
             reads=["ident", "gd_gam"], writes=[dgk]); yield
        pg, pgk = self.ps.next()
        S.op("pe", lambda e: e.matmul(pg[:, 0:128], lhsT=self.ones_f[:], rhs=dg[:], start=True, stop=True), reads=["ones_f", dgk], writes=[pgk]); yield
        e1, e1k = self.gd_m.next()
        S.op("dve", lambda e: e.scalar_tensor_tensor(out=e1[:], in0=pg[:, 0:128], scalar=gcol, in1=maskA, op0=ALU.subtract, op1=ALU.add),
             reads=[pgk, "gd_gam", "gd_c"], writes=[e1k]); yield
        e2, e2k = self.gd_e.next()
        S.op("dve", lambda e: e.scalar_tensor_tensor(out=e2[:], in0=pg[:, 0:128], scalar=gcol, in1=maskT, op0=ALU.subtract, op1=ALU.add),
             reads=[pgk, "gd_gam", "gd_c"], writes=[e2k]); yield
        eg, egk = self.gd_e.next()
        S.op("dve", lambda e: e.tensor_copy(out=eg[:], in_=pg[:, 0:128]), reads=[pgk], writes=[egk]); yield
        S.op("act", lambda e: e.activation(out=e1[:], in_=e1[:], func=AF.Exp, scale=-1.0), reads=[e1k], writes=[e1k]); yield
        S.op("act", lambda e: e.activation(out=e2[:], in_=e2[:], func=AF.Exp), reads=[e2k], writes=[e2k]); yield
        S.op("act", lambda e: e.activation(out=eg[:], in_=eg[:], func=AF.Exp), reads=[egk], writes=[egk]); yield
        L, Lk = self.gd_m.next()
        S.op("dve", lambda e: e.scalar_tensor_tensor(out=L[:], in0=e1[:], scalar=tab["beta"][:, n, col:col + 1], in1=kk[:, 0, :],
                                                     op0=ALU.mult, op1=ALU.mult), reads=[e1k, "gd_beta", kkk], writes=[Lk]); yield
        pt, ptk = self.ps.next()
        S.op("pe", lambda e: e.transpose(out=pt[:, 0:128], in_=L[:], identity=self.ident[:]), reads=[Lk, "ident"], writes=[ptk]); yield
        M, Mk = self.gd_m.next()
        S.op("act", lambda e: e.activation(out=M[:], in_=pt[:, 0:128], func=AF.Identity), reads=[ptk], writes=[Mk]); yield
        P, Pk = self.gd_m.next()
        S.op("dve", lambda e: e.tensor_tensor(out=P[:], in0=M[:], in1=self.ident[:], op=ALU.add), reads=[Mk, "ident"], writes=[Pk]); yield
        for lvl in range(1, 7):
            p2, p2k = self.ps.next()
            S.op("pe", lambda e: e.matmul(p2[:, 0:128], lhsT=M[:], rhs=L[:], start=True, stop=True), reads=[Mk, Lk], writes=[p2k]); yield
            if lvl < 6:
                S.op("pe", lambda e: e.matmul(p2[:, 128:256], lhsT=L[:], rhs=M[:], start=True, stop=True), reads=[Mk, Lk], writes=[p2k]); yield
            L2, L2k = self.gd_m.next()
            S.op("act", lambda e: e.activation(out=L2[:], in_=p2[:, 0:128], func=AF.Identity), reads=[p2k], writes=[L2k]); yield
            if lvl < 6:
                M2, M2k = self.gd_m.next()
                S.op("act", lambda e: e.activation(out=M2[:], in_=p2[:, 128:256], func=AF.Identity), reads=[p2k], writes=[M2k]); yield
            p3, p3k = self.ps.next()
            S.op("pe", lambda e: e.matmul(p3[:, 0:128], lhsT=L2[:], rhs=P[:], start=True, stop=True), reads=[L2k, Pk], writes=[p3k]); yield
            P2, P2k = self.gd_m.next()
            S.op("dve", lambda e: e.tensor_tensor(out=P2[:], in0=p3[:, 0:128], in1=P[:], op=ALU.add), reads=[p3k, Pk], writes=[P2k]); yield
            P, Pk = P2, P2k
            L, Lk = L2, L2k
            if lvl < 6:
                M, Mk = M2, M2k
        TT, TTk = self.gd_tt.next()
        S.op("pool", lambda e: e.tensor_copy(out=TT[:], in_=P[:]), reads=[Pk], writes=[TTk]); yield
        cx.update(kk=kk, kkk=kkk, e2=e2, e2k=e2k, eg=eg, egk=egk, TT=TT, TTk=TTk)

    def gdn_scan_gen(self, cx):
        S = self.S
        tab = self.gd_tab
        h, d, n = cx["h"], cx["d"], cx["n"]
        col = d * 16 + h
        kk, kkk, e2, e2k, eg, egk, TT, TTk = (cx[k_] for k_ in ("kk", "kkk", "e2", "e2k", "eg", "egk", "TT", "TTk"))
        Sf, Sb = self.gd_S[d], self.gd_Sb[d]
        sk, sbk = "gd_S%d" % d, "gd_Sb%d" % d
        c0 = n * 128
        kc_ = self.gd_k[:, c0:c0 + 128]
        qc_ = self.gd_q[:, c0:c0 + 128]
        pks, pksk = self.ps.next()
        S.op("pe", lambda e: e.matmul(pks[:, 0:128], lhsT=kc_, rhs=Sb[:], start=True, stop=True), reads=["gd_k", sbk], writes=[pksk]); yield
        vb, vbk = self.gd_mb.next()
        S.op("pool", lambda e: e.tensor_scalar(out=vb[:], in0=self.gd_vtok[:, n, :], scalar1=tab["beta"][:, n, col:col + 1], scalar2=None, op0=ALU.mult),
             reads=["gd_vtok", "gd_beta"], writes=[vbk]); yield
        R, Rk = self.gd_mb.next()
        S.op("dve", lambda e: e.scalar_tensor_tensor(out=R[:], in0=pks[:, 0:128], scalar=tab["nb"][:, n, col:col + 1], in1=vb[:],
                                                     op0=ALU.mult, op1=ALU.add), reads=[pksk, "gd_nb", vbk], writes=[Rk]); yield
        pv, pvk = self.ps.next()
        S.op("pe", lambda e: e.matmul(pv[:, 0:128], lhsT=TT[:], rhs=R[:], start=True, stop=True), reads=[TTk, Rk], writes=[pvk]); yield
        VN, VNk = self.gd_mb.next()
        S.op("act", lambda e: e.activation(out=VN[:], in_=pv[:, 0:128], func=AF.Identity), reads=[pvk], writes=[VNk]); yield
        qg, qgk = self.gd_mb.next()
        S.op("dve", lambda e: e.tensor_tensor(out=qg[:], in0=qc_, in1=eg[:], op=ALU.mult), reads=["gd_q", egk], writes=[qgk]); yield
        at, atk = self.gd_mb.next()
        S.op("dve", lambda e: e.tensor_tensor(out=at[:], in0=kk[:, 1, :], in1=e2[:], op=ALU.mult), reads=[kkk, e2k], writes=[atk]); yield
        po, pok = self.ps.next()
        S.op("pe", lambda e: e.matmul(po[:, 0:128], lhsT=Sb[:], rhs=qg[:], start=True, stop=False), reads=[sbk, qgk], writes=[pok]); yield
        S.op("pe", lambda e: e.matmul(po[:, 0:128], lhsT=VN[:], rhs=at[:], start=False, stop=True), reads=[VNk, atk], writes=[pok]); yield
        if d == 0:
            S.op("act", lambda e: e.activation(out=self.gd_o[:, c0:c0 + 128], in_=po[:, 0:128], func=AF.Identity), reads=[pok], writes=["gd_o"]); yield
        else:
            S.op("dve", lambda e: e.tensor_tensor(out=self.gd_o[:, c0:c0 + 128], in0=po[:, 0:128], in1=self.gd_o[:, c0:c0 + 128], op=ALU.add),
                 reads=[pok, "gd_o"], writes=["gd_o"]); yield
        ke, kek = self.gd_mb.next()
        S.op("pool", lambda e: e.tensor_scalar(out=ke[:], in0=self.gd_ktok[:, n, :], scalar1=tab["ksc"][:, n, col:col + 1], scalar2=None, op0=ALU.mult),
             reads=["gd_ktok", "gd_ksc"], writes=[kek]); yield
        pS, pSk = self.ps.next()
        S.op("pe", lambda e: e.matmul(pS[:, 0:128], lhsT=ke[:], rhs=VN[:], start=True, stop=True), reads=[kek, VNk], writes=[pSk]); yield
        S.op("dve", lambda e: e.scalar_tensor_tensor(out=Sf[:], in0=Sf[:], scalar=tab["egl"][:, n, col:col + 1], in1=pS[:, 0:128],
                                                     op0=ALU.mult, op1=ALU.add), reads=[sk, "gd_egl", pSk], writes=[sk]); yield
        S.op("act", lambda e: e.activation(out=Sb[:], in_=Sf[:], func=AF.Identity), reads=[sk], writes=[sbk]); yield

    def mixer_gdn(self, li, j, s):
        nc, S, T, Tp, NT = self.nc, self.S, self.T, self.Tp, self.NT
        C = self.gd_c
        tab = self.gd_tab
        Uf, Ub, SELL, SELF = C[:, 0, :], C[:, 1, :], C[:, 2, :], C[:, 3, :]
        maskA = [C[:, 4, :], C[:, 5, :]]
        maskT = [C[:, 6, :], C[:, 7, :]]
        nstrict = [C[:, 8, :], C[:, 9, :]]
        if s == 0:
            S.dma("sp", C[:], self.dram["c_gdn"], writes=["gd_c"])
            S.dma("sp", self.gd_gain[:], self.dram["gdn_o_norm"][j].rearrange("(p o) -> p o", o=1), writes=["hn_gain"])
            wg_, wgk_ = self.wring.next()
            S.op("dve", lambda e: e.memset(wg_[:], 0.0), writes=[wgk_])
            S.op("dve", lambda e: e.memset(self.gd_gc[:], 0.0), writes=["gd_gc"])
            for q4 in range(4):
                src = self.dram["gdn_w_in"][j][:, 3 * E + q4 * 16:3 * E + (q4 + 1) * 16].rearrange("(kc p) w -> p kc w", p=128)
                S.dma("sp", wg_[:, :, q4 * 32:q4 * 32 + 16], src, writes=[wgk_])
            for d in range(2):
                S.dma("sp", self.gd_gc[d * 32:d * 32 + 16, 0:1], self.dram["gdn_dt_bias"][j][d].rearrange("(p o) -> p o", o=1), writes=["gd_gc"])
                S.dma("sp", self.gd_gc[d * 32:d * 32 + 16, 1:2], self.dram["gdn_a_log"][j][d].rearrange("(p o) -> p o", o=1), writes=["gd_gc"])
            S.op("pool", lambda e: e.tensor_copy(out=self.gd_wgb[:], in_=wg_[:]), reads=[wgk_], writes=["gd_wgb"])
            S.op("act", lambda e: e.activation(out=self.gd_gc[0:64, 1:2], in_=self.gd_gc[0:64, 1:2], func=AF.Exp), reads=["gd_gc"], writes=["gd_gc"])
            S.op("dve", lambda e: e.tensor_scalar(out=self.gd_gc[0:64, 1:2], in0=self.gd_gc[0:64, 1:2], scalar1=-1.0, scalar2=None, op0=ALU.mult),
                 reads=["gd_gc"], writes=["gd_gc"])
        gf = self.gd_xs
        S.op("dve", lambda e: e.memset(gf[:], 0.0), writes=["gd_xs"])
        for (t0, nt) in tok_blocks(T, 512):
            ps, pk = self.proj_fm(self.gd_wgb, "gd_wgb", 128, t0, nt)
            S.op("act", lambda e: e.activation(out=gf[0:64, t0:t0 + nt], in_=ps[0:64, :nt], func=AF.Exp, bias=self.gd_gc[0:64, 0:1]),
                 reads=[pk, "gd_gc"], writes=["gd_xs"])
            S.op("act", lambda e: e.activation(out=gf[64:128, t0:t0 + nt], in_=ps[64:128, :nt], func=AF.Sigmoid), reads=[pk], writes=["gd_xs"])
        S.op("act", lambda e: e.activation(out=gf[0:64, 0:T], in_=gf[0:64, 0:T], func=AF.Ln, bias=1.0), reads=["gd_xs"], writes=["gd_xs"])
        S.op("dve", lambda e: e.tensor_scalar(out=gf[0:64, 0:T], in0=gf[0:64, 0:T], scalar1=self.gd_gc[0:64, 1:2], scalar2=None, op0=ALU.mult),
             reads=["gd_xs", "gd_gc"], writes=["gd_xs"])
        for n in range(NT):
            ps, pk = self.ps.next()
            S.op("pe", lambda e: e.transpose(out=ps[:, :128], in_=gf[:, n * 128:(n + 1) * 128], identity=self.ident[:]),
                 reads=["gd_xs", "ident"], writes=[pk])
            gtn, gtk = self.gd_gtn.next()
            S.op("act", lambda e: e.activation(out=gtn[:], in_=ps[:, :128], func=AF.Identity), reads=[pk], writes=[gtk])
            ps, pk = self.ps.next()
            S.op("pe", lambda e: e.matmul(ps[:, 0:16], lhsT=Uf, rhs=gtn[:, 0:16], start=True, stop=True), reads=["gd_c", gtk], writes=[pk])
            S.op("pe", lambda e: e.matmul(ps[:, 16:32], lhsT=Ub, rhs=gtn[:, 32:48], start=True, stop=True), reads=["gd_c", gtk], writes=[pk])
            S.op("dve", lambda e: e.tensor_copy(out=tab["gam"][:, n, :], in_=ps[:, 0:32]), reads=[pk], writes=["gd_gam"])
            S.op("pool", lambda e: e.tensor_copy(out=tab["beta"][:, n, 0:16], in_=gtn[:, 64:80]), reads=[gtk], writes=["gd_beta"])
            S.op("pool", lambda e: e.tensor_copy(out=tab["beta"][:, n, 16:32], in_=gtn[:, 96:112]), reads=[gtk], writes=["gd_beta"])
        for n in range(NT):
            ps, pk = self.ps.next()
            S.op("pe", lambda e: e.matmul(ps[:, 0:16], lhsT=SELL, rhs=tab["gam"][:, n, 0:16], start=True, stop=True), reads=["gd_c", "gd_gam"], writes=[pk])
            S.op("pe", lambda e: e.matmul(ps[:, 16:32], lhsT=SELF, rhs=tab["gam"][:, n, 16:32], start=True, stop=True), reads=["gd_c", "gd_gam"], writes=[pk])
            S.op("dve", lambda e: e.tensor_copy(out=tab["egl"][:, n, :], in_=ps[:, 0:32]), reads=[pk], writes=["gd_egl"])
        fl = lambda t: t[:].rearrange("p n c -> p (n c)")
        S.op("dve", lambda e: e.tensor_tensor(out=fl(tab["ksc"]), in0=fl(tab["egl"]), in1=fl(tab["gam"]), op=ALU.subtract), reads=["gd_egl", "gd_gam"], writes=["gd_ksc"])
        S.op("act", lambda e: e.activation(out=fl(tab["ksc"]), in_=fl(tab["ksc"]), func=AF.Exp), reads=["gd_ksc"], writes=["gd_ksc"])
        S.op("act", lambda e: e.activation(out=fl(tab["egl"]), in_=fl(tab["egl"]), func=AF.Exp), reads=["gd_egl", "gd_ksc"], writes=["gd_egl"])
        S.op("act", lambda e: e.activation(out=fl(tab["nb"]), in_=fl(tab["gam"]), func=AF.Exp), reads=["gd_gam"], writes=["gd_nb"])
        S.op("dve", lambda e: e.scalar_tensor_tensor(out=fl(tab["nb"]), in0=fl(tab["nb"]), scalar=-1.0, in1=fl(tab["beta"]), op0=ALU.mult, op1=ALU.mult),
             reads=["gd_nb", "gd_beta"], writes=["gd_nb"])
        import os
        dbg = bool(os.environ.get("GDNDBG"))
        if dbg:
            for nm in ("gam", "beta", "nb", "ksc", "egl"):
                d_ = nc.dram_tensor("dbg_" + nm, [128, NT, 32], F32, kind="ExternalOutput").ap()
                S.dma("sp", d_, tab[nm][:], reads=["gd_" + nm], writes=["dbg_" + nm])
        S.op("dve", lambda e: e.memset(self.gd_xs[:], 0.0), writes=["gd_xs"])
        S.op("dve", lambda e: e.memset(self.gd_xp[:], 0.0), writes=["gd_xp"])
        S.op("dve", lambda e: e.memset(self.gd_q[:], 0.0), writes=["gd_q"])
        S.op("dve", lambda e: e.memset(self.gd_k[:], 0.0), writes=["gd_k"])
        tabkeys = ["gd_gam", "gd_beta", "gd_nb", "gd_ksc", "gd_egl"]
        for kh in range(8):
            self.gdn_qkv_fm(j, kh * 128, kh * 128, self.gd_xs)
            self.gdn_l2norm_to(self.gd_q, "gd_q", 128 ** -0.5)
            self.gdn_qkv_fm(j, D + kh * 128, D + kh * 128, self.gd_xs)
            self.gdn_l2norm_to(self.gd_k, "gd_k", 1.0)
            self.gdn_to_tok(self.gd_ktok, "gd_ktok")
            for hv in range(2):
                h = kh * 2 + hv
                self.gdn_qkv_fm(j, 2 * D + h * 128, 2 * D + h * 128, self.gd_xs)
                self.gdn_to_tok(self.gd_vtok, "gd_vtok")
                for d in range(2):
                    S.op("dve", lambda e: e.memset(self.gd_S[d][:], 0.0), writes=["gd_S%d" % d])
                    S.op("dve", lambda e: e.memset(self.gd_Sb[d][:], 0.0), writes=["gd_Sb%d" % d])
                probs = [(0, n) for n in range(NT)] + [(1, n) for n in range(NT - 1, -1, -1)]
                ctxs = [dict(h=h, d=d_, n=n_) for (d_, n_) in probs]
                NP = len(ctxs)
                ia = ib = 0
                inflight = []
                doneA = set()
                Bg = None
                while ib < NP:
                    while len(inflight) < 2 and ia < NP and ia < ib + 3:
                        inflight.append((ia, self.gdn_inv_gen(ctxs[ia])))
                        ia += 1
                    if Bg is None and ib in doneA:
                        Bg = self.gdn_scan_gen(ctxs[ib])
                    for item in list(inflight):
                        try:
                            next(item[1])
                        except StopIteration:
                            doneA.add(item[0])
                            inflight.remove(item)
                    if Bg is not None:
                        try:
                            next(Bg)
                        except StopIteration:
                            Bg = None
                            ib += 1
                if dbg and h in (0, 15):
                    d_ = nc.dram_tensor("dbg_o%d" % h, [128, Tp], F32, kind="ExternalOutput").ap()
                    S.dma("sp", d_, self.gd_o[:], reads=["gd_o"], writes=["dbg_o%d" % h])
                    if h == 0:
                        for nm, t_ in (("q", self.gd_q), ("k", self.gd_k)):
                            d_ = nc.dram_tensor("dbg_" + nm, [128, Tp], BF16, kind="ExternalOutput").ap()
                            S.dma("sp", d_, t_[:], reads=["gd_" + nm], writes=["dbg_" + nm])
                        d_ = nc.dram_tensor("dbg_v", [128, NT, 128], BF16, kind="ExternalOutput").ap()
                        S.dma("sp", d_, self.gd_vtok[:], reads=["gd_vtok"], writes=["dbg_v"])
                wz, wzk = self.load_wcols("gdn_w_in", j, 2 * E + h * 128, 128)
                self.hn_ps = self.ps
                for (t0, nt) in tok_blocks(T, 512):
                    ps, pk = self.proj_fm(wz, wzk, 128, t0, nt)
                    sz, szk = self.gd_sz.next()
                    S.op("act", lambda e: e.activation(out=sz[:, :nt], in_=ps[:, :nt], func=AF.Silu), reads=[pk], writes=[szk])
                    self.head_norm_gate(self.gd_o[:, t0:t0 + nt], "gd_o", nt, self.gd_gain[:, 0:1], sz[:, :nt], szk, s, h * 128, t0, 1e-6, 1.0)

    def setup_conformer(self):
        NE = E // 128
        self.cf_ypad = self.sb("cf_ypad", [128, self.T + 30], BF16)
        self.cf_a = self.ring("cf_a", 2, [128, 512], F32)
        self.cf_sg = self.ring("cf_sg", 2, [128, 512], F32)
        self.cf_sz = self.ring("cf_sz", 2, [128, 512], BF16)
        self.cf_cz = self.ring("cf_cz", 2, [128, 512], F32)
        self.cf_diag = self.sb("cf_diag", [128, 31, 128], BF16)
        self.cf_dw = self.sb("cf_dw", [128, 31], F32)
        self.cf_vec = self.sb("cf_vec", [128, 8, NE], F32)
        self.CZ = self.dt("cf_cz_scr", [self.nseq, E, self.T], F32)
        self.SZ = self.dt("cf_sz_scr", [self.nseq, E, self.T], BF16)
        self.cf_czall = self.ring("cf_czall", 1, [128, NE, 512], F32)
        self.cf_szall = self.ring("cf_szall", 1, [128, NE, 512], BF16)
        self.cf_tmpb = self.ring("cf_tmpb", 2, [128, 512], BF16)
        self.cf_stat = self.ring("cf_stat", 2, [128, 3, 512], F32)
        self.cf_n = self.ring("cf_n", 2, [128, 512], F32)
        self.cf_g = self.ring("cf_g", 2, [128, 512], BF16)

    def mixer_conformer(self, li, j, s):
        nc, S, T = self.nc, self.S, self.T
        NE = E // 128
        vec = self.cf_vec
        if s == 0:
            def ldv(slot, ap):
                S.dma("sp", vec[:, slot, :], ap.rearrange("(c p) -> p c", p=128), writes=["cf_vec"],
                      allow_slow_non_contiguous=True)
            ldv(0, self.dram["cf_b_in"][j][0:E])
            ldv(1, self.dram["cf_b_in"][j][E:2 * E])
            ldv(2, self.dram["cf_b_in"][j][2 * E:3 * E])
            ldv(3, self.dram["cf_dw_b"][j])
            ldv(4, self.dram["cf_ln_g"][j])
            ldv(5, self.dram["cf_ln_b"][j])
            S.op("dve", lambda e: e.memset(self.cf_ypad[:], 0.0), writes=["cf_ypad"])
        blocks = tok_blocks(T, 512)
        for c in range(NE):
            S.dma("sp", self.cf_dw[:], self.dram["cf_dw_w"][j][:, c * 128:(c + 1) * 128].rearrange("k p -> p k"),
                  writes=["cf_dw"], allow_slow_non_contiguous=True)
            for k in range(31):
                S.op("pool", lambda e: e.tensor_scalar(out=self.cf_diag[:, k, :], in0=self.ident[:], scalar1=self.cf_dw[:, k:k + 1],
                                                       scalar2=None, op0=ALU.mult),
                     reads=["ident", "cf_dw"], writes=["cf_diag"])
            wa, wak = self.load_wcols("cf_w_in", j, c * 128, 128)
            wg, wgk = self.load_wcols("cf_w_in", j, E + c * 128, 128)
            for (t0, nt) in blocks:
                ps, pk = self.proj_fm(wa, wak, 128, t0, nt)
                a, ak = self.cf_a.next()
                S.op("act", lambda e: e.activation(out=a[:, :nt], in_=ps[:, :nt], func=AF.Identity,
                                                   bias=vec[:, 0, c:c + 1]), reads=[pk, "cf_vec"], writes=[ak])
                ps, pk = self.proj_fm(wg, wgk, 128, t0, nt)
                sg, sgk = self.cf_sg.next()
                S.op("act", lambda e: e.activation(out=sg[:, :nt], in_=ps[:, :nt], func=AF.Sigmoid,
                                                   bias=vec[:, 1, c:c + 1]), reads=[pk, "cf_vec"], writes=[sgk])
                S.op("dve", lambda e: e.tensor_tensor(out=self.cf_ypad[:, 15 + t0:15 + t0 + nt], in0=a[:, :nt], in1=sg[:, :nt], op=ALU.mult),
                     reads=[ak, sgk], writes=["cf_ypad"])
            wz, wzk = self.load_wcols("cf_w_in", j, 2 * E + c * 128, 128)
            for (t0, nt) in blocks:
                ps, pk = self.proj_fm(wz, wzk, 128, t0, nt)
                sz, szk = self.cf_sz.next()
                S.op("act", lambda e: e.activation(out=sz[:, :nt], in_=ps[:, :nt], func=AF.Silu,
                                                   bias=vec[:, 2, c:c + 1]), reads=[pk, "cf_vec"], writes=[szk])
                S.dma("sp", self.SZ[s, c * 128:(c + 1) * 128, t0:t0 + nt], sz[:, :nt], reads=[szk], writes=[("SZ", s)])
            for (t0, nt) in blocks:
                ps, pk = self.ps.next()
                for k in range(31):
                    S.op("pe", lambda e: e.matmul(ps[:, :nt], lhsT=self.cf_diag[:, k, :], rhs=self.cf_ypad[:, t0 + k:t0 + k + nt],
                                                  start=(k == 0), stop=(k == 30)),
                         reads=["cf_diag", "cf_ypad"], writes=[pk])
                cz, czk = self.cf_cz.next()
                S.op("act", lambda e: e.activation(out=cz[:, :nt], in_=ps[:, :nt], func=AF.Identity,
                                                   bias=vec[:, 3, c:c + 1]), reads=[pk, "cf_vec"], writes=[czk])
                S.dma("sp", self.CZ[s, c * 128:(c + 1) * 128, t0:t0 + nt], cz[:, :nt], reads=[czk], writes=[("CZ", s)])
        for (t0, nt) in blocks:
            ca, cak = self.cf_czall.next()
            sa, sak = self.cf_szall.next()
            S.dma("sp", ca[:, :, :nt], self.CZ[s, :, t0:t0 + nt].rearrange("(c p) t -> p c t", p=128),
                  reads=[("CZ", s)], writes=[cak])
            S.dma("sp", sa[:, :, :nt], self.SZ[s, :, t0:t0 + nt].rearrange("(c p) t -> p c t", p=128),
                  reads=[("SZ", s)], writes=[sak])
            p1, p1k = self.ps.next()
            p2, p2k = self.ps.next()
            for c in range(NE):
                tb, tbk = self.cf_tmpb.next()
                S.op("dve", lambda e: e.tensor_copy(out=tb[:, :nt], in_=ca[:, c, :nt]), reads=[cak], writes=[tbk])
                S.op("pe", lambda e: e.matmul(p1[:, :nt], lhsT=self.ones_b[:], rhs=tb[:, :nt], start=(c == 0), stop=(c == NE - 1)),
                     reads=["ones_b", tbk], writes=[p1k])
                tb2, tb2k = self.cf_tmpb.next()
                S.op("act", lambda e: e.activation(out=tb2[:, :nt], in_=ca[:, c, :nt], func=AF.Square), reads=[cak], writes=[tb2k])
                S.op("pe", lambda e: e.matmul(p2[:, :nt], lhsT=self.ones_b[:], rhs=tb2[:, :nt], start=(c == 0), stop=(c == NE - 1)),
                     reads=["ones_b", tb2k], writes=[p2k])
            st, stk = self.cf_stat.next()
            S.op("dve", lambda e: e.tensor_scalar(out=st[:, 0, :nt], in0=p1[:, :nt], scalar1=1.0 / E, scalar2=None, op0=ALU.mult),
                 reads=[p1k], writes=[stk])
            S.op("dve", lambda e: e.tensor_tensor(out=st[:, 1, :nt], in0=st[:, 0, :nt], in1=st[:, 0, :nt], op=ALU.mult),
                 reads=[stk], writes=[stk])
            S.op("dve", lambda e: e.scalar_tensor_tensor(out=st[:, 1, :nt], in0=p2[:, :nt], scalar=1.0 / E, in1=st[:, 1, :nt],
                                                         op0=ALU.mult, op1=ALU.subtract), reads=[p2k, stk], writes=[stk])
            S.op("act", lambda e: e.activation(out=st[:, 2, :nt], in_=st[:, 1, :nt], func=AF.Ln, bias=1e-5), reads=[stk], writes=[stk])
            S.op("act", lambda e: e.activation(out=st[:, 2, :nt], in_=st[:, 2, :nt], func=AF.Exp, scale=-0.5), reads=[stk], writes=[stk])
            for c in range(NE):
                n, nk = self.cf_n.next()
                S.op("dve", lambda e: e.tensor_tensor(out=n[:, :nt], in0=ca[:, c, :nt], in1=st[:, 0, :nt], op=ALU.subtract),
                     reads=[cak, stk], writes=[nk])
                S.op("dve", lambda e: e.tensor_tensor(out=n[:, :nt], in0=n[:, :nt], in1=st[:, 2, :nt], op=ALU.mult),
                     reads=[nk, stk], writes=[nk])
                S.op("act", lambda e: e.activation(out=n[:, :nt], in_=n[:, :nt], func=AF.Silu, scale=vec[:, 4, c:c + 1],
                                                   bias=vec[:, 5, c:c + 1]), reads=[nk, "cf_vec"], writes=[nk])
                g, gk = self.cf_g.next()
                S.op("dve", lambda e: e.tensor_tensor(out=g[:, :nt], in0=n[:, :nt], in1=sa[:, c, :nt], op=ALU.mult),
                     reads=[nk, sak], writes=[gk])
                S.dma("sp", self.G[s, c * 128:(c + 1) * 128, t0:t0 + nt], g[:, :nt], reads=[gk], writes=[("G", s)])

    def build(self):
        self.setup_common()
        self.init_h()
        nl = len(self.layers)
        WOUT = {0: ("hy_w_out", None), 1: ("da_w_out", None), 2: ("gdn_w_out", None), 3: ("cf_w_out", "cf_b_out")}
        for li, m in enumerate(self.layers):
            j = 0
            if m == 0:
                self.hy_dims()
                self.open_scope()
                self.hyena_filters(j)
                self.close_scope()
            self.open_scope()
            self.setup_am()
            if m == 0:
                self.setup_hyena_proj()
            if m == 1:
                self.setup_attention()
            elif m == 2:
                self.setup_gdn()
            elif m == 3:
                self.setup_conformer()
            for s in range(self.nseq):
                self.phase_a(li, s)
                if m == 0:
                    self.hyena_proj(j, s)
                elif m == 1:
                    self.mixer_attention(li, j, s, self.layer_index(li))
                elif m == 2:
                    self.mixer_gdn(li, j, s)
                elif m == 3:
                    self.mixer_conformer(li, j, s)
            self.close_scope()
            if m == 0:
                self.open_scope()
                self.hyena_spectral(j)
                self.close_scope()
            self.open_scope()
            self.setup_z()
            for s in range(self.nseq):
                self.phase_z(li, s, WOUT[m][0], j, bias_name=WOUT[m][1], final=(li == nl - 1))
            self.close_scope()
        self.S.finish("sp")
        self.stack.close()
        return self.nc

    def layer_index(self, li):
        return 1 if self.layers != [0, 1, 2, 3] else li


def rope_consts(T):
    inv = (10000.0 ** (-np.arange(0, 64, 2, dtype=np.float32) / np.float32(64))).astype(np.float32)
    ang = (np.arange(T, dtype=np.float32)[:, None] * inv[None, :]).astype(np.float32)
    cos = np.cos(ang).astype(np.float32)
    sin = np.sin(ang).astype(np.float32)
    idx = np.arange(128) % 32
    C = np.ascontiguousarray(cos[:, idx].T)
    Sg = np.ascontiguousarray(sin[:, idx].T)
    P = np.zeros((128, 128), np.float32)
    for p in range(128):
        r = p % 64
        if r < 32:
            P[p, p + 32] = -1.0
        else:
            P[p, p - 32] = 1.0
    return C, Sg, np.ascontiguousarray(P.T)


def gdn_consts():
    i = np.arange(128)[:, None]
    j = np.arange(128)[None, :]
    c = np.zeros((10, 128, 128), np.float32)
    c[0] = (i <= j)
    c[1] = (i >= j)
    c[2][127, :] = 1.0
    c[3][0, :] = 1.0
    big = 1.0e4
    c[4] = np.where(i > j, 0.0, big)
    c[5] = np.where(i < j, 0.0, big)
    c[6] = np.where(j >= i, 0.0, -big)
    c[7] = np.where(j <= i, 0.0, -big)
    c[8] = np.where(i > j, -1.0, 0.0)
    c[9] = np.where(i < j, -1.0, 0.0)
    return np.ascontiguousarray(c.transpose(1, 0, 2))


def hyena_consts(T):
    import ml_dtypes
    N = 2 * T
    NF = T + 1
    NFC = (NF + 127) // 128
    t = np.arange(T, dtype=np.int64)
    k = np.arange(NFC * 128, dtype=np.int64)
    valid = (k < NF)
    ang = 2.0 * np.pi * ((t[:, None] * k[None, :]) % N).astype(np.float64) / N
    Fc = np.cos(ang) * valid[None, :]
    Fs = -np.sin(ang) * valid[None, :]
    F = np.concatenate([Fc, Fs], axis=1)
    ck = np.where((k == 0) | (k == N // 2), 1.0, 2.0) * valid / N
    Gc = (np.cos(ang) * ck[None, :]).T
    Gs = (-np.sin(ang) * ck[None, :]).T
    G = np.concatenate([Gc, Gs], axis=0)
    tl = np.linspace(0.0, 1.0, T, dtype=np.float32)[:, None]
    w = (2.0 * np.float32(math.pi) * np.arange(T, dtype=np.float32)[:, None] / np.float32(T)).astype(np.float32)
    bands = np.linspace(1e-4, 15, 16, dtype=np.float32)[None, :]
    z = np.concatenate([tl, np.cos(bands * w), -np.sin(bands * w)], axis=-1).astype(np.float32)
    max_decay = math.log(1e-2) / 0.3
    min_decay = math.log(1e-2) / 1.5
    deltas = np.abs(np.linspace(min_decay, max_decay, E, dtype=np.float32)).astype(np.float32)
    NT = (T + 127) // 128
    tlp = np.zeros((NT * 128,), np.float32)
    tlp[:T] = -tl[:, 0]
    return {"c_dft_f": np.ascontiguousarray(F).astype(ml_dtypes.bfloat16),
            "c_dft_g": np.ascontiguousarray(G).astype(ml_dtypes.bfloat16),
            "c_hy_z": np.ascontiguousarray(z.T), "c_hy_delta": deltas,
            "c_hy_tl": np.ascontiguousarray(tlp.reshape(NT, 128).T)}


def const_inputs(T):
    C, Sg, PT = rope_consts(T)
    return {"c_ident": np.eye(128, dtype=np.float32), "c_rope_cos": C, "c_rope_sin": Sg, "c_rope_perm": PT,
            "c_gdn": gdn_consts(), **hyena_consts(T)}


def build_program(T, nseq, layers, inputs):
    shapes = {k: v.shape for k, v in inputs.items() if k != "x"}
    b = Builder(T, nseq, layers, shapes)
    nc = b.build()
    return nc, b


def kernel(**inputs):
    ncores = 8
    x = np.ascontiguousarray(inputs["x"], dtype=np.float32)
    B, L, _ = x.shape
    nseq = B // ncores
    T = L + NMETA
    consts = const_inputs(T)
    params = {k: np.ascontiguousarray(v, dtype=np.float32) for k, v in inputs.items() if k != "x"}
    params.update(consts)
    nc, b = build_program(T, nseq, [0, 1, 2, 3], dict(params, x=x))
    in_maps = []
    for c in range(ncores):
        m = dict(params)
        m["x"] = x[c * nseq:(c + 1) * nseq]
        in_maps.append(m)
    res = run_bass_kernel_spmd(nc, in_maps, core_ids=list(range(ncores)))
    return np.concatenate([r["out"] for r in res.results], axis=0)
```
